# Optimizing a Trainium2 kernel written in Bass

```python
import jax
import jax.numpy as jnp
from jax import lax
import numpy as np

D_MODEL = 1024
BATCH = 16
SEQ = 4096
DEPTH = 4
DEC_BATCH = 16
DEC_SEQ = 16
PAST_LEN = 4096

CHUNK = 64
SUB_BLOCK = 16
QBLOCK = 128
N_A = DEPTH // 2
N_B = DEPTH - N_A
MIX_WIDTH = D_MODEL
MEM_WIDTH = D_MODEL // 4
MAIN_WIDTH = MIX_WIDTH - MEM_WIDTH
HG_DK = 128
HG_DV = 128
HG_HEADS = MAIN_WIDTH // HG_DK
FOX_HD = 64
FOX_HEADS = MAIN_WIDTH // FOX_HD
MEM_HEADS = 4
MEM_HD = MEM_WIDTH // MEM_HEADS
N_MEM = 256
D_FF = -(-(8 * D_MODEL) // (3 * 256)) * 256
A_IN = 4 * MAIN_WIDTH + MEM_WIDTH
B_IN = MAIN_WIDTH + MEM_WIDTH
KV_OUT = 2 * MAIN_WIDTH + FOX_HEADS
EPS = 1e-6
K_MAX = 0.999999
FOX_F_BIAS = 3.0
NEG = -1e30

kernel_name = "yoco_hgrn2_fox_stream_step"


def _rms_norm(x, g):
    xf = x.astype(jnp.float32)
    y = xf * lax.rsqrt(jnp.mean(xf * xf, axis=-1, keepdims=True) + EPS)
    return (y * g.astype(jnp.float32)).astype(x.dtype)


def _hgrn_chunk_step(s, inp):
    q, k, v, g = inp
    bsz, h, c, dk = q.shape
    ns = c // SUB_BLOCK
    b = jnp.cumsum(g, axis=2)
    o_inter = jnp.einsum("bhck,bhkv->bhcv", q * jnp.exp(b), s)
    qs = q.reshape(bsz, h, ns, SUB_BLOCK, dk)
    ks = k.reshape(bsz, h, ns, SUB_BLOCK, dk)
    bs = b.reshape(bsz, h, ns, SUB_BLOCK, dk)
    ref = jnp.concatenate([jnp.zeros_like(bs[:, :, :1, 0]), bs[:, :, :-1, -1]], axis=2)
    q_hat = qs * jnp.exp(bs - ref[:, :, :, None, :])
    e_off = jnp.minimum(ref[:, :, :, None, None, :] - bs[:, :, None, :, :, :], 0.0)
    k_hat = ks[:, :, None] * jnp.exp(e_off)
    a_off = jnp.einsum("bhitk,bhijsk->bhijts", q_hat, k_hat)
    e_diag = jnp.minimum(bs[:, :, :, :, None, :] - bs[:, :, :, None, :, :], 0.0)
    a_diag = jnp.einsum("bhitk,bhisk,bhitsk->bhits", qs, ks, jnp.exp(e_diag))
    eye = np.eye(ns, dtype=bool)[:, :, None, None]
    lower = np.tril(np.ones((ns, ns), dtype=bool), -1)[:, :, None, None]
    tri = np.tril(np.ones((SUB_BLOCK, SUB_BLOCK), dtype=bool))
    a = jnp.where(eye, jnp.where(tri, a_diag[:, :, :, None], 0.0), jnp.where(lower, a_off, 0.0))
    a = a.transpose(0, 1, 2, 4, 3, 5).reshape(bsz, h, c, c)
    o = o_inter + jnp.einsum("bhts,bhsv->bhtv", a, v)
    b_last = b[:, :, -1]
    s_new = jnp.exp(b_last)[..., None] * s + jnp.einsum(
        "bhck,bhcv->bhkv", k * jnp.exp(b_last[:, :, None] - b), v)
    return s_new, o


def _hgrn_recurrence(q, k, v, g, s0):
    bsz, t, h, _ = q.shape
    pad = (-t) % CHUNK
    n = (t + pad) // CHUNK

    def to_chunks(x):
        x = jnp.pad(x, ((0, 0), (0, pad), (0, 0), (0, 0)))
        return x.reshape(bsz, n, CHUNK, h, x.shape[-1]).transpose(1, 0, 3, 2, 4)

    s_fin, o = lax.scan(_hgrn_chunk_step, s0, (to_chunks(q), to_chunks(k), to_chunks(v), to_chunks(g)))
    o = o.transpose(1, 0, 3, 2, 4).reshape(bsz, n * CHUNK, h, -1)[:, :t]
    return o, s_fin


def _hgrn2_mixer(pq, pf, pi, pg, lb, gnorm, s0):
    bsz, t, _ = pq.shape
    shp = (bsz, t, HG_HEADS, HG_DK)
    q = jax.nn.silu(pq.astype(jnp.float32)).reshape(shp)
    z = pf.astype(jnp.float32).reshape(shp)
    lbh = lb.astype(jnp.float32).reshape(HG_HEADS, HG_DK)
    k = jnp.minimum((1.0 - lbh) * jax.nn.sigmoid(-z), K_MAX)
    log_f = jnp.log1p(-k)
    v = pi.astype(jnp.float32).reshape(bsz, t, HG_HEADS, HG_DV)
    o, s_fin = _hgrn_recurrence(q, k, v, log_f, s0.astype(jnp.float32))
    gate = jax.nn.silu(pg.astype(jnp.float32)).reshape(bsz, t, HG_HEADS, HG_DV)
    o = _rms_norm(o, gnorm) * gate
    return o.reshape(bsz, t, MAIN_WIDTH).astype(pq.dtype), s_fin


def _fox_block(q, k, v, c_q, c_k, t0):
    s = jnp.einsum("bqhd,bkhd->bhqk", q, k).astype(jnp.float32) * (FOX_HD ** -0.5)
    s = s + (jnp.swapaxes(c_q, 1, 2)[..., :, None] - jnp.swapaxes(c_k, 1, 2)[..., None, :])
    q_pos = t0 + jnp.arange(q.shape[1])
    k_pos = jnp.arange(k.shape[1])
    s = jnp.where(k_pos[None, :] <= q_pos[:, None], s, NEG)
    p = jax.nn.softmax(s, axis=-1)
    return jnp.einsum("bhqk,bkhd->bqhd", p.astype(v.dtype), v)


def _fox_attention(q, k_all, v_all, c_all, t0):
    bsz, t = q.shape[:2]
    if t % QBLOCK == 0:
        nb = t // QBLOCK
        qb = q.reshape(bsz, nb, QBLOCK, FOX_HEADS, FOX_HD).swapaxes(0, 1)
        cb = c_all[:, t0:].reshape(bsz, nb, QBLOCK, FOX_HEADS).swapaxes(0, 1)
        starts = t0 + QBLOCK * jnp.arange(nb)
        o = lax.map(lambda blk: _fox_block(blk[0], k_all, v_all, blk[1], c_all, blk[2]), (qb, cb, starts))
        o = o.swapaxes(0, 1)
    else:
        o = _fox_block(q, k_all, v_all, c_all[:, t0:], c_all, t0)
    return o.reshape(bsz, t, MAIN_WIDTH)


def _shared_kv(h, g_norm, w_kv, b_f, g_k):
    bsz, t, _ = h.shape
    p = _rms_norm(h, g_norm) @ w_kv
    k = _rms_norm(p[..., :MAIN_WIDTH].reshape(bsz, t, FOX_HEADS, FOX_HD), g_k)
    v = p[..., MAIN_WIDTH:2 * MAIN_WIDTH].reshape(bsz, t, FOX_HEADS, FOX_HD)
    logf = jax.nn.log_sigmoid((p[..., 2 * MAIN_WIDTH:] + b_f).astype(jnp.float32))
    return k, v, logf


def _memory_kv(mem, g_norm, w_kv, g_k):
    bsz, m, _ = mem.shape
    k, v = jnp.split(_rms_norm(mem, g_norm) @ w_kv, 2, axis=-1)
    k = _rms_norm(k.reshape(bsz, m, MEM_HEADS, MEM_HD), g_k)
    return k, v.reshape(bsz, m, MEM_HEADS, MEM_HD)


def _memory_attend(pm, g_q, mem_k, mem_v):
    bsz, t, _ = pm.shape
    q = _rms_norm(pm.reshape(bsz, t, MEM_HEADS, MEM_HD), g_q)
    s = jnp.einsum("bthd,bmhd->bhtm", q, mem_k).astype(jnp.float32) * (MEM_HD ** -0.5)
    p = jax.nn.softmax(s, axis=-1)
    o = jnp.einsum("bhtm,bmhd->bthd", p.astype(mem_v.dtype), mem_v)
    return o.reshape(bsz, t, MEM_WIDTH)


def _swiglu(h, g, w_up, w_down):
    gate, up = jnp.split(_rms_norm(h, g) @ w_up, 2, axis=-1)
    return (jax.nn.silu(gate) * up) @ w_down


def _trunk(x, mem_k, mem_v, hg_states, past, lower_bounds, prm):
    bsz, t, _ = x.shape
    h = x
    new_states = []
    shared = None
    new_kv = None
    for l in range(DEPTH):
        a = _rms_norm(h, prm["norm_mix"][l])
        if l < N_A:
            proj = a @ prm["w_in_a"][l]
            pq, pf, pi, pg, pm = jnp.split(
                proj, [MAIN_WIDTH, 2 * MAIN_WIDTH, 3 * MAIN_WIDTH, 4 * MAIN_WIDTH], axis=-1)
            o_main, s_new = _hgrn2_mixer(pq, pf, pi, pg, lower_bounds[l], prm["hg_gnorm"][l], hg_states[l])
            new_states.append(s_new)
        else:
            j = l - N_A
            proj = a @ prm["w_in_b"][j]
            pq, pm = jnp.split(proj, [MAIN_WIDTH], axis=-1)
            q = _rms_norm(pq.reshape(bsz, t, FOX_HEADS, FOX_HD), prm["fox_gq"][j])
            o_main = _fox_attention(q, shared[0], shared[1], shared[2], shared[3])
        o_mem = _memory_attend(pm, prm["mem_gq"][l], mem_k[l], mem_v[l])
        h = h + jnp.concatenate([o_main, o_mem], axis=-1) @ prm["w_out"][l]
        h = h + _swiglu(h, prm["norm_ffn"][l], prm["w_ffn_up"][l], prm["w_ffn_down"][l])
        if l == N_A - 1:
            k_new, v_new, logf_new = _shared_kv(h, prm["norm_kv"], prm["w_kv"], prm["b_f"], prm["fox_gk"])
            new_kv = (k_new, v_new, logf_new)
            if past is None:
                k_all, v_all, logf_all = k_new, v_new, logf_new
            else:
                k_all = jnp.concatenate([past[0], k_new.astype(past[0].dtype)], axis=1)
                v_all = jnp.concatenate([past[1], v_new.astype(past[1].dtype)], axis=1)
                logf_all = jnp.concatenate([past[2].astype(jnp.float32), logf_new], axis=1)
            c_all = jnp.cumsum(logf_all, axis=1)
            shared = (k_all, v_all, c_all, k_all.shape[1] - t)
    return h, new_states, new_kv


def setup_inputs(seed: int = 0) -> dict:
    key = jax.random.key(seed)
    ks = jax.random.split(key, 28)
    f32 = jnp.float32
    d = D_MODEL

    def nrm(k, shape, scale=1.0):
        return jax.random.normal(k, shape, f32) * scale

    def gain(k, shape):
        return 1.0 + 0.02 * jax.random.normal(k, shape, f32)

    return {
        "x_prompt": nrm(ks[0], (BATCH, SEQ, d)),
        "x_sample": nrm(ks[1], (DEC_BATCH, DEC_SEQ, d)),
        "mem_prompt": nrm(ks[2], (BATCH, N_MEM, d)),
        "state_hgrn_0": nrm(ks[3], (DEC_BATCH, HG_HEADS, HG_DK, HG_DV), 0.5),
        "state_hgrn_1": nrm(ks[4], (DEC_BATCH, HG_HEADS, HG_DK, HG_DV), 0.5),
        "cache_fox_k": nrm(ks[5], (DEC_BATCH, PAST_LEN, FOX_HEADS, FOX_HD)),
        "cache_fox_v": nrm(ks[6], (DEC_BATCH, PAST_LEN, FOX_HEADS, FOX_HD)),
        "cache_fox_logf": jax.nn.log_sigmoid(FOX_F_BIAS + nrm(ks[7], (DEC_BATCH, PAST_LEN, FOX_HEADS))),
        "cache_mem_k": nrm(ks[8], (DEPTH, DEC_BATCH, N_MEM, MEM_HEADS, MEM_HD)),
        "cache_mem_v": nrm(ks[9], (DEPTH, DEC_BATCH, N_MEM, MEM_HEADS, MEM_HD)),
        "norm_mix": gain(ks[10], (DEPTH, d)),
        "w_in_a": nrm(ks[11], (N_A, d, A_IN), d ** -0.5),
        "lb_logits": nrm(ks[12], (N_A, MAIN_WIDTH), 0.1),
        "hg_gnorm": gain(ks[13], (N_A, HG_DV)),
        "w_in_b": nrm(ks[14], (N_B, d, B_IN), d ** -0.5),
        "fox_gq": gain(ks[15], (N_B, FOX_HD)),
        "norm_kv": gain(ks[16], (d,)),
        "w_kv": nrm(ks[17], (d, KV_OUT), d ** -0.5),
        "b_f": FOX_F_BIAS + nrm(ks[18], (FOX_HEADS,), 0.5),
        "fox_gk": gain(ks[19], (FOX_HD,)),
        "norm_mem": gain(ks[20], (DEPTH, d)),
        "w_mem_kv": nrm(ks[21], (DEPTH, d, 2 * MEM_WIDTH), d ** -0.5),
        "mem_gq": gain(ks[22], (DEPTH, MEM_HD)),
        "mem_gk": gain(ks[23], (DEPTH, MEM_HD)),
        "w_out": nrm(ks[24], (DEPTH, MIX_WIDTH, d), (2 * DEPTH * MIX_WIDTH) ** -0.5),
        "norm_ffn": gain(ks[25], (DEPTH, d)),
        "w_ffn_up": nrm(ks[26], (DEPTH, d, 2 * D_FF), d ** -0.5),
        "w_ffn_down": nrm(ks[27], (DEPTH, D_FF, d), (2 * DEPTH * D_FF) ** -0.5),
    }


def reference(x_prompt, x_sample, mem_prompt, state_hgrn_0, state_hgrn_1, cache_fox_k, cache_fox_v,
              cache_fox_logf, cache_mem_k, cache_mem_v, norm_mix, w_in_a, lb_logits, hg_gnorm, w_in_b,
              fox_gq, norm_kv, w_kv, b_f, fox_gk, norm_mem, w_mem_kv, mem_gq, mem_gk, w_out, norm_ffn,
              w_ffn_up, w_ffn_down):
    prm = {
        "norm_mix": norm_mix, "w_in_a": w_in_a, "hg_gnorm": hg_gnorm, "w_in_b": w_in_b,
        "fox_gq": fox_gq, "norm_kv": norm_kv, "w_kv": w_kv, "b_f": b_f, "fox_gk": fox_gk,
        "mem_gq": mem_gq, "w_out": w_out, "norm_ffn": norm_ffn, "w_ffn_up": w_ffn_up,
        "w_ffn_down": w_ffn_down,
    }
    p_lb = jax.nn.softmax(lb_logits.astype(jnp.float32), axis=0)
    lower_bounds = jnp.cumsum(p_lb, axis=0) - p_lb[0]

    mk, mv = [], []
    for l in range(DEPTH):
        k_l, v_l = _memory_kv(mem_prompt, norm_mem[l], w_mem_kv[l], mem_gk[l])
        mk.append(k_l)
        mv.append(v_l)
    p_mem_k = jnp.stack(mk)
    p_mem_v = jnp.stack(mv)
    s_zero = jnp.zeros((x_prompt.shape[0], HG_HEADS, HG_DK, HG_DV), jnp.float32)
    y_prompt, p_states, p_kv = _trunk(x_prompt, p_mem_k, p_mem_v, [s_zero] * N_A, None, lower_bounds, prm)

    y_sample, s_states, s_kv = _trunk(x_sample, cache_mem_k, cache_mem_v, [state_hgrn_0, state_hgrn_1],
                                      (cache_fox_k, cache_fox_v, cache_fox_logf), lower_bounds, prm)
    return (y_prompt, y_sample, p_states[0], p_states[1], p_kv[0], p_kv[1], p_kv[2], p_mem_k, p_mem_v,
            s_states[0], s_states[1], s_kv[0], s_kv[1], s_kv[2])
```

```python
import numpy as np
from contextlib import ExitStack
import concourse.bass as bass
import concourse.mybir as mybir
from concourse.bass_utils import run_bass_kernel_spmd

F32 = mybir.dt.float32
BF16 = mybir.dt.bfloat16
AF = mybir.ActivationFunctionType
ALU = mybir.AluOpType
AX = mybir.AxisListType

D = 1024
DFF = 2816
MAIN = 768
A_IN = 3328
KV_OUT = 1548
EPS = 1e-6
K_MAX = 0.999999
N_MEM = 256
TS = 16
NW = 4
WELEM = 4096

C_ONES, C_BD, C_ID, C_TRI, C_SWAP, C_RESET, CW = 0, 128, 256, 384, 512, 640, 1152
P_NMIX, P_NFFN, P_NMEM, P_NKV, P_LB, P_GN, P_FGQ, P_MGQ, P_FGK, P_MGK, P_BF, PW = 0, 32, 64, 96, 104, 116, 118, 120, 124, 188, 444, 456


class T:
    __slots__ = ("name", "w", "r", "wa", "excl")

    def __init__(self, name="", excl=False):
        self.name = name
        self.w = None
        self.r = []
        self.wa = []
        self.excl = excl


class Prog:
    ENG = ("tensor", "vector", "scalar", "gpsimd", "sync")

    def __init__(self, nc):
        self.nc = nc
        self.ops = {e: [] for e in self.ENG}
        self.cnt = {}
        self.waited = {e: {} for e in self.ENG}
        self.semkeys = []
        for e in self.ENG:
            self._newsem("E_" + e)
        self.n_ops = 0
        self.dead = False

    def _newsem(self, key):
        self.semkeys.append(key)
        self.cnt[key] = 0

    def dma_sem(self, key):
        k = "D_" + key
        if k not in self.cnt:
            self._newsem(k)
        return k

    def op(self, eng, fn, reads=(), writes=(), dma=None, writes_acc=()):
        waits = {}

        def need(dep, same_ok):
            if dep is None:
                return
            key, val, deng = dep
            if deng == eng and not same_ok and not key.startswith("D_"):
                return
            if waits.get(key, 0) < val:
                waits[key] = val

        if self.dead:
            return ("E_" + eng, self.cnt["E_" + eng], eng)
        for t in reads:
            need(t.w, True)
            for r in t.wa:
                need(r, True)
            if t.excl:
                for r in t.r:
                    need(r, False)
        for t in writes:
            need(t.w, False)
            for r in t.wa:
                need(r, False)
            for r in t.r:
                need(r, False)
        for t in writes_acc:
            for r in t.r:
                need(r, False)
        wl = []
        wd = self.waited[eng]
        for key, val in waits.items():
            if wd.get(key, 0) < val:
                wd[key] = val
                wl.append((key, val))
        if dma is None:
            key = "E_" + eng
            self.cnt[key] += 1
            inc = 1
        else:
            key = dma
            self.cnt[key] += 16
            inc = 16
        me = (key, self.cnt[key], eng)
        self.ops[eng].append((wl, fn, key, inc))
        for t in writes:
            t.w = me
            t.r = []
            t.wa = []
        for t in writes_acc:
            t.wa.append(me)
        for t in reads:
            t.r.append(me)
        self.n_ops += 1
        return me

    def barrier(self, force=False):
        if self.dead and not force:
            return
        snap = dict(self.cnt)
        for e in self.ENG:
            wl = []
            for key, val in snap.items():
                if val == 0 or key == "E_" + e:
                    continue
                if self.waited[e].get(key, 0) < val:
                    self.waited[e][key] = val
                    wl.append((key, val))
            if wl:
                self.ops[e].append((wl, None, None, 0))

    def emit(self):
        nc = self.nc
        self.barrier(force=True)
        with ExitStack() as st:
            sems = {}
            for k in self.semkeys:
                sems[k] = st.enter_context(nc.semaphore(k))
            block = st.enter_context(nc.Block())
            for e in self.ENG:
                ops = self.ops[e]
                if not ops:
                    continue

                def body(eng, ops=ops):
                    for wl, fn, key, inc in ops:
                        for wk, wv in wl:
                            eng.wait_ge(sems[wk], wv)
                        if fn is not None:
                            fn(eng).then_inc(sems[key], inc)
                getattr(block, e)(body)


class _Stop(Exception):
    pass


DBG_STOP = [None]


_PROG = [None]


def chk(n):
    if DBG_STOP[0] == n:
        _PROG[0].dead = True


def build(NS, SEQ, NSS, PAST):
    nc = bass.Bass("TRN2", target_bir_lowering=False)

    def dram(name, shape, dtype, kind):
        return nc.dram_tensor(name, list(shape), dtype, kind=kind).ap()

    I, O, S = "ExternalInput", "ExternalOutput", "Internal"
    LS = PAST + TS
    xp = dram("xp", [NS, SEQ, D], F32, I)
    xs = dram("xs", [NSS, TS, D], F32, I)
    memp = dram("memp", [NS, N_MEM, D], F32, I)
    st_in = [dram("st0", [NSS, 6, 128, 128], F32, I), dram("st1", [NSS, 6, 128, 128], F32, I)]
    ck = dram("ck", [NSS, PAST, MAIN], F32, I)
    cv = dram("cv", [NSS, PAST, MAIN], F32, I)
    clf = dram("clf", [NSS, PAST, 12], F32, I)
    cmk = dram("cmk", [4, NSS, N_MEM, 256], F32, I)
    cmv = dram("cmv", [4, NSS, N_MEM, 256], F32, I)
    consts_d = dram("consts", [128, CW], F32, I)
    prm_d = dram("prm", [128, PW], F32, I)
    wshapes = {"w_in_a": [2, D, A_IN], "w_in_b": [2, D, D], "w_kv": [1, D, KV_OUT], "w_mem_kv": [4, D, 512],
               "w_out": [4, D, D], "w_ffn_up": [4, D, 2 * DFF], "w_ffn_down": [4, DFF, D]}
    wf = {k: dram(k, v, F32, I) for k, v in wshapes.items()}
    wb = {k: dram(k + "_b", v, BF16, S) for k, v in wshapes.items()}
    yp = dram("yp", [NS, SEQ, D], F32, O)
    ys = dram("ys", [NSS, TS, D], F32, O)
    pst = [dram("pst0", [NS, 6, 128, 128], F32, O), dram("pst1", [NS, 6, 128, 128], F32, O)]
    pk = dram("pk", [NS, SEQ, MAIN], F32, O)
    pv = dram("pv", [NS, SEQ, MAIN], F32, O)
    plf = dram("plf", [NS, SEQ, 12], F32, O)
    pmk = dram("pmk", [4, NS, N_MEM, 256], F32, O)
    pmv = dram("pmv", [4, NS, N_MEM, 256], F32, O)
    sst = [dram("sst0", [NSS, 6, 128, 128], F32, O), dram("sst1", [NSS, 6, 128, 128], F32, O)]
    sk = dram("sk", [NSS, TS, MAIN], F32, O)
    sv = dram("sv", [NSS, TS, MAIN], F32, O)
    slf = dram("slf", [NSS, TS, 12], F32, O)
    KA = [dram(f"KA{i}", [12, 70, SEQ if i < NS else LS], BF16, S) for i in range(NS + NSS)]
    VS = [dram(f"VS{i}", [SEQ if i < NS else LS, MAIN], BF16, S) for i in range(NS + NSS)]
    QA = [dram(f"QA{i}", [12, 70, 512], BF16, S) for i in range(2)]

    P = Prog(nc)
    _PROG[0] = P
    uid = [0]

    with ExitStack() as top:
        def sb(st, name, shape, dtype):
            uid[0] += 1
            return st.enter_context(nc.sbuf_tensor(f"{name}_{uid[0]}", list(shape), dtype))

        cst = sb(top, "cst", [128, CW], F32)
        prm = sb(top, "prm", [128, PW], F32)
        cb = sb(top, "cb", [128, 5, 128], BF16)
        drv = sb(top, "drv", [128, 32], F32)
        h = sb(top, "h", [128, 8, 512], F32)
        xn = sb(top, "xn", [128, 8, 512], BF16)
        mix = sb(top, "mix", [128, 8, 512], BF16)
        wbuf = [sb(top, f"wbuf{i}", [128, WELEM], BF16) for i in range(NW)]
        S32 = [sb(top, f"S32_{l}", [128, 6, 128], F32) for l in range(2)]
        Sbf = [sb(top, f"Sbf_{l}", [128, 6, 128], BF16) for l in range(2)]
        MKT = sb(top, "MKT", [128, 4, 2, 256], BF16)
        MVb = sb(top, "MVb", [128, 4, 2, 2, 256], BF16)
        cTlast = sb(top, "cTlast", [12, 2], F32)
        pb = [top.enter_context(nc.psum_tensor(f"pb{i}", [128, 512], F32)) for i in range(8)]
        Tpb = [T(f"pb{i}", excl=True) for i in range(8)]
        Tst = [T(f"st{i}") for i in range(8)]
        Tcst, Tprm, Tcb, Tdrv = T("cst"), T("prm"), T("cb"), T("drv")
        Th = [T(f"h{c}") for c in range(8)]
        Txn = [T(f"xn{c}") for c in range(8)]
        Tmix = [T(f"mix{c}") for c in range(8)]
        Twbuf = [T(f"wbuf{i}") for i in range(NW)]
        TS32 = [[T() for _ in range(6)] for _ in range(2)]
        TSbf = [[T() for _ in range(6)] for _ in range(2)]
        TMK, TMV, TcTl = T("MKT"), T("MVb"), T("cTlast")
        psn = [0]

        def ps():
            psn[0] = (psn[0] + 1) % 8
            return psn[0]

        ones_b, bd_b, id_b, tri_b = cb[:, 0, :], cb[:, 1, :], cb[:, 2, :], cb[:, 3, :]
        ones_f = cst[:, C_ONES:C_ONES + 128]
        id_f = cst[:, C_ID:C_ID + 128]
        tri_f = cst[:, C_TRI:C_TRI + 128]
        swap_f = cst[:, C_SWAP:C_SWAP + 128]
        reset_f = cst[:, C_RESET:C_RESET + 512]

        def pcol(c):
            return prm[:, c:c + 1]

        def mm(out, lhsT, rhs, start, stop, reads, writes):
            P.op("tensor", lambda e: e.matmul(out, lhsT, rhs, start=start, stop=stop), reads=reads, writes=writes)

        def tr(out, in_, ident, reads, writes):
            P.op("tensor", lambda e: e.transpose(out, in_, ident), reads=reads, writes=writes)

        def act(out, in_, func, reads, writes, scale=1.0, bias=None):
            if bias is None:
                P.op("scalar", lambda e: e.activation(out=out, in_=in_, func=func, scale=scale), reads=reads, writes=writes)
            else:
                P.op("scalar", lambda e: e.activation(out=out, in_=in_, func=func, scale=scale, bias=bias), reads=reads, writes=writes)

        def tt(out, in0, in1, op, reads, writes, eng="vector"):
            P.op(eng, lambda e: e.tensor_tensor(out=out, in0=in0, in1=in1, op=op), reads=reads, writes=writes)

        def tsc(out, in0, s1, op0, reads, writes, s2=None, op1=None, eng="vector"):
            if op1 is None:
                P.op(eng, lambda e: e.tensor_scalar(out=out, in0=in0, scalar1=s1, scalar2=None, op0=op0), reads=reads, writes=writes)
            else:
                P.op(eng, lambda e: e.tensor_scalar(out=out, in0=in0, scalar1=s1, scalar2=s2, op0=op0, op1=op1), reads=reads, writes=writes)

        def stt(out, in0, scalar, in1, op0, op1, reads, writes):
            P.op("vector", lambda e: e.scalar_tensor_tensor(out=out, in0=in0, scalar=scalar, in1=in1, op0=op0, op1=op1),
                 reads=reads, writes=writes)

        def cp(out, in_, reads, writes, eng="vector"):
            if eng == "scalar":
                P.op("scalar", lambda e: e.copy(out=out, in_=in_), reads=reads, writes=writes)
            else:
                P.op(eng, lambda e: e.tensor_copy(out=out, in_=in_), reads=reads, writes=writes)

        def recip(out, in_, reads, writes):
            P.op("vector", lambda e: e.reciprocal(out=out, in_=in_), reads=reads, writes=writes)

        def mset(ap, val, writes, eng="gpsimd"):
            P.op(eng, lambda e: e.memset(ap, val), writes=writes)

        dman = [0]

        def dma(out, in_, reads, writes, q="sync", key=None, acc=()):
            if key is None:
                dman[0] += 1
                key = f"g{dman[0] % 48}_{q}"
            return P.op(q, lambda e: e.dma_start(out=out, in_=in_), reads=reads, writes=writes, dma=P.dma_sem(key),
                        writes_acc=acc)

        Twb = {}
        ncast = [0]

        def cast_weight(name, l):
            shp = wshapes[name]
            rows, cols = shp[1], shp[2]
            step = max(128, (2 * 1024 * 1024 // cols) // 128 * 128)
            ts_ = []
            r = 0
            while r < rows:
                r1 = min(rows, r + step)
                t = T(f"{name}{l}_{r}")
                ncast[0] += 1
                dma(wb[name][l, r:r1, :], wf[name][l, r:r1, :], [], [t], q="gpsimd", key=f"wcast{ncast[0] % 16}")
                ts_.append(t)
                r = r1
            Twb[(name, l)] = ts_

        wplan = []
        wstate = {"issued": 0, "used": 0}

        def w_issue():
            while wstate["issued"] < len(wplan) and wstate["issued"] < wstate["used"] + NW:
                i = wstate["issued"]
                name, l, row0, nk, c0, ncols = wplan[i]
                slot = i % NW
                dst = wbuf[slot][:, 0:nk * ncols].rearrange("p (k c) -> p k c", c=ncols)
                src = wb[name][l, row0:row0 + nk * 128, c0:c0 + ncols].rearrange("(k p) c -> p k c", p=128)
                dma(dst, src, Twb[(name, l)], [Twbuf[slot]], q="sync", key=f"w{slot}")
                wstate["issued"] += 1

        def w_next(spec):
            i = wstate["used"]
            assert wplan[i] == spec, (i, wplan[i], spec)
            w_issue()
            wstate["used"] += 1
            slot = i % NW
            nk, ncols = spec[3], spec[5]
            return wbuf[slot][:, 0:nk * ncols].rearrange("p (k c) -> p k c", c=ncols), Twbuf[slot]

        def plan_cols(name, l, total, step=512, nk=8):
            return [(name, l, 0, nk, c, min(step, total - c)) for c in range(0, total, step)]

        def plan_down(l):
            out = []
            for jp in range(4):
                for kh in range(2):
                    out.append(("w_ffn_down", l, kh * 11 * 128, 11, jp * 256, 256))
            return out

        def plan_tile():
            pl = []
            for l in range(4):
                if l < 2:
                    pl += plan_cols("w_in_a", l, A_IN)
                else:
                    pl += plan_cols("w_in_b", l - 2, D)
                pl += plan_cols("w_out", l, D)
                pl += plan_cols("w_ffn_up", l, 2 * DFF)
                pl += plan_down(l)
                if l == 1:
                    pl += plan_cols("w_kv", 0, KV_OUT)
            return pl

        def rms_xn(gcol0, T_, arena):
            sqr = [sb(arena, "sqr", [128, 512], BF16) for _ in range(2)]
            Tsq = [T(), T()]
            rs = sb(arena, "rs", [128, 512], F32)
            Trs = T()
            b = ps()
            for c in range(8):
                i = c % 2
                act(sqr[i][:, 0:T_], h[:, c, 0:T_], AF.Square, [Th[c]], [Tsq[i]])
                mm(pb[b][:, 0:T_], ones_b, sqr[i][:, 0:T_], c == 0, c == 7, [Tsq[i], Tcb], [Tpb[b]])
            act(rs[:, 0:T_], pb[b][:, 0:T_], AF.Sqrt, [Tpb[b]], [Trs], scale=1.0 / D, bias=EPS)
            recip(rs[:, 0:T_], rs[:, 0:T_], [Trs], [Trs])
            for c in range(8):
                stt(xn[:, c, 0:T_], h[:, c, 0:T_], pcol(gcol0 + c), rs[:, 0:T_], ALU.mult, ALU.mult,
                    [Th[c], Trs, Tprm], [Txn[c]])

        def rstd_bufs(arena, n=2):
            return {"i": 0, "b": [(sb(arena, "hsq", [128, 512], BF16), T(), sb(arena, "hrs", [128, 512], F32), T()) for _ in range(n)]}

        def head_rstd(src, Tsrc, n, onesap, T_, rb, bank=None):
            rb["i"] = (rb["i"] + 1) % len(rb["b"])
            sq, Tsq, rs, Trs = rb["b"][rb["i"]]
            act(sq[:, 0:T_], src, AF.Square, [Tsrc], [Tsq])
            b = ps() if bank is None else bank
            mm(pb[b][:, 0:T_], onesap, sq[:, 0:T_], True, True, [Tsq, Tcb], [Tpb[b]])
            act(rs[:, 0:T_], pb[b][:, 0:T_], AF.Sqrt, [Tpb[b]], [Trs], scale=1.0 / n, bias=EPS)
            recip(rs[:, 0:T_], rs[:, 0:T_], [Trs], [Trs])
            return rs, Trs

        def attn_pair(lhsK, rhsQ, lhsV, blocks, nq, out_ap, Tout, Kreads, Qreads, Vreads, arena_bufs):
            Pt, TPt, R, TR, comb, Tcomb = arena_bufs
            Ob = []
            for hh in range(2):
                ob = ps()
                while ob in Ob:
                    ob = ps()
                Ob.append(ob)
            nb = len(blocks)
            items = [(hh, bi) + tuple(blocks[bi]) for hh in range(2) for bi in range(nb)]
            DEPTH = 2
            NP_ = len(Pt)

            def issue_s(idx):
                hh, bi, j, nk, q0, diag = items[idx]
                w = nq - q0
                sbk = ps()
                while sbk in Ob:
                    sbk = ps()
                mm(pb[sbk][0:nk, 0:w], lhsK(hh, j, nk), rhsQ(hh, q0), True, True, Kreads + Qreads, [Tpb[sbk]])
                pi = idx % NP_
                act(Pt[pi][0:nk, 0:w], pb[sbk][0:nk, 0:w], AF.Exp, [Tpb[sbk]], [TPt[pi]])
                if diag:
                    tt(Pt[pi][0:nk, 0:nk], Pt[pi][0:nk, 0:nk], tri_b[0:nk, 0:nk], ALU.mult, [TPt[pi], Tcb], [TPt[pi]],
                       eng="gpsimd")

            def issue_pv(idx):
                hh, bi, j, nk, q0, diag = items[idx]
                w = nq - q0
                pi = idx % NP_
                mm(pb[Ob[hh]][:, q0:nq], lhsV(hh, j, nk), Pt[pi][0:nk, 0:w], bi == 0, bi == nb - 1,
                   [TPt[pi]] + Vreads, [Tpb[Ob[hh]]])

            for idx in range(len(items)):
                issue_s(idx)
                if idx >= DEPTH:
                    issue_pv(idx - DEPTH)
            for idx in range(max(0, len(items) - DEPTH), len(items)):
                issue_pv(idx)
            oa, ob_ = Ob
            recip(R[0:64, 0:nq], pb[ob_][0:64, 0:nq], [Tpb[ob_]], [TR])
            recip(R[64:128, 0:nq], pb[oa][64:128, 0:nq], [Tpb[oa]], [TR])
            pr = ps()
            while pr in Ob:
                pr = ps()
            mm(pb[pr][:, 0:nq], swap_f, R[:, 0:nq], True, True, [TR, Tcst], [Tpb[pr]])
            cp(comb[0:64, 0:nq], pb[oa][0:64, 0:nq], [Tpb[oa]], [Tcomb])
            cp(comb[64:128, 0:nq], pb[ob_][64:128, 0:nq], [Tpb[ob_]], [Tcomb])
            tt(out_ap, comb[:, 0:nq], pb[pr][:, 0:nq], ALU.mult, [Tcomb, Tpb[pr]], [Tout])

        def attn_bufs(arena):
            Pt = [sb(arena, "Pt", [128, 512], BF16) for _ in range(4)]
            R = sb(arena, "R", [128, 512], F32)
            comb = sb(arena, "comb", [128, 512], F32)
            return Pt, [T(), T(), T(), T()], R, T(), comb, T()

        def mem_attend(l, qm, Tqm, T_, arena):
            bufs = attn_bufs(arena)
            Qm = sb(arena, "Qm", [128, 2, 512], BF16)
            TQm = [T(), T()]
            rb = rstd_bufs(arena)
            for cc in range(2):
                rs, Trs = head_rstd(qm[:, cc, 0:T_], Tqm[cc], 64, bd_b, T_, rb)
                stt(Qm[:, cc, 0:T_], qm[:, cc, 0:T_], drv[:, 16 + l:17 + l], rs[:, 0:T_], ALU.mult, ALU.mult,
                    [Tqm[cc], Trs, Tdrv], [TQm[cc]])
            blocks = [(0, 128, 0, False), (1, 128, 0, False)]
            for cc in range(2):
                def lhsK(hh, j, nk, cc=cc):
                    return MKT[hh * 64:(hh + 1) * 64, l, cc, j * 128:(j + 1) * 128]

                def rhsQ(hh, q0, cc=cc):
                    return Qm[hh * 64:(hh + 1) * 64, cc, 0:T_]

                def lhsV(hh, j, nk, cc=cc):
                    return MVb[:, l, j, cc, 0:128] if hh == 0 else MVb[:, l, j, cc, 128:256]
                attn_pair(lhsK, rhsQ, lhsV, blocks, T_, mix[:, 6 + cc, 0:T_], Tmix[6 + cc], [TMK], [TQm[cc]], [TMV], bufs)

        def out_proj_ffn(l, T_):
            for (name, ll, r0, nk, c0, ncols) in plan_cols("w_out", l, D):
                wv, Tw = w_next((name, ll, r0, nk, c0, ncols))
                for jj in range(ncols // 128):
                    j = c0 // 128 + jj
                    b = ps()
                    for k in range(8):
                        mm(pb[b][:, 0:T_], wv[:, k, jj * 128:(jj + 1) * 128], mix[:, k, 0:T_], k == 0, k == 7,
                           [Tw, Tmix[k]], [Tpb[b]])
                    tt(h[:, j, 0:T_], h[:, j, 0:T_], pb[b][:, 0:T_], ALU.add, [Th[j], Tpb[b]], [Th[j]])
            P.barrier()
            with ExitStack() as arena:
                rms_xn(P_NFFN + 8 * l, T_, arena)
                hid = sb(arena, "hid", [128, 22, 512], BF16)
                Thid = [T() for _ in range(22)]
                for (name, ll, r0, nk, c0, ncols) in plan_cols("w_ffn_up", l, 2 * DFF):
                    wv, Tw = w_next((name, ll, r0, nk, c0, ncols))
                    for jj in range(ncols // 128):
                        j = c0 // 128 + jj
                        b = ps()
                        for k in range(8):
                            mm(pb[b][:, 0:T_], wv[:, k, jj * 128:(jj + 1) * 128], xn[:, k, 0:T_], k == 0, k == 7,
                               [Tw, Txn[k]], [Tpb[b]])
                        if j < 22:
                            act(hid[:, j, 0:T_], pb[b][:, 0:T_], AF.Silu, [Tpb[b]], [Thid[j]])
                        else:
                            tt(hid[:, j - 22, 0:T_], pb[b][:, 0:T_], hid[:, j - 22, 0:T_], ALU.mult,
                               [Tpb[b], Thid[j - 22]], [Thid[j - 22]])
                for jp in range(4):
                    bs = [ps(), ps()]
                    for kh in range(2):
                        spec = ("w_ffn_down", l, kh * 11 * 128, 11, jp * 256, 256)
                        wv, Tw = w_next(spec)
                        for jj in range(2):
                            for k in range(11):
                                kk = kh * 11 + k
                                mm(pb[bs[jj]][:, 0:T_], wv[:, k, jj * 128:(jj + 1) * 128], hid[:, kk, 0:T_],
                                   kk == 0, kk == 21, [Tw, Thid[kk]], [Tpb[bs[jj]]])
                    for jj in range(2):
                        j = jp * 2 + jj
                        tt(h[:, j, 0:T_], h[:, j, 0:T_], pb[bs[jj]][:, 0:T_], ALU.add, [Th[j], Tpb[bs[jj]]], [Th[j]])
                P.barrier()

        def layer_a(l, T_, seq_is_sample):
            CL = min(64, T_)
            NCH = T_ // CL
            G = min(2, NCH)
            GT = G * CL
            NG = NCH // G
            with ExitStack() as arena:
                rms_xn(P_NMIX + 8 * l, T_, arena)
                sqb = sb(arena, "sqb", [128, 6, 512], BF16)
                kb = sb(arena, "kb", [128, 6, 512], BF16)
                bfa = sb(arena, "bfa", [128, 6, 512], F32)
                gate = sb(arena, "gate", [128, 6, 512], BF16)
                qm = sb(arena, "qm", [128, 2, 512], F32)
                Vt = sb(arena, "Vt", [128, 4, MAIN], BF16)
                tmp = [sb(arena, "tmpa", [128, 512], F32) for _ in range(3)]
                Ttmp = [T(), T(), T()]
                Tsq_, Tkb, Tbf, Tgate, Tqm = ([T() for _ in range(6)] for _ in range(5))
                Tqm = [T(), T()]
                TVt = [T() for _ in range(4)]
                tn = [0]

                def nxt():
                    tn[0] = (tn[0] + 1) % 3
                    return tn[0]
                Qp = sb(arena, "Qp", [128, 6, 512], BF16)
                Qt = sb(arena, "Qt", [128, 6, 512], BF16)
                Kt = sb(arena, "Kt", [128, 6, 512], BF16)
                KhT = sb(arena, "KhT", [128, 6, 512], BF16)
                Kh = sb(arena, "Kh", [128, 4, MAIN], BF16)
                dec = sb(arena, "dec", [128, 6, 8], F32)
                ATt = sb(arena, "ATt", [128, 6, 4, 128], BF16)
                rs6 = sb(arena, "rs6", [128, 6, 512], F32)
                TQp, TQt, TKt, TKhT, Tdec = ([T() for _ in range(6)] for _ in range(5))
                TKh = [T() for _ in range(4)]
                TAT = [[T() for _ in range(4)] for _ in range(6)]
                mset(ATt[:], 0.0, [t for row in TAT for t in row], eng="gpsimd")
                def hgrn_elem(hh):
                    P.op("vector", lambda e, hh=hh: e.tensor_tensor_scan(out=bfa[:, hh, 0:T_], data0=reset_f[:, 0:T_],
                                                                          data1=bfa[:, hh, 0:T_], initial=0.0,
                                                                          op0=ALU.mult, op1=ALU.add),
                         reads=[Tbf[hh], Tcst], writes=[Tbf[hh]])
                    b3 = bfa[:, hh, 0:T_].rearrange("p (c l) -> p c l", l=CL)
                    mid = b3[:, :, CL // 2 - 1:CL // 2].broadcast_to([128, NCH, CL])
                    last = b3[:, :, CL - 1:CL].broadcast_to([128, NCH, CL])
                    i0 = nxt()
                    act(tmp[i0][:, 0:T_], bfa[:, hh, 0:T_], AF.Exp, [Tbf[hh]], [Ttmp[i0]])
                    tt(Qp[:, hh, 0:T_], sqb[:, hh, 0:T_], tmp[i0][:, 0:T_], ALU.mult, [Tsq_[hh], Ttmp[i0]], [TQp[hh]])
                    e3 = tmp[i0][:, 0:T_].rearrange("p (c l) -> p c l", l=CL)
                    cp(dec[:, hh, 0:NCH], e3[:, :, CL - 1], [Ttmp[i0]], [Tdec[hh]])
                    i1 = nxt()
                    t3 = tmp[i1][:, 0:T_].rearrange("p (c l) -> p c l", l=CL)
                    tt(t3, b3, mid, ALU.subtract, [Tbf[hh]], [Ttmp[i1]])
                    i2 = nxt()
                    act(tmp[i2][:, 0:T_], tmp[i1][:, 0:T_], AF.Exp, [Ttmp[i1]], [Ttmp[i2]])
                    tt(Qt[:, hh, 0:T_], sqb[:, hh, 0:T_], tmp[i2][:, 0:T_], ALU.mult, [Tsq_[hh], Ttmp[i2]], [TQt[hh]])
                    act(tmp[i2][:, 0:T_], tmp[i1][:, 0:T_], AF.Exp, [Ttmp[i1], TQt[hh]], [Ttmp[i2]], scale=-1.0)
                    tt(Kt[:, hh, 0:T_], kb[:, hh, 0:T_], tmp[i2][:, 0:T_], ALU.mult, [Tkb[hh], Ttmp[i2]], [TKt[hh]])
                    tt(t3, b3, last, ALU.subtract, [Tbf[hh], TKt[hh]], [Ttmp[i1]])
                    act(tmp[i1][:, 0:T_], tmp[i1][:, 0:T_], AF.Exp, [Ttmp[i1]], [Ttmp[i1]], scale=-1.0)
                    tt(KhT[:, hh, 0:T_], kb[:, hh, 0:T_], tmp[i1][:, 0:T_], ALU.mult, [Tkb[hh], Ttmp[i1]], [TKhT[hh]])
                for (name, ll, r0, nk, c0, ncols) in plan_cols("w_in_a", l, A_IN):
                    wv, Tw = w_next((name, ll, r0, nk, c0, ncols))
                    bidx = c0 // 512
                    tok = {3: (0, 512, 0), 4: (0, 256, 512)}.get(bidx)
                    if tok is not None:
                        wc0, wn, vc0 = tok
                        for tb in range(NG):
                            b = ps()
                            for k in range(8):
                                mm(pb[b][0:GT, 0:wn], xn[:, k, tb * GT:(tb + 1) * GT], wv[:, k, wc0:wc0 + wn], k == 0, k == 7,
                                   [Tw, Txn[k]], [Tpb[b]])
                            cp(Vt[0:GT, tb, vc0:vc0 + wn], pb[b][0:GT, 0:wn], [Tpb[b]], [TVt[tb]], eng="scalar")
                    for jj in range(ncols // 128):
                        j = c0 // 128 + jj
                        if 12 <= j < 18:
                            continue
                        b = ps()
                        for k in range(8):
                            mm(pb[b][:, 0:T_], wv[:, k, jj * 128:(jj + 1) * 128], xn[:, k, 0:T_], k == 0, k == 7,
                               [Tw, Txn[k]], [Tpb[b]])
                        if j < 6:
                            act(sqb[:, j, 0:T_], pb[b][:, 0:T_], AF.Silu, [Tpb[b]], [Tsq_[j]])
                        elif j < 12:
                            hh = j - 6
                            i0 = nxt()
                            act(tmp[i0][:, 0:T_], pb[b][:, 0:T_], AF.Sigmoid, [Tpb[b]], [Ttmp[i0]], scale=-1.0)
                            tsc(tmp[i0][:, 0:T_], tmp[i0][:, 0:T_], drv[:, l * 6 + hh:l * 6 + hh + 1], ALU.mult,
                                [Ttmp[i0], Tdrv], [Ttmp[i0]], s2=K_MAX, op1=ALU.min)
                            cp(kb[:, hh, 0:T_], tmp[i0][:, 0:T_], [Ttmp[i0]], [Tkb[hh]], eng="gpsimd")
                            act(bfa[:, hh, 0:T_], tmp[i0][:, 0:T_], AF.Ln, [Ttmp[i0]], [Tbf[hh]], scale=-1.0, bias=1.0)
                            hgrn_elem(hh)
                        elif j < 24:
                            act(gate[:, j - 18, 0:T_], pb[b][:, 0:T_], AF.Silu, [Tpb[b]], [Tgate[j - 18]])
                        else:
                            cp(qm[:, j - 24, 0:T_], pb[b][:, 0:T_], [Tpb[b]], [Tqm[j - 24]], eng="scalar")
                chk(30)
                with ExitStack() as a2:
                    mem_attend(l, qm, Tqm, T_, a2)
                chk(31)
                chk(32)
                for tb in range(NG):
                    b = ps()
                    pbv = pb[b][:].bitcast(BF16)
                    for hh in range(6):
                        tr(pbv[0:GT, hh * 128:(hh + 1) * 128], KhT[:, hh, tb * GT:(tb + 1) * GT], id_b, [TKhT[hh], Tcb], [Tpb[b]])
                    cp(Kh[0:GT, tb, :], pbv[0:GT, 0:MAIN], [Tpb[b]], [TKh[tb]])
                for hh in range(6):
                    for gi in range(NG):
                        b = ps()
                        r = 0
                        sl = slice(gi * GT, (gi + 1) * GT)
                        mm(pb[b][0:GT, r * 128:r * 128 + GT], Kt[:, hh, sl], Qt[:, hh, sl], True, True,
                           [TKt[hh], TQt[hh]], [Tpb[b]])
                        for ci in range(G):
                            ps_ = slice(ci * CL, (ci + 1) * CL)
                            tt(ATt[ps_, hh, gi, ci * CL:(ci + 1) * CL], pb[b][ps_, r * 128 + ci * CL:r * 128 + (ci + 1) * CL],
                               tri_f[ps_, ci * CL:(ci + 1) * CL], ALU.mult, [Tpb[b], Tcst], [TAT[hh][gi]])
                P.barrier()
                chk(33)
                srn = [0]
                for gi in range(NG):
                    sl = slice(gi * GT, (gi + 1) * GT)
                    for hh in range(6):
                        mm(pb[hh][:, sl], Vt[0:GT, gi, hh * 128:(hh + 1) * 128], ATt[0:GT, hh, gi, 0:GT], True, False,
                           [TVt[gi], TAT[hh][gi]], [Tpb[hh]])
                    for ci in range(G):
                        c = gi * G + ci
                        cs = slice(c * CL, (c + 1) * CL)
                        for hh in range(6):
                            mm(pb[hh][:, cs], Sbf[l][:, hh, :], Qp[:, hh, cs], False, ci == G - 1,
                               [TSbf[l][hh], TQp[hh]], [Tpb[hh]])
                        prow = slice(ci * CL, (ci + 1) * CL)
                        for hh in range(6):
                            srn[0] = (srn[0] + 1) % 2
                            r = srn[0]
                            reg = pb[6 + r][:, 0:128]
                            mm(reg, Kh[prow, gi, hh * 128:(hh + 1) * 128], Vt[prow, gi, hh * 128:(hh + 1) * 128], True, True,
                               [TKh[gi], TVt[gi]], [Tpb[6 + r]])
                            stt(S32[l][:, hh, :], S32[l][:, hh, :], dec[:, hh, c:c + 1], reg, ALU.mult, ALU.add,
                                [TS32[l][hh], Tdec[hh], Tpb[6 + r]], [TS32[l][hh]])
                            cp(Sbf[l][:, hh, :], S32[l][:, hh, :], [TS32[l][hh]], [TSbf[l][hh]], eng="gpsimd")
                P.barrier()
                chk(34)
                To32 = [T() for _ in range(6)]
                Tsq6 = [T() for _ in range(6)]
                Trs6 = [T() for _ in range(6)]
                for hh in range(6):
                    cp(bfa[:, hh, 0:T_], pb[hh][:, 0:T_], [Tpb[hh]], [To32[hh]], eng="scalar")
                    act(Qt[:, hh, 0:T_], pb[hh][:, 0:T_], AF.Square, [Tpb[hh]], [Tsq6[hh]])
                for hh in range(6):
                    mm(pb[hh][:, 0:T_], ones_b, Qt[:, hh, 0:T_], True, True, [Tsq6[hh], Tcb], [Tpb[hh]])
                for hh in range(6):
                    act(rs6[:, hh, 0:T_], pb[hh][:, 0:T_], AF.Sqrt, [Tpb[hh]], [Trs6[hh]], scale=1.0 / 128, bias=EPS)
                    recip(rs6[:, hh, 0:T_], rs6[:, hh, 0:T_], [Trs6[hh]], [Trs6[hh]])
                    stt(bfa[:, hh, 0:T_], bfa[:, hh, 0:T_], pcol(P_GN + l), rs6[:, hh, 0:T_], ALU.mult, ALU.mult,
                        [To32[hh], Trs6[hh], Tprm], [To32[hh]])
                    tt(mix[:, hh, 0:T_], bfa[:, hh, 0:T_], gate[:, hh, 0:T_], ALU.mult, [To32[hh], Tgate[hh]], [Tmix[hh]])
                P.barrier()
            chk(35)
            out_proj_ffn(l, T_)

        def c_block(lf_ap, Tlf, n, cT_ap, TcT, first_col_prev):
            b = ps()
            mm(pb[b][0:12, 0:n], lf_ap, tri_f[0:n, 0:n], True, True, [Tlf, Tcst], [Tpb[b]])
            tsc(cT_ap, pb[b][0:12, 0:n], first_col_prev, ALU.add, [Tpb[b], TcT, TcTl], [TcT])

        def split_and_store(si, cT, TcT, n, t0, arena, qa=None):
            n0 = sb(arena, "n0", [12, 512], F32)
            r1 = sb(arena, "r1", [12, 512], F32)
            parts = sb(arena, "parts", [12, 3, 512], BF16)
            qparts = sb(arena, "qparts", [12, 3, 512], BF16)
            onesr = sb(arena, "onesr", [12, 3, 512], BF16)
            Tn0, Tr1, Tparts, Tqp, Tones = T(), T(), T(), T(), T()
            mset(onesr[:], 1.0, [Tones], eng="gpsimd")
            tsc(n0[:, 0:n], cT, -1.0, ALU.mult, [TcT], [Tn0])
            cp(parts[:, 0, 0:n], n0[:, 0:n], [Tn0], [Tparts])
            tt(r1[:, 0:n], n0[:, 0:n], parts[:, 0, 0:n], ALU.subtract, [Tn0, Tparts], [Tr1])
            cp(parts[:, 1, 0:n], r1[:, 0:n], [Tr1], [Tparts])
            tt(n0[:, 0:n], r1[:, 0:n], parts[:, 1, 0:n], ALU.subtract, [Tr1, Tparts], [Tn0])
            cp(parts[:, 2, 0:n], n0[:, 0:n], [Tn0], [Tparts])
            dma(KA[si][:, 64:67, t0:t0 + n], parts[:, :, 0:n], [Tparts], [], q="gpsimd", acc=[TKA[si]])
            dma(KA[si][:, 67:70, t0:t0 + n], onesr[:, :, 0:n], [Tones], [], q="gpsimd", acc=[TKA[si]])
            if qa is not None:
                tsc(qparts[:, :, 0:n], parts[:, :, 0:n], -1.0, ALU.mult, [Tparts], [Tqp])
                dma(QA[qa][:, 64:67, 0:n], onesr[:, :, 0:n], [Tones], [], q="gpsimd", acc=[TQA[qa]])
                dma(QA[qa][:, 67:70, 0:n], qparts[:, :, 0:n], [Tqp], [], q="gpsimd", acc=[TQA[qa]])

        TKA = [T(f"KA{i}") for i in range(NS + NSS)]
        TVS = [T(f"VS{i}") for i in range(NS + NSS)]
        TQA = [T("QA0"), T("QA1")]

        def kv_stage(si, sl_, t0, T_, qa, kout, vout, lfout, o0):
            GT = min(128, T_)
            NTB = T_ // GT
            with ExitStack() as arena:
                rms_xn(P_NKV, T_, arena)
                kvt = [sb(arena, "kvt", [128, KV_OUT], F32) for _ in range(NTB)]
                Tkvt = [T() for _ in range(NTB)]
                kT = sb(arena, "kT", [128, 6, 512], BF16)
                TkT = [T() for _ in range(6)]
                vb = sb(arena, "vb", [128, MAIN], BF16)
                Tvb = T()
                sqk = sb(arena, "sqk", [128, MAIN], F32)
                Tsqk = T()
                ss = sb(arena, "ss", [128, 12], F32)
                Tss = T()
                lft = sb(arena, "lft", [128, 12], F32)
                Tlft = T()
                cT = sb(arena, "cT", [12, 512], F32)
                TcT = T()
                for spec in plan_cols("w_kv", 0, KV_OUT):
                    wv, Tw = w_next(spec)
                    c0, ncols = spec[4], spec[5]
                    for tb in range(NTB):
                        b = ps()
                        for k in range(8):
                            mm(pb[b][0:GT, 0:ncols], xn[:, k, tb * GT:(tb + 1) * GT], wv[:, k, 0:ncols], k == 0, k == 7,
                               [Tw, Txn[k]], [Tpb[b]])
                        cp(kvt[tb][0:GT, c0:c0 + ncols], pb[b][0:GT, 0:ncols], [Tpb[b]], [Tkvt[tb]], eng="scalar")
                for tb in range(NTB):
                    ki = tb
                    rows = slice(t0 + tb * GT, t0 + (tb + 1) * GT)
                    orows = slice(o0 + tb * GT, o0 + (tb + 1) * GT)
                    dma(vout[sl_, orows, :], kvt[ki][0:GT, MAIN:2 * MAIN], [Tkvt[ki]], [], q="gpsimd")
                    cp(vb[0:GT, :], kvt[ki][0:GT, MAIN:2 * MAIN], [Tkvt[ki]], [Tvb], eng="gpsimd")
                    dma(VS[si][rows, :], vb[0:GT, :], [Tvb], [], q="gpsimd", acc=[TVS[si]])
                    k3 = kvt[ki][0:GT, 0:MAIN].rearrange("p (h d) -> p h d", d=64)
                    tt(sqk[0:GT, :], kvt[ki][0:GT, 0:MAIN], kvt[ki][0:GT, 0:MAIN], ALU.mult, [Tkvt[ki]], [Tsqk])
                    P.op("vector", lambda e, GT=GT: e.tensor_reduce(out=ss[0:GT, :], in_=sqk[0:GT, :].rearrange("p (h d) -> p h d", d=64),
                                                                     axis=AX.X, op=ALU.add), reads=[Tsqk], writes=[Tss])
                    act(ss[0:GT, :], ss[0:GT, :], AF.Sqrt, [Tss], [Tss], scale=1.0 / 64, bias=EPS)
                    recip(ss[0:GT, :], ss[0:GT, :], [Tss], [Tss])
                    tt(k3, k3, ss[0:GT, :].rearrange("p (h o) -> p h o", o=1).broadcast_to([GT, 12, 64]), ALU.mult,
                       [Tkvt[ki], Tss], [Tkvt[ki]])
                    tt(k3, k3, prm[0:GT, P_FGK:P_FGK + 64].rearrange("p (o d) -> p o d", o=1).broadcast_to([GT, 12, 64]), ALU.mult,
                       [Tkvt[ki], Tprm], [Tkvt[ki]])
                    dma(kout[sl_, orows, :], kvt[ki][0:GT, 0:MAIN], [Tkvt[ki]], [], q="gpsimd")
                    for cc in range(6):
                        b = ps()
                        tr(pb[b][:, 0:GT], kvt[ki][0:GT, cc * 128:(cc + 1) * 128], id_f[0:GT, 0:GT], [Tkvt[ki], Tcst], [Tpb[b]])
                        cp(kT[:, cc, tb * GT:(tb + 1) * GT], pb[b][:, 0:GT], [Tpb[b]], [TkT[cc]], eng=("scalar" if cc % 2 else "vector"))
                    tt(lft[0:GT, :], kvt[ki][0:GT, 2 * MAIN:2 * MAIN + 12], prm[0:GT, P_BF:P_BF + 12], ALU.add,
                       [Tkvt[ki], Tprm], [Tlft])
                    act(lft[0:GT, :], lft[0:GT, :], AF.Exp, [Tlft], [Tlft], scale=-1.0)
                    act(lft[0:GT, :], lft[0:GT, :], AF.Ln, [Tlft], [Tlft], bias=1.0)
                    tsc(lft[0:GT, :], lft[0:GT, :], -1.0, ALU.mult, [Tlft], [Tlft])
                    dma(lfout[sl_, orows, :], lft[0:GT, :], [Tlft], [], q="gpsimd")
                    prev = cTlast[:, 0:1] if tb == 0 else cT[:, tb * GT - 1:tb * GT]
                    c_block(lft[0:GT, :], Tlft, GT, cT[:, tb * GT:(tb + 1) * GT], TcT, prev)
                cp(cTlast[:, 0:1], cT[:, T_ - 1:T_], [TcT], [TcTl])
                for cc in range(6):
                    for hh in range(2):
                        dma(KA[si][2 * cc + hh, 0:64, t0:t0 + T_], kT[hh * 64:(hh + 1) * 64, cc, 0:T_], [TkT[cc]], [], q="gpsimd", acc=[TKA[si]])
                split_and_store(si, cT[:, 0:T_], TcT, T_, t0, arena, qa=qa)
                P.barrier()

        def layer_b(l, si, t0, T_, qa):
            j2 = l - 2
            L = t0 + T_
            NBF = L // 128
            rem = L - NBF * 128
            NB = NBF + (1 if rem else 0)
            with ExitStack() as arena:
                rms_xn(P_NMIX + 8 * l, T_, arena)
                bufs = attn_bufs(arena)
                KAb = [[sb(arena, "KAb", [70, L], BF16) for _ in range(2)], None]
                QAb = [[sb(arena, "QAb", [70, 512], BF16) for _ in range(2)] for _ in range(2)]
                Vb = [sb(arena, "Vb", [128, NB, 256], BF16), None]
                TKAb = [[T(), T()], [T(), T()]]
                TQAb = [[T(), T()], [T(), T()]]
                TVb = [T(), T()]
                Qn = sb(arena, "Qn", [128, 6, 512], BF16)
                TQn = [T() for _ in range(6)]
                qm = sb(arena, "qm", [128, 2, 512], F32)
                Tqm = [T(), T()]
                mset(Vb[0][:, :, 64:192], 1.0, [TVb[0]], eng="gpsimd")

                def issue_kv(cc):
                    st_ = cc % 2
                    for hh in range(2):
                        dma(KAb[st_][hh][:, :], KA[si][2 * cc + hh, :, 0:L], [TKA[si]], [TKAb[st_][hh]], q="sync", key=f"kab{st_}{hh}")
                    for u in range(2):
                        vc = cc * 128 + u * 64
                        dc = u * 192
                        if NBF:
                            dma(Vb[st_][:, 0:NBF, dc:dc + 64], VS[si][0:NBF * 128, vc:vc + 64].rearrange("(b p) d -> p b d", p=128),
                                [TVS[si]], [], q="sync", key=f"vb{st_}{u}", acc=[TVb[st_]])
                        if rem:
                            dma(Vb[st_][0:rem, NBF, dc:dc + 64], VS[si][NBF * 128:L, vc:vc + 64],
                                [TVS[si]], [], q="sync", key=f"vb{st_}{u}", acc=[TVb[st_]])

                def issue_q(cc):
                    st_ = cc % 2
                    for hh in range(2):
                        dma(QAb[st_][hh][:, 0:T_], QA[qa][2 * cc + hh, :, 0:T_], [TQA[qa]], [TQAb[st_][hh]], q="sync", key=f"qab{st_}{hh}")

                issue_kv(0)
                with ExitStack() as a1:
                    qf6 = sb(a1, "qf6", [128, 6, 512], F32)
                    Tqf = [T() for _ in range(6)]
                    sq6 = sb(a1, "sq6", [128, 6, 512], BF16)
                    Tsq6 = [T() for _ in range(6)]
                    rs6 = sb(a1, "rs6b", [128, 6, 512], F32)
                    Trs6 = [T() for _ in range(6)]
                    for (name, ll, r0, nk, c0, ncols) in plan_cols("w_in_b", j2, D):
                        wv, Tw = w_next((name, ll, r0, nk, c0, ncols))
                        for jj in range(ncols // 128):
                            j = c0 // 128 + jj
                            b = ps()
                            for k in range(8):
                                mm(pb[b][:, 0:T_], wv[:, k, jj * 128:(jj + 1) * 128], xn[:, k, 0:T_], k == 0, k == 7,
                                   [Tw, Txn[k]], [Tpb[b]])
                            if j < 6:
                                cp(qf6[:, j, 0:T_], pb[b][:, 0:T_], [Tpb[b]], [Tqf[j]], eng="scalar")
                                act(sq6[:, j, 0:T_], pb[b][:, 0:T_], AF.Square, [Tpb[b]], [Tsq6[j]])
                            else:
                                cp(qm[:, j - 6, 0:T_], pb[b][:, 0:T_], [Tpb[b]], [Tqm[j - 6]], eng="scalar")
                    nb_ = []
                    for j in range(6):
                        b = ps()
                        nb_.append(b)
                        mm(pb[b][:, 0:T_], bd_b, sq6[:, j, 0:T_], True, True, [Tsq6[j], Tcb], [Tpb[b]])
                    for j in range(6):
                        b = nb_[j]
                        act(rs6[:, j, 0:T_], pb[b][:, 0:T_], AF.Sqrt, [Tpb[b]], [Trs6[j]], scale=1.0 / 64, bias=EPS)
                        recip(rs6[:, j, 0:T_], rs6[:, j, 0:T_], [Trs6[j]], [Trs6[j]])
                        stt(Qn[:, j, 0:T_], qf6[:, j, 0:T_], drv[:, 12 + j2:13 + j2], rs6[:, j, 0:T_], ALU.mult, ALU.mult,
                            [Tqf[j], Trs6[j], Tdrv], [TQn[j]])
                        for hh in range(2):
                            dma(QA[qa][2 * j + hh, 0:64, 0:T_], Qn[hh * 64:(hh + 1) * 64, j, 0:T_], [TQn[j]], [], q="gpsimd", acc=[TQA[qa]])
                    P.barrier()
                issue_q(0)
                issue_q(1)
                with ExitStack() as a2:
                    mem_attend(l, qm, Tqm, T_, a2)
                    P.barrier()
                KAb[1] = [sb(arena, "KAb", [70, L], BF16) for _ in range(2)]
                Vb[1] = sb(arena, "Vb", [128, NB, 256], BF16)
                mset(Vb[1][:, :, 64:192], 1.0, [TVb[1]], eng="gpsimd")
                issue_kv(1)
                blocks = []
                for j in range(NB):
                    nk = 128 if j < NBF else rem
                    if j * 128 < t0:
                        blocks.append((j, nk, 0, False))
                    else:
                        blocks.append((j, nk, j * 128 - t0, True))
                for cc in range(6):
                    st_ = cc % 2

                    def lhsK(hh, j, nk, st_=st_):
                        return KAb[st_][hh][:, j * 128:j * 128 + nk]

                    def rhsQ(hh, q0, st_=st_):
                        return QAb[st_][hh][:, q0:T_]

                    def lhsV(hh, j, nk, st_=st_):
                        return Vb[st_][0:nk, j, 0:128] if hh == 0 else Vb[st_][0:nk, j, 128:256]
                    attn_pair(lhsK, rhsQ, lhsV, blocks, T_, mix[:, cc, 0:T_], Tmix[cc], TKAb[st_], TQAb[st_], [TVb[st_]], bufs)
                    if cc + 2 < 6:
                        issue_kv(cc + 2)
                        issue_q(cc + 2)
                P.barrier()
            out_proj_ffn(l, T_)

        def run_tile(si, sl_, t0, T_, xin_d, yout_d, kout, vout, lfout, qa, sample):
            GT = min(128, T_)
            NTB = T_ // GT
            with ExitStack() as arena:
                xin = sb(arena, "xin", [128, 4, D], F32)
                Txin = T()
                dma(xin[0:GT, 0:NTB, :], xin_d[sl_, t0 - (PAST if sample else 0):t0 - (PAST if sample else 0) + T_, :]
                    .rearrange("(b p) d -> p b d", p=GT), [], [Txin], q="sync", key="xin")
                for c in range(8):
                    b = ps()
                    for tb in range(NTB):
                        tr(pb[b][:, tb * GT:(tb + 1) * GT], xin[0:GT, tb, c * 128:(c + 1) * 128], id_f[0:GT, 0:GT], [Txin, Tcst], [Tpb[b]])
                    cp(h[:, c, 0:T_], pb[b][:, 0:T_], [Tpb[b]], [Th[c]], eng=("scalar" if c % 2 else "vector"))
                P.barrier()
            chk(2)
            layer_a(0, T_, sample)
            chk(3)
            layer_a(1, T_, sample)
            chk(4)
            kv_stage(si, sl_, t0, T_, qa, kout, vout, lfout, t0 - (PAST if sample else 0))
            chk(5)
            layer_b(2, si, t0, T_, qa)
            chk(6)
            layer_b(3, si, t0, T_, qa)
            chk(7)
            with ExitStack() as arena:
                yo = sb(arena, "yo", [128, 4, D], F32)
                Tyo = T()
                for tb in range(NTB):
                    for c4 in range(2):
                        b = ps()
                        for c in range(4):
                            cc = c4 * 4 + c
                            tr(pb[b][0:GT, c * 128:(c + 1) * 128], h[:, cc, tb * GT:(tb + 1) * GT], id_f, [Th[cc], Tcst], [Tpb[b]])
                        cp(yo[0:GT, tb, c4 * 512:(c4 + 1) * 512], pb[b][0:GT, :], [Tpb[b]], [Tyo], eng=("scalar" if c4 else "vector"))
                tq = t0 - (PAST if sample else 0)
                dma(yout_d[sl_, tq:tq + T_, :].rearrange("(b p) d -> p b d", p=GT), yo[0:GT, 0:NTB, :], [Tyo], [], q="gpsimd")
                P.barrier()

        def mem_prologue(sl_, sample):
            with ExitStack() as arena:
                mset(MVb[:, :, :, :, 64:192], 1.0, [TMV], eng="gpsimd")
                kvm = [sb(arena, "kvm", [128, 512], F32) for _ in range(2)]
                Tkvm = [T(), T()]
                sqk = sb(arena, "msq", [128, 256], F32)
                ss = sb(arena, "mss", [128, 4], F32)
                Tsqk, Tss = T(), T()
                if not sample:
                    mt = sb(arena, "mt", [128, 2, D], F32)
                    Tmt = T()
                    dma(mt[:, :, :], memp[sl_].rearrange("(b p) d -> p b d", p=128), [], [Tmt], q="sync", key="xin")
                    for c in range(8):
                        b = ps()
                        for tb in range(2):
                            tr(pb[b][:, tb * 128:(tb + 1) * 128], mt[:, tb, c * 128:(c + 1) * 128], id_f, [Tmt, Tcst], [Tpb[b]])
                        cp(h[:, c, 0:256], pb[b][:, 0:256], [Tpb[b]], [Th[c]], eng=("scalar" if c % 2 else "vector"))
                n = 0
                for l in range(4):
                    if not sample:
                        with ExitStack() as a2:
                            rms_xn(P_NMEM + 8 * l, 256, a2)
                            P.barrier()
                        spec = ("w_mem_kv", l, 0, 8, 0, 512)
                        wv, Tw = w_next(spec)
                    for mb in range(2):
                        ki = n % 2
                        n += 1
                        if not sample:
                            b = ps()
                            for k in range(8):
                                mm(pb[b][:, 0:512], xn[:, k, mb * 128:(mb + 1) * 128], wv[:, k, :], k == 0, k == 7, [Tw, Txn[k]], [Tpb[b]])
                            cp(kvm[ki][:, :], pb[b][:, :], [Tpb[b]], [Tkvm[ki]], eng="scalar")
                            k3 = kvm[ki][:, 0:256].rearrange("p (h d) -> p h d", d=64)
                            tt(sqk[:, :], kvm[ki][:, 0:256], kvm[ki][:, 0:256], ALU.mult, [Tkvm[ki]], [Tsqk])
                            P.op("vector", lambda e: e.tensor_reduce(out=ss[:, :], in_=sqk[:, :].rearrange("p (h d) -> p h d", d=64),
                                                                     axis=AX.X, op=ALU.add), reads=[Tsqk], writes=[Tss])
                            act(ss[:, :], ss[:, :], AF.Sqrt, [Tss], [Tss], scale=1.0 / 64, bias=EPS)
                            recip(ss[:, :], ss[:, :], [Tss], [Tss])
                            tt(k3, k3, ss[:, :].rearrange("p (h o) -> p h o", o=1).broadcast_to([128, 4, 64]), ALU.mult, [Tkvm[ki], Tss], [Tkvm[ki]])
                            tt(k3, k3, prm[:, P_MGK + 64 * l:P_MGK + 64 * (l + 1)].rearrange("p (o d) -> p o d", o=1).broadcast_to([128, 4, 64]),
                               ALU.mult, [Tkvm[ki], Tprm], [Tkvm[ki]])
                            dma(pmk[l, sl_, mb * 128:(mb + 1) * 128, :], kvm[ki][:, 0:256], [Tkvm[ki]], [], q="gpsimd")
                            dma(pmv[l, sl_, mb * 128:(mb + 1) * 128, :], kvm[ki][:, 256:512], [Tkvm[ki]], [], q="gpsimd")
                        else:
                            dma(kvm[ki][:, 0:256], cmk[l, sl_, mb * 128:(mb + 1) * 128, :], [], [Tkvm[ki]], q="sync", key=f"kvm{ki}")
                            dma(kvm[ki][:, 256:512], cmv[l, sl_, mb * 128:(mb + 1) * 128, :], [], [Tkvm[ki]], q="sync", key=f"kvm{ki}")
                        for cc in range(2):
                            b = ps()
                            tr(pb[b][:, 0:128], kvm[ki][:, cc * 128:(cc + 1) * 128], id_f, [Tkvm[ki], Tcst], [Tpb[b]])
                            cp(MKT[:, l, cc, mb * 128:(mb + 1) * 128], pb[b][:, 0:128], [Tpb[b]], [TMK], eng=("scalar" if cc else "vector"))
                        cp(MVb[:, l, mb, :, :].rearrange("p c (s d) -> p c s d", d=64)[:, :, 0::3, :], kvm[ki][:, 256:512].rearrange("p (c u d) -> p c u d", c=2, u=2), [Tkvm[ki]], [TMV])
                P.barrier()

        def cache_import(si, sl_):
            with ExitStack() as arena:
                ckt = [sb(arena, "ckt", [128, MAIN], F32) for _ in range(2)]
                Tckt = [T(), T()]
                kT = sb(arena, "kTc", [128, 6, 512], BF16)
                TkT = [T() for _ in range(6)]
                lft = sb(arena, "lftc", [128, 4, 12], F32)
                Tlft = T()
                cT = sb(arena, "cTc", [12, 512], F32)
                TcT = T()
                for r in range(0, PAST, 1024):
                    r1 = min(PAST, r + 1024)
                    dma(VS[si][r:r1, :], cv[sl_, r:r1, :], [], [], q="gpsimd", acc=[TVS[si]])
                mset(cTlast[:, 0:1], 0.0, [TcTl], eng="vector")
                for g0 in range(0, PAST, 512):
                    n = min(512, PAST - g0)
                    nb = n // 128
                    dma(lft[:, 0:nb, :], clf[sl_, g0:g0 + n, :].rearrange("(b p) h -> p b h", p=128), [], [Tlft], q="sync", key="lftc")
                    for tb in range(nb):
                        ki = tb % 2
                        dma(ckt[ki][:, :], ck[sl_, g0 + tb * 128:g0 + (tb + 1) * 128, :], [], [Tckt[ki]], q="sync", key=f"ckt{ki}")
                        for cc in range(6):
                            b = ps()
                            tr(pb[b][:, 0:128], ckt[ki][:, cc * 128:(cc + 1) * 128], id_f, [Tckt[ki], Tcst], [Tpb[b]])
                            cp(kT[:, cc, tb * 128:(tb + 1) * 128], pb[b][:, 0:128], [Tpb[b]], [TkT[cc]], eng=("scalar" if cc % 2 else "vector"))
                        prev = cTlast[:, 0:1] if tb == 0 else cT[:, tb * 128 - 1:tb * 128]
                        c_block(lft[:, tb, :], Tlft, 128, cT[:, tb * 128:(tb + 1) * 128], TcT, prev)
                    cp(cTlast[:, 0:1], cT[:, n - 1:n], [TcT], [TcTl])
                    for cc in range(6):
                        for hh in range(2):
                            dma(KA[si][2 * cc + hh, 0:64, g0:g0 + n], kT[hh * 64:(hh + 1) * 64, cc, 0:n], [TkT[cc]], [], q="gpsimd", acc=[TKA[si]])
                    with ExitStack() as a2:
                        split_and_store(si, cT[:, 0:n], TcT, n, g0, a2, qa=None)
                        P.barrier()
                P.barrier()

        try:
            dma(cst[:, :], consts_d, [], [Tcst], q="sync", key="cst")
            dma(prm[:, :], prm_d, [], [Tprm], q="sync", key="prm")
            for l in range(4):
                cast_weight("w_mem_kv", l)
            for l in range(4):
                if l < 2:
                    cast_weight("w_in_a", l)
                else:
                    cast_weight("w_in_b", l - 2)
                cast_weight("w_out", l)
                cast_weight("w_ffn_up", l)
                cast_weight("w_ffn_down", l)
                if l == 1:
                    cast_weight("w_kv", 0)
            cp(cb[:, 0, :], cst[:, C_ONES:C_ONES + 128], [Tcst], [Tcb])
            cp(cb[:, 1, :], cst[:, C_BD:C_BD + 128], [Tcst], [Tcb])
            cp(cb[:, 2, :], cst[:, C_ID:C_ID + 128], [Tcst], [Tcb])
            cp(cb[:, 3, :], cst[:, C_TRI:C_TRI + 128], [Tcst], [Tcb])
            mset(drv[:, :], 1.0, [Tdrv], eng="vector")
            tt(drv[:, 6:12], prm[:, P_LB:P_LB + 6], prm[:, P_LB + 6:P_LB + 12], ALU.subtract, [Tprm], [Tdrv])
            act(drv[:, 6:12], drv[:, 6:12], AF.Sigmoid, [Tdrv], [Tdrv])
            tsc(drv[:, 12:14], prm[:, P_FGQ:P_FGQ + 2], 0.125, ALU.mult, [Tprm], [Tdrv])
            tsc(drv[:, 16:20], prm[:, P_MGQ:P_MGQ + 4], 0.125, ALU.mult, [Tprm], [Tdrv])

            NT = SEQ // 512
            for s in range(NS):
                wplan.extend([("w_mem_kv", l, 0, 8, 0, 512) for l in range(4)])
                for _ in range(NT):
                    wplan.extend(plan_tile())
            for s in range(NSS):
                wplan.extend(plan_tile())

            chk(0)
            for s in range(NS):
                mem_prologue(s, False)
                chk(1)
                for l in range(2):
                    mset(S32[l][:], 0.0, TS32[l], eng="vector")
                    mset(Sbf[l][:], 0.0, TSbf[l], eng="gpsimd")
                mset(cTlast[:, 0:1], 0.0, [TcTl], eng="vector")
                for ti in range(NT):
                    run_tile(s, s, ti * 512, 512, xp, yp, pk, pv, plf, ti % 2, False)
                for l in range(2):
                    dma(pst[l][s].rearrange("h k v -> k h v"), S32[l][:, :, :], TS32[l], [], q="gpsimd")
                P.barrier()
            for s in range(NSS):
                si = NS + s
                mem_prologue(s, True)
                for l in range(2):
                    dma(S32[l][:, :, :], st_in[l][s].rearrange("h k v -> k h v"), [], TS32[l], q="sync", key="stin")
                    for hh in range(6):
                        cp(Sbf[l][:, hh, :], S32[l][:, hh, :], [TS32[l][hh]], [TSbf[l][hh]], eng="gpsimd")
                cache_import(si, s)
                run_tile(si, s, PAST, TS, xs, ys, sk, sv, slf, 0, True)
                for l in range(2):
                    dma(sst[l][s].rearrange("h k v -> k h v"), S32[l][:, :, :], TS32[l], [], q="gpsimd")
                P.barrier()
        except _Stop:
            wstate["used"] = len(wplan)
        assert P.dead or wstate["used"] == len(wplan), (wstate, len(wplan))
        P.emit()
    return nc, P


def make_consts():
    c = np.zeros((128, CW), np.float32)
    c[:, C_ONES:C_ONES + 128] = 1.0
    c[:64, C_BD:C_BD + 64] = 1.0
    c[64:, C_BD + 64:C_BD + 128] = 1.0
    c[:, C_ID:C_ID + 128] = np.eye(128, dtype=np.float32)
    c[:, C_TRI:C_TRI + 128] = np.triu(np.ones((128, 128), np.float32))
    sw = np.zeros((128, 128), np.float32)
    for k in range(128):
        sw[k, (k + 64) % 128] = 1.0
    c[:, C_SWAP:C_SWAP + 128] = sw
    r = np.ones(512, np.float32)
    r[0::64] = 0.0
    c[:, C_RESET:C_RESET + 512] = r[None, :]
    return c


def make_params(norm_mix, norm_ffn, norm_mem, norm_kv, lb_logits, hg_gnorm, fox_gq, mem_gq, fox_gk, mem_gk, b_f):
    p = np.zeros((128, PW), np.float32)

    def fm(v):
        v = np.asarray(v, np.float32).reshape(-1, 8, 128)
        return v.transpose(2, 0, 1).reshape(128, -1)
    p[:, P_NMIX:P_NMIX + 32] = fm(norm_mix)
    p[:, P_NFFN:P_NFFN + 32] = fm(norm_ffn)
    p[:, P_NMEM:P_NMEM + 32] = fm(norm_mem)
    p[:, P_NKV:P_NKV + 8] = fm(np.asarray(norm_kv)[None])
    p[:, P_LB:P_LB + 12] = np.asarray(lb_logits, np.float32).reshape(2, 6, 128).transpose(2, 0, 1).reshape(128, 12)
    p[:, P_GN:P_GN + 2] = np.asarray(hg_gnorm, np.float32).T
    p[:, P_FGQ:P_FGQ + 2] = np.tile(np.asarray(fox_gq, np.float32).T, (2, 1))
    p[:, P_MGQ:P_MGQ + 4] = np.tile(np.asarray(mem_gq, np.float32).T, (2, 1))
    p[:, P_FGK:P_FGK + 64] = np.asarray(fox_gk, np.float32)[None, :]
    p[:, P_MGK:P_MGK + 256] = np.asarray(mem_gk, np.float32).reshape(1, 256)
    p[:, P_BF:P_BF + 12] = np.asarray(b_f, np.float32)[None, :]
    return p


_CACHE = {}


def run(inputs, n_cores, NS, SEQ, NSS, PAST):
    key = (NS, SEQ, NSS, PAST)
    if key not in _CACHE:
        _CACHE[key] = build(NS, SEQ, NSS, PAST)[0]
    nc = _CACHE[key]
    f = lambda a: np.ascontiguousarray(np.asarray(a, np.float32))
    consts = make_consts()
    prm = make_params(inputs["norm_mix"], inputs["norm_ffn"], inputs["norm_mem"], inputs["norm_kv"], inputs["lb_logits"],
                      inputs["hg_gnorm"], inputs["fox_gq"], inputs["mem_gq"], inputs["fox_gk"], inputs["mem_gk"], inputs["b_f"])
    shared = {"consts": consts, "prm": prm}
    for k in ("w_in_a", "w_in_b", "w_mem_kv", "w_out", "w_ffn_up", "w_ffn_down"):
        shared[k] = f(inputs[k])
    shared["w_kv"] = f(inputs["w_kv"])[None]
    in_maps = []
    for c in range(n_cores):
        ps_ = slice(c * NS, (c + 1) * NS)
        ss_ = slice(c * NSS, (c + 1) * NSS)
        m = dict(shared)
        m["xp"] = f(inputs["x_prompt"][ps_])
        m["xs"] = f(inputs["x_sample"][ss_])
        m["memp"] = f(inputs["mem_prompt"][ps_])
        m["st0"] = f(inputs["state_hgrn_0"][ss_])
        m["st1"] = f(inputs["state_hgrn_1"][ss_])
        m["ck"] = f(inputs["cache_fox_k"][ss_]).reshape(NSS, PAST, MAIN)
        m["cv"] = f(inputs["cache_fox_v"][ss_]).reshape(NSS, PAST, MAIN)
        m["clf"] = f(inputs["cache_fox_logf"][ss_])
        m["cmk"] = f(inputs["cache_mem_k"][:, ss_]).reshape(4, NSS, N_MEM, 256)
        m["cmv"] = f(inputs["cache_mem_v"][:, ss_]).reshape(4, NSS, N_MEM, 256)
        in_maps.append(m)
    res = run_bass_kernel_spmd(nc, in_maps, core_ids=list(range(n_cores)))
    R = res.results
    cat = lambda k, ax=0: np.concatenate([np.asarray(r[k]) for r in R], axis=ax)
    B = n_cores * NS
    BS = n_cores * NSS
    outs = (
        cat("yp"), cat("ys"), cat("pst0"), cat("pst1"),
        cat("pk").reshape(B, SEQ, 12, 64), cat("pv").reshape(B, SEQ, 12, 64), cat("plf"),
        cat("pmk", 1).reshape(4, B, N_MEM, 4, 64), cat("pmv", 1).reshape(4, B, N_MEM, 4, 64),
        cat("sst0"), cat("sst1"),
        cat("sk").reshape(BS, TS, 12, 64), cat("sv").reshape(BS, TS, 12, 64), cat("slf"),
    )
    return tuple(np.ascontiguousarray(o, dtype=np.float32) for o in outs)


def kernel(**inputs):
    return run(inputs, 8, 2, 4096, 2, 4096)
```

```python
import numpy as np
from contextlib import ExitStack
import concourse.bass as bass
import concourse.mybir as mybir
from concourse.bass_utils import run_bass_kernel_spmd

F32 = mybir.dt.float32
BF16 = mybir.dt.bfloat16
AF = mybir.ActivationFunctionType
ALU = mybir.AluOpType
AX = mybir.AxisListType

D = 1024
DFF = 2816
MAIN = 768
A_IN = 3328
KV_OUT = 1548
EPS = 1e-6
K_MAX = 0.999999
N_MEM = 256
TS = 16
NW = 4
WELEM = 4096

C_ONES, C_BD, C_ID, C_TRI, C_SWAP, C_RESET, CW = 0, 128, 256, 384, 512, 640, 1152
P_NMIX, P_NFFN, P_NMEM, P_NKV, P_LB, P_GN, P_FGQ, P_MGQ, P_FGK, P_MGK, P_BF, PW = 0, 32, 64, 96, 104, 116, 118, 120, 124, 188, 444, 456


class T:
    __slots__ = ("name", "w", "r", "wa", "excl")

    def __init__(self, name="", excl=False):
        self.name = name
        self.w = None
        self.r = []
        self.wa = []
        self.excl = excl


class Prog:
    ENG = ("tensor", "vector", "scalar", "gpsimd", "sync")

    def __init__(self, nc):
        self.nc = nc
        self.ops = {e: [] for e in self.ENG}
        self.cnt = {}
        self.waited = {e: {} for e in self.ENG}
        self.semkeys = []
        for e in self.ENG:
            self._newsem("E_" + e)
        self.n_ops = 0
        self.dead = False

    def _newsem(self, key):
        self.semkeys.append(key)
        self.cnt[key] = 0

    def dma_sem(self, key):
        k = "D_" + key
        if k not in self.cnt:
            self._newsem(k)
        return k

    def op(self, eng, fn, reads=(), writes=(), dma=None, writes_acc=()):
        waits = {}

        def need(dep, same_ok):
            if dep is None:
                return
            key, val, deng = dep
            if deng == eng and not same_ok and not key.startswith("D_"):
                return
            if waits.get(key, 0) < val:
                waits[key] = val

        if self.dead:
            return ("E_" + eng, self.cnt["E_" + eng], eng)
        for t in reads:
            need(t.w, True)
            for r in t.wa:
                need(r, True)
            if t.excl:
                for r in t.r:
                    need(r, False)
        for t in writes:
            need(t.w, False)
            for r in t.wa:
                need(r, False)
            for r in t.r:
                need(r, False)
        for t in writes_acc:
            for r in t.r:
                need(r, False)
        wl = []
        wd = self.waited[eng]
        for key, val in waits.items():
            if wd.get(key, 0) < val:
                wd[key] = val
                wl.append((key, val))
        if dma is None:
            key = "E_" + eng
            self.cnt[key] += 1
            inc = 1
        else:
            key = dma
            self.cnt[key] += 16
            inc = 16
        me = (key, self.cnt[key], eng)
        self.ops[eng].append((wl, fn, key, inc))
        for t in writes:
            t.w = me
            t.r = []
            t.wa = []
        for t in writes_acc:
            t.wa.append(me)
        for t in reads:
            t.r.append(me)
        self.n_ops += 1
        return me

    def barrier(self, force=False):
        if self.dead and not force:
            return
        snap = dict(self.cnt)
        for e in self.ENG:
            wl = []
            for key, val in snap.items():
                if val == 0 or key == "E_" + e:
                    continue
                if self.waited[e].get(key, 0) < val:
                    self.waited[e][key] = val
                    wl.append((key, val))
            if wl:
                self.ops[e].append((wl, None, None, 0))

    def emit(self):
        nc = self.nc
        self.barrier(force=True)
        with ExitStack() as st:
            sems = {}
            for k in self.semkeys:
                sems[k] = st.enter_context(nc.semaphore(k))
            block = st.enter_context(nc.Block())
            for e in self.ENG:
                ops = self.ops[e]
                if not ops:
                    continue

                def body(eng, ops=ops):
                    for wl, fn, key, inc in ops:
                        for wk, wv in wl:
                            eng.wait_ge(sems[wk], wv)
                        if fn is not None:
                            fn(eng).then_inc(sems[key], inc)
                getattr(block, e)(body)


class _Stop(Exception):
    pass


DBG_STOP = [None]


_PROG = [None]


def chk(n):
    if DBG_STOP[0] == n:
        _PROG[0].dead = True


def build(NS, SEQ, NSS, PAST):
    nc = bass.Bass("TRN2", target_bir_lowering=False)

    def dram(name, shape, dtype, kind):
        return nc.dram_tensor(name, list(shape), dtype, kind=kind).ap()

    I, O, S = "ExternalInput", "ExternalOutput", "Internal"
    LS = PAST + TS
    xp = dram("xp", [NS, SEQ, D], F32, I)
    xs = dram("xs", [NSS, TS, D], F32, I)
    memp = dram("memp", [NS, N_MEM, D], F32, I)
    st_in = [dram("st0", [NSS, 6, 128, 128], F32, I), dram("st1", [NSS, 6, 128, 128], F32, I)]
    ck = dram("ck", [NSS, PAST, MAIN], F32, I)
    cv = dram("cv", [NSS, PAST, MAIN], F32, I)
    clf = dram("clf", [NSS, PAST, 12], F32, I)
    cmk = dram("cmk", [4, NSS, N_MEM, 256], F32, I)
    cmv = dram("cmv", [4, NSS, N_MEM, 256], F32, I)
    consts_d = dram("consts", [128, CW], F32, I)
    prm_d = dram("prm", [128, PW], F32, I)
    wshapes = {"w_in_a": [2, D, A_IN], "w_in_b": [2, D, D], "w_kv": [1, D, KV_OUT], "w_mem_kv": [4, D, 512],
               "w_out": [4, D, D], "w_ffn_up": [4, D, 2 * DFF], "w_ffn_down": [4, DFF, D]}
    wf = {k: dram(k, v, F32, I) for k, v in wshapes.items()}
    wb = {k: dram(k + "_b", v, BF16, S) for k, v in wshapes.items()}
    yp = dram("yp", [NS, SEQ, D], F32, O)
    ys = dram("ys", [NSS, TS, D], F32, O)
    pst = [dram("pst0", [NS, 6, 128, 128], F32, O), dram("pst1", [NS, 6, 128, 128], F32, O)]
    pk = dram("pk", [NS, SEQ, MAIN], F32, O)
    pv = dram("pv", [NS, SEQ, MAIN], F32, O)
    plf = dram("plf", [NS, SEQ, 12], F32, O)
    pmk = dram("pmk", [4, NS, N_MEM, 256], F32, O)
    pmv = dram("pmv", [4, NS, N_MEM, 256], F32, O)
    sst = [dram("sst0", [NSS, 6, 128, 128], F32, O), dram("sst1", [NSS, 6, 128, 128], F32, O)]
    sk = dram("sk", [NSS, TS, MAIN], F32, O)
    sv = dram("sv", [NSS, TS, MAIN], F32, O)
    slf = dram("slf", [NSS, TS, 12], F32, O)
    KA = [dram(f"KA{i}", [12, 70, SEQ if i < NS else LS], BF16, S) for i in range(NS + NSS)]
    VS = [dram(f"VS{i}", [SEQ if i < NS else LS, MAIN], BF16, S) for i in range(NS + NSS)]
    QA = [dram(f"QA{i}", [12, 70, 512], BF16, S) for i in range(2)]

    P = Prog(nc)
    _PROG[0] = P
    uid = [0]

    with ExitStack() as top:
        def sb(st, name, shape, dtype):
            uid[0] += 1
            return st.enter_context(nc.sbuf_tensor(f"{name}_{uid[0]}", list(shape), dtype))

        cst = sb(top, "cst", [128, CW], F32)
        prm = sb(top, "prm", [128, PW], F32)
        cb = sb(top, "cb", [128, 5, 128], BF16)
        drv = sb(top, "drv", [128, 32], F32)
        h = sb(top, "h", [128, 8, 512], F32)
        xn = sb(top, "xn", [128, 8, 512], BF16)
        mix = sb(top, "mix", [128, 8, 512], BF16)
        wbuf = [sb(top, f"wbuf{i}", [128, WELEM], BF16) for i in range(NW)]
        S32 = [sb(top, f"S32_{l}", [128, 6, 128], F32) for l in range(2)]
        Sbf = [sb(top, f"Sbf_{l}", [128, 6, 128], BF16) for l in range(2)]
        MKT = sb(top, "MKT", [128, 4, 2, 256], BF16)
        MVb = sb(top, "MVb", [128, 4, 2, 2, 256], BF16)
        cTlast = sb(top, "cTlast", [12, 2], F32)
        pb = [top.enter_context(nc.psum_tensor(f"pb{i}", [128, 512], F32)) for i in range(8)]
        Tpb = [T(f"pb{i}", excl=True) for i in range(8)]
        Tst = [T(f"st{i}") for i in range(8)]
        Tcst, Tprm, Tcb, Tdrv = T("cst"), T("prm"), T("cb"), T("drv")
        Th = [T(f"h{c}") for c in range(8)]
        Txn = [T(f"xn{c}") for c in range(8)]
        Tmix = [T(f"mix{c}") for c in range(8)]
        Twbuf = [T(f"wbuf{i}") for i in range(NW)]
        TS32 = [[T() for _ in range(6)] for _ in range(2)]
        TSbf = [[T() for _ in range(6)] for _ in range(2)]
        TMK, TMV, TcTl = T("MKT"), T("MVb"), T("cTlast")
        psn = [0]

        def ps():
            psn[0] = (psn[0] + 1) % 8
            return psn[0]

        ones_b, bd_b, id_b, tri_b = cb[:, 0, :], cb[:, 1, :], cb[:, 2, :], cb[:, 3, :]
        ones_f = cst[:, C_ONES:C_ONES + 128]
        id_f = cst[:, C_ID:C_ID + 128]
        tri_f = cst[:, C_TRI:C_TRI + 128]
        swap_f = cst[:, C_SWAP:C_SWAP + 128]
        reset_f = cst[:, C_RESET:C_RESET + 512]

        def pcol(c):
            return prm[:, c:c + 1]

        def mm(out, lhsT, rhs, start, stop, reads, writes):
            P.op("tensor", lambda e: e.matmul(out, lhsT, rhs, start=start, stop=stop), reads=reads, writes=writes)

        def tr(out, in_, ident, reads, writes):
            P.op("tensor", lambda e: e.transpose(out, in_, ident), reads=reads, writes=writes)

        def act(out, in_, func, reads, writes, scale=1.0, bias=None):
            if bias is None:
                P.op("scalar", lambda e: e.activation(out=out, in_=in_, func=func, scale=scale), reads=reads, writes=writes)
            else:
                P.op("scalar", lambda e: e.activation(out=out, in_=in_, func=func, scale=scale, bias=bias), reads=reads, writes=writes)

        def tt(out, in0, in1, op, reads, writes, eng="vector"):
            P.op(eng, lambda e: e.tensor_tensor(out=out, in0=in0, in1=in1, op=op), reads=reads, writes=writes)

        def tsc(out, in0, s1, op0, reads, writes, s2=None, op1=None, eng="vector"):
            if op1 is None:
                P.op(eng, lambda e: e.tensor_scalar(out=out, in0=in0, scalar1=s1, scalar2=None, op0=op0), reads=reads, writes=writes)
            else:
                P.op(eng, lambda e: e.tensor_scalar(out=out, in0=in0, scalar1=s1, scalar2=s2, op0=op0, op1=op1), reads=reads, writes=writes)

        def stt(out, in0, scalar, in1, op0, op1, reads, writes):
            P.op("vector", lambda e: e.scalar_tensor_tensor(out=out, in0=in0, scalar=scalar, in1=in1, op0=op0, op1=op1),
                 reads=reads, writes=writes)

        def cp(out, in_, reads, writes, eng="vector"):
            if eng == "scalar":
                P.op("scalar", lambda e: e.copy(out=out, in_=in_), reads=reads, writes=writes)
            else:
                P.op(eng, lambda e: e.tensor_copy(out=out, in_=in_), reads=reads, writes=writes)

        def recip(out, in_, reads, writes):
            P.op("vector", lambda e: e.reciprocal(out=out, in_=in_), reads=reads, writes=writes)

        def mset(ap, val, writes, eng="gpsimd"):
            P.op(eng, lambda e: e.memset(ap, val), writes=writes)

        dman = [0]

        def dma(out, in_, reads, writes, q="sync", key=None, acc=()):
            if key is None:
                dman[0] += 1
                key = f"g{dman[0] % 48}_{q}"
            return P.op(q, lambda e: e.dma_start(out=out, in_=in_), reads=reads, writes=writes, dma=P.dma_sem(key),
                        writes_acc=acc)

        Twb = {}
        ncast = [0]

        def cast_weight(name, l):
            shp = wshapes[name]
            rows, cols = shp[1], shp[2]
            step = max(128, (2 * 1024 * 1024 // cols) // 128 * 128)
            ts_ = []
            r = 0
            while r < rows:
                r1 = min(rows, r + step)
                t = T(f"{name}{l}_{r}")
                ncast[0] += 1
                dma(wb[name][l, r:r1, :], wf[name][l, r:r1, :], [], [t], q="gpsimd", key=f"wcast{ncast[0] % 16}")
                ts_.append(t)
                r = r1
            Twb[(name, l)] = ts_

        wplan = []
        wstate = {"issued": 0, "used": 0}

        def w_issue():
            while wstate["issued"] < len(wplan) and wstate["issued"] < wstate["used"] + NW:
                i = wstate["issued"]
                name, l, row0, nk, c0, ncols = wplan[i]
                slot = i % NW
                dst = wbuf[slot][:, 0:nk * ncols].rearrange("p (k c) -> p k c", c=ncols)
                src = wb[name][l, row0:row0 + nk * 128, c0:c0 + ncols].rearrange("(k p) c -> p k c", p=128)
                dma(dst, src, Twb[(name, l)], [Twbuf[slot]], q="sync", key=f"w{slot}")
                wstate["issued"] += 1

        def w_next(spec):
            i = wstate["used"]
            assert wplan[i] == spec, (i, wplan[i], spec)
            w_issue()
            wstate["used"] += 1
            slot = i % NW
            nk, ncols = spec[3], spec[5]
            return wbuf[slot][:, 0:nk * ncols].rearrange("p (k c) -> p k c", c=ncols), Twbuf[slot]

        def plan_cols(name, l, total, step=512, nk=8):
            return [(name, l, 0, nk, c, min(step, total - c)) for c in range(0, total, step)]

        def plan_down(l):
            out = []
            for jp in range(4):
                for kh in range(2):
                    out.append(("w_ffn_down", l, kh * 11 * 128, 11, jp * 256, 256))
            return out

        def plan_tile():
            pl = []
            for l in range(4):
                if l < 2:
                    pl += plan_cols("w_in_a", l, A_IN)
                else:
                    pl += plan_cols("w_in_b", l - 2, D)
                pl += plan_cols("w_out", l, D)
                pl += plan_cols("w_ffn_up", l, 2 * DFF)
                pl += plan_down(l)
                if l == 1:
                    pl += plan_cols("w_kv", 0, KV_OUT)
            return pl

        def rms_xn(gcol0, T_, arena):
            sqr = [sb(arena, "sqr", [128, 512], BF16) for _ in range(2)]
            Tsq = [T(), T()]
            rs = sb(arena, "rs", [128, 512], F32)
            Trs = T()
            b = ps()
            for c in range(8):
                i = c % 2
                act(sqr[i][:, 0:T_], h[:, c, 0:T_], AF.Square, [Th[c]], [Tsq[i]])
                mm(pb[b][:, 0:T_], ones_b, sqr[i][:, 0:T_], c == 0, c == 7, [Tsq[i], Tcb], [Tpb[b]])
            act(rs[:, 0:T_], pb[b][:, 0:T_], AF.Ln, [Tpb[b]], [Trs], scale=1.0 / D, bias=EPS)
            act(rs[:, 0:T_], rs[:, 0:T_], AF.Exp, [Trs], [Trs], scale=-0.5)
            for c in range(8):
                stt(xn[:, c, 0:T_], h[:, c, 0:T_], pcol(gcol0 + c), rs[:, 0:T_], ALU.mult, ALU.mult,
                    [Th[c], Trs, Tprm], [Txn[c]])

        def rstd_bufs(arena, n=2):
            return {"i": 0, "b": [(sb(arena, "hsq", [128, 512], BF16), T(), sb(arena, "hrs", [128, 512], F32), T()) for _ in range(n)]}

        def head_rstd(src, Tsrc, n, onesap, T_, rb, bank=None):
            rb["i"] = (rb["i"] + 1) % len(rb["b"])
            sq, Tsq, rs, Trs = rb["b"][rb["i"]]
            act(sq[:, 0:T_], src, AF.Square, [Tsrc], [Tsq])
            b = ps() if bank is None else bank
            mm(pb[b][:, 0:T_], onesap, sq[:, 0:T_], True, True, [Tsq, Tcb], [Tpb[b]])
            act(rs[:, 0:T_], pb[b][:, 0:T_], AF.Ln, [Tpb[b]], [Trs], scale=1.0 / n, bias=EPS)
            act(rs[:, 0:T_], rs[:, 0:T_], AF.Exp, [Trs], [Trs], scale=-0.5)
            return rs, Trs

        def attn_pair(lhsK, rhsQ, lhsV, blocks, nq, out_ap, Tout, Kreads, Qreads, Vreads, arena_bufs):
            Pt, TPt, R, TR, comb, Tcomb = arena_bufs
            Ob = []
            for hh in range(2):
                ob = ps()
                while ob in Ob:
                    ob = ps()
                Ob.append(ob)
            nb = len(blocks)
            items = [(hh, bi) + tuple(blocks[bi]) for hh in range(2) for bi in range(nb)]
            DEPTH = 2
            NP_ = len(Pt)

            def issue_s(idx):
                hh, bi, j, nk, q0, diag = items[idx]
                w = nq - q0
                sbk = ps()
                while sbk in Ob:
                    sbk = ps()
                mm(pb[sbk][0:nk, 0:w], lhsK(hh, j, nk), rhsQ(hh, q0), True, True, Kreads + Qreads, [Tpb[sbk]])
                pi = idx % NP_
                act(Pt[pi][0:nk, 0:w], pb[sbk][0:nk, 0:w], AF.Exp, [Tpb[sbk]], [TPt[pi]])
                if diag:
                    tt(Pt[pi][0:nk, 0:nk], Pt[pi][0:nk, 0:nk], tri_b[0:nk, 0:nk], ALU.mult, [TPt[pi], Tcb], [TPt[pi]],
                       eng="gpsimd")

            def issue_pv(idx):
                hh, bi, j, nk, q0, diag = items[idx]
                w = nq - q0
                pi = idx % NP_
                mm(pb[Ob[hh]][:, q0:nq], lhsV(hh, j, nk), Pt[pi][0:nk, 0:w], bi == 0, bi == nb - 1,
                   [TPt[pi]] + Vreads, [Tpb[Ob[hh]]])

            for idx in range(len(items)):
                issue_s(idx)
                if idx >= DEPTH:
                    issue_pv(idx - DEPTH)
            for idx in range(max(0, len(items) - DEPTH), len(items)):
                issue_pv(idx)
            oa, ob_ = Ob
            act(R[0:64, 0:nq], pb[ob_][0:64, 0:nq], AF.Ln, [Tpb[ob_]], [TR])
            act(R[64:128, 0:nq], pb[oa][64:128, 0:nq], AF.Ln, [Tpb[oa]], [TR])
            act(R[:, 0:nq], R[:, 0:nq], AF.Exp, [TR], [TR], scale=-1.0)
            pr = ps()
            while pr in Ob:
                pr = ps()
            mm(pb[pr][:, 0:nq], swap_f, R[:, 0:nq], True, True, [TR, Tcst], [Tpb[pr]])
            cp(comb[0:64, 0:nq], pb[oa][0:64, 0:nq], [Tpb[oa]], [Tcomb], eng="scalar")
            cp(comb[64:128, 0:nq], pb[ob_][64:128, 0:nq], [Tpb[ob_]], [Tcomb], eng="scalar")
            tt(out_ap, comb[:, 0:nq], pb[pr][:, 0:nq], ALU.mult, [Tcomb, Tpb[pr]], [Tout])

        def attn_bufs(arena):
            Pt = [sb(arena, "Pt", [128, 512], BF16) for _ in range(4)]
            R = sb(arena, "R", [128, 512], F32)
            comb = sb(arena, "comb", [128, 512], F32)
            return Pt, [T(), T(), T(), T()], R, T(), comb, T()

        def mem_attend(l, qm, Tqm, T_, arena):
            bufs = attn_bufs(arena)
            Qm = sb(arena, "Qm", [128, 2, 512], BF16)
            TQm = [T(), T()]
            rb = rstd_bufs(arena)
            for cc in range(2):
                rs, Trs = head_rstd(qm[:, cc, 0:T_], Tqm[cc], 64, bd_b, T_, rb)
                stt(Qm[:, cc, 0:T_], qm[:, cc, 0:T_], drv[:, 16 + l:17 + l], rs[:, 0:T_], ALU.mult, ALU.mult,
                    [Tqm[cc], Trs, Tdrv], [TQm[cc]])
            blocks = [(0, 128, 0, False), (1, 128, 0, False)]
            for cc in range(2):
                def lhsK(hh, j, nk, cc=cc):
                    return MKT[hh * 64:(hh + 1) * 64, l, cc, j * 128:(j + 1) * 128]

                def rhsQ(hh, q0, cc=cc):
                    return Qm[hh * 64:(hh + 1) * 64, cc, 0:T_]

                def lhsV(hh, j, nk, cc=cc):
                    return MVb[:, l, j, cc, 0:128] if hh == 0 else MVb[:, l, j, cc, 128:256]
                attn_pair(lhsK, rhsQ, lhsV, blocks, T_, mix[:, 6 + cc, 0:T_], Tmix[6 + cc], [TMK], [TQm[cc]], [TMV], bufs)

        def out_proj_ffn(l, T_):
            for (name, ll, r0, nk, c0, ncols) in plan_cols("w_out", l, D):
                wv, Tw = w_next((name, ll, r0, nk, c0, ncols))
                for jj in range(ncols // 128):
                    j = c0 // 128 + jj
                    b = ps()
                    for k in range(8):
                        mm(pb[b][:, 0:T_], wv[:, k, jj * 128:(jj + 1) * 128], mix[:, k, 0:T_], k == 0, k == 7,
                           [Tw, Tmix[k]], [Tpb[b]])
                    tt(h[:, j, 0:T_], h[:, j, 0:T_], pb[b][:, 0:T_], ALU.add, [Th[j], Tpb[b]], [Th[j]])
            P.barrier()
            with ExitStack() as arena:
                rms_xn(P_NFFN + 8 * l, T_, arena)
                hid = sb(arena, "hid", [128, 22, 512], BF16)
                Thid = [T() for _ in range(22)]
                for (name, ll, r0, nk, c0, ncols) in plan_cols("w_ffn_up", l, 2 * DFF):
                    wv, Tw = w_next((name, ll, r0, nk, c0, ncols))
                    for jj in range(ncols // 128):
                        j = c0 // 128 + jj
                        b = ps()
                        for k in range(8):
                            mm(pb[b][:, 0:T_], wv[:, k, jj * 128:(jj + 1) * 128], xn[:, k, 0:T_], k == 0, k == 7,
                               [Tw, Txn[k]], [Tpb[b]])
                        if j < 22:
                            act(hid[:, j, 0:T_], pb[b][:, 0:T_], AF.Silu, [Tpb[b]], [Thid[j]])
                        else:
                            tt(hid[:, j - 22, 0:T_], pb[b][:, 0:T_], hid[:, j - 22, 0:T_], ALU.mult,
                               [Tpb[b], Thid[j - 22]], [Thid[j - 22]])
                for jp in range(4):
                    bs = [ps(), ps()]
                    for kh in range(2):
                        spec = ("w_ffn_down", l, kh * 11 * 128, 11, jp * 256, 256)
                        wv, Tw = w_next(spec)
                        for jj in range(2):
                            for k in range(11):
                                kk = kh * 11 + k
                                mm(pb[bs[jj]][:, 0:T_], wv[:, k, jj * 128:(jj + 1) * 128], hid[:, kk, 0:T_],
                                   kk == 0, kk == 21, [Tw, Thid[kk]], [Tpb[bs[jj]]])
                    for jj in range(2):
                        j = jp * 2 + jj
                        tt(h[:, j, 0:T_], h[:, j, 0:T_], pb[bs[jj]][:, 0:T_], ALU.add, [Th[j], Tpb[bs[jj]]], [Th[j]])
                P.barrier()

        def layer_a(l, T_, seq_is_sample):
            CL = min(64, T_)
            NCH = T_ // CL
            G = min(2, NCH)
            GT = G * CL
            NG = NCH // G
            with ExitStack() as arena:
                rms_xn(P_NMIX + 8 * l, T_, arena)
                sqb = sb(arena, "sqb", [128, 6, 512], BF16)
                kb = sb(arena, "kb", [128, 6, 512], BF16)
                bfa = sb(arena, "bfa", [128, 6, 512], F32)
                gate = sb(arena, "gate", [128, 6, 512], BF16)
                qm = sb(arena, "qm", [128, 2, 512], F32)
                Vt = sb(arena, "Vt", [128, 4, MAIN], BF16)
                tmp = [sb(arena, "tmpa", [128, 512], F32) for _ in range(3)]
                Ttmp = [T(), T(), T()]
                Tsq_, Tkb, Tbf, Tgate, Tqm = ([T() for _ in range(6)] for _ in range(5))
                Tqm = [T(), T()]
                TVt = [T() for _ in range(4)]
                tn = [0]

                def nxt():
                    tn[0] = (tn[0] + 1) % 3
                    return tn[0]
                Qp = sb(arena, "Qp", [128, 6, 512], BF16)
                Qt = sb(arena, "Qt", [128, 6, 512], BF16)
                Kt = sb(arena, "Kt", [128, 6, 512], BF16)
                KhT = sb(arena, "KhT", [128, 6, 512], BF16)
                Kh = sb(arena, "Kh", [128, 4, MAIN], BF16)
                dec = sb(arena, "dec", [128, 6, 8], F32)
                ATt = sb(arena, "ATt", [128, 6, 4, 128], BF16)
                rs6 = sb(arena, "rs6", [128, 6, 512], F32)
                TQp, TQt, TKt, TKhT, Tdec = ([T() for _ in range(6)] for _ in range(5))
                TKh = [T() for _ in range(4)]
                TAT = [[T() for _ in range(4)] for _ in range(6)]
                mset(ATt[:], 0.0, [t for row in TAT for t in row], eng="gpsimd")
                def hgrn_elem(hh):
                    P.op("vector", lambda e, hh=hh: e.tensor_tensor_scan(out=bfa[:, hh, 0:T_], data0=reset_f[:, 0:T_],
                                                                          data1=bfa[:, hh, 0:T_], initial=0.0,
                                                                          op0=ALU.mult, op1=ALU.add),
                         reads=[Tbf[hh], Tcst], writes=[Tbf[hh]])
                    b3 = bfa[:, hh, 0:T_].rearrange("p (c l) -> p c l", l=CL)
                    mid = b3[:, :, CL // 2 - 1:CL // 2].broadcast_to([128, NCH, CL])
                    last = b3[:, :, CL - 1:CL].broadcast_to([128, NCH, CL])
                    i0 = nxt()
                    act(tmp[i0][:, 0:T_], bfa[:, hh, 0:T_], AF.Exp, [Tbf[hh]], [Ttmp[i0]])
                    tt(Qp[:, hh, 0:T_], sqb[:, hh, 0:T_], tmp[i0][:, 0:T_], ALU.mult, [Tsq_[hh], Ttmp[i0]], [TQp[hh]])
                    e3 = tmp[i0][:, 0:T_].rearrange("p (c l) -> p c l", l=CL)
                    cp(dec[:, hh, 0:NCH], e3[:, :, CL - 1], [Ttmp[i0]], [Tdec[hh]])
                    i1 = nxt()
                    t3 = tmp[i1][:, 0:T_].rearrange("p (c l) -> p c l", l=CL)
                    tt(t3, b3, mid, ALU.subtract, [Tbf[hh]], [Ttmp[i1]])
                    i2 = nxt()
                    act(tmp[i2][:, 0:T_], tmp[i1][:, 0:T_], AF.Exp, [Ttmp[i1]], [Ttmp[i2]])
                    tt(Qt[:, hh, 0:T_], sqb[:, hh, 0:T_], tmp[i2][:, 0:T_], ALU.mult, [Tsq_[hh], Ttmp[i2]], [TQt[hh]])
                    act(tmp[i2][:, 0:T_], tmp[i1][:, 0:T_], AF.Exp, [Ttmp[i1], TQt[hh]], [Ttmp[i2]], scale=-1.0)
                    tt(Kt[:, hh, 0:T_], kb[:, hh, 0:T_], tmp[i2][:, 0:T_], ALU.mult, [Tkb[hh], Ttmp[i2]], [TKt[hh]])
                    tt(t3, b3, last, ALU.subtract, [Tbf[hh], TKt[hh]], [Ttmp[i1]])
                    act(tmp[i1][:, 0:T_], tmp[i1][:, 0:T_], AF.Exp, [Ttmp[i1]], [Ttmp[i1]], scale=-1.0)
                    tt(KhT[:, hh, 0:T_], kb[:, hh, 0:T_], tmp[i1][:, 0:T_], ALU.mult, [Tkb[hh], Ttmp[i1]], [TKhT[hh]])
                for (name, ll, r0, nk, c0, ncols) in plan_cols("w_in_a", l, A_IN):
                    wv, Tw = w_next((name, ll, r0, nk, c0, ncols))
                    bidx = c0 // 512
                    tok = {3: (0, 512, 0), 4: (0, 256, 512)}.get(bidx)
                    if tok is not None:
                        wc0, wn, vc0 = tok
                        for tb in range(NG):
                            b = ps()
                            for k in range(8):
                                mm(pb[b][0:GT, 0:wn], xn[:, k, tb * GT:(tb + 1) * GT], wv[:, k, wc0:wc0 + wn], k == 0, k == 7,
                                   [Tw, Txn[k]], [Tpb[b]])
                            cp(Vt[0:GT, tb, vc0:vc0 + wn], pb[b][0:GT, 0:wn], [Tpb[b]], [TVt[tb]], eng="scalar")
                    for jj in range(ncols // 128):
                        j = c0 // 128 + jj
                        if 12 <= j < 18:
                            continue
                        b = ps()
                        for k in range(8):
                            mm(pb[b][:, 0:T_], wv[:, k, jj * 128:(jj + 1) * 128], xn[:, k, 0:T_], k == 0, k == 7,
                               [Tw, Txn[k]], [Tpb[b]])
                        if j < 6:
                            act(sqb[:, j, 0:T_], pb[b][:, 0:T_], AF.Silu, [Tpb[b]], [Tsq_[j]])
                        elif j < 12:
                            hh = j - 6
                            i0 = nxt()
                            act(tmp[i0][:, 0:T_], pb[b][:, 0:T_], AF.Sigmoid, [Tpb[b]], [Ttmp[i0]], scale=-1.0)
                            tsc(tmp[i0][:, 0:T_], tmp[i0][:, 0:T_], drv[:, l * 6 + hh:l * 6 + hh + 1], ALU.mult,
                                [Ttmp[i0], Tdrv], [Ttmp[i0]], s2=K_MAX, op1=ALU.min)
                            cp(kb[:, hh, 0:T_], tmp[i0][:, 0:T_], [Ttmp[i0]], [Tkb[hh]], eng="gpsimd")
                            act(bfa[:, hh, 0:T_], tmp[i0][:, 0:T_], AF.Ln, [Ttmp[i0]], [Tbf[hh]], scale=-1.0, bias=1.0)
                            hgrn_elem(hh)
                        elif j < 24:
                            act(gate[:, j - 18, 0:T_], pb[b][:, 0:T_], AF.Silu, [Tpb[b]], [Tgate[j - 18]])
                        else:
                            cp(qm[:, j - 24, 0:T_], pb[b][:, 0:T_], [Tpb[b]], [Tqm[j - 24]], eng="scalar")
                chk(30)
                with ExitStack() as a2:
                    mem_attend(l, qm, Tqm, T_, a2)
                chk(31)
                chk(32)
                for tb in range(NG):
                    b = ps()
                    pbv = pb[b][:].bitcast(BF16)
                    for hh in range(6):
                        tr(pbv[0:GT, hh * 128:(hh + 1) * 128], KhT[:, hh, tb * GT:(tb + 1) * GT], id_b, [TKhT[hh], Tcb], [Tpb[b]])
                    cp(Kh[0:GT, tb, :], pbv[0:GT, 0:MAIN], [Tpb[b]], [TKh[tb]])
                for hh in range(6):
                    for gi in range(NG):
                        b = ps()
                        r = 0
                        sl = slice(gi * GT, (gi + 1) * GT)
                        mm(pb[b][0:GT, r * 128:r * 128 + GT], Kt[:, hh, sl], Qt[:, hh, sl], True, True,
                           [TKt[hh], TQt[hh]], [Tpb[b]])
                        for ci in range(G):
                            ps_ = slice(ci * CL, (ci + 1) * CL)
                            tt(ATt[ps_, hh, gi, ci * CL:(ci + 1) * CL], pb[b][ps_, r * 128 + ci * CL:r * 128 + (ci + 1) * CL],
                               tri_f[ps_, ci * CL:(ci + 1) * CL], ALU.mult, [Tpb[b], Tcst], [TAT[hh][gi]])
                P.barrier()
                chk(33)
                srn = [0]
                for gi in range(NG):
                    sl = slice(gi * GT, (gi + 1) * GT)
                    for hh in range(6):
                        mm(pb[hh][:, sl], Vt[0:GT, gi, hh * 128:(hh + 1) * 128], ATt[0:GT, hh, gi, 0:GT], True, False,
                           [TVt[gi], TAT[hh][gi]], [Tpb[hh]])
                    for ci in range(G):
                        c = gi * G + ci
                        cs = slice(c * CL, (c + 1) * CL)
                        for hh in range(6):
                            mm(pb[hh][:, cs], Sbf[l][:, hh, :], Qp[:, hh, cs], False, ci == G - 1,
                               [TSbf[l][hh], TQp[hh]], [Tpb[hh]])
                        prow = slice(ci * CL, (ci + 1) * CL)
                        for hh in range(6):
                            srn[0] = (srn[0] + 1) % 2
                            r = srn[0]
                            reg = pb[6 + r][:, 0:128]
                            mm(reg, Kh[prow, gi, hh * 128:(hh + 1) * 128], Vt[prow, gi, hh * 128:(hh + 1) * 128], True, True,
                               [TKh[gi], TVt[gi]], [Tpb[6 + r]])
                            stt(S32[l][:, hh, :], S32[l][:, hh, :], dec[:, hh, c:c + 1], reg, ALU.mult, ALU.add,
                                [TS32[l][hh], Tdec[hh], Tpb[6 + r]], [TS32[l][hh]])
                            cp(Sbf[l][:, hh, :], S32[l][:, hh, :], [TS32[l][hh]], [TSbf[l][hh]], eng="gpsimd")
                P.barrier()
                chk(34)
                To32 = [T() for _ in range(6)]
                Tsq6 = [T() for _ in range(6)]
                Trs6 = [T() for _ in range(6)]
                for hh in range(6):
                    cp(bfa[:, hh, 0:T_], pb[hh][:, 0:T_], [Tpb[hh]], [To32[hh]], eng="scalar")
                    act(Qt[:, hh, 0:T_], pb[hh][:, 0:T_], AF.Square, [Tpb[hh]], [Tsq6[hh]])
                for hh in range(6):
                    mm(pb[hh][:, 0:T_], ones_b, Qt[:, hh, 0:T_], True, True, [Tsq6[hh], Tcb], [Tpb[hh]])
                for hh in range(6):
                    act(rs6[:, hh, 0:T_], pb[hh][:, 0:T_], AF.Ln, [Tpb[hh]], [Trs6[hh]], scale=1.0 / 128, bias=EPS)
                    act(rs6[:, hh, 0:T_], rs6[:, hh, 0:T_], AF.Exp, [Trs6[hh]], [Trs6[hh]], scale=-0.5)
                    stt(bfa[:, hh, 0:T_], bfa[:, hh, 0:T_], pcol(P_GN + l), rs6[:, hh, 0:T_], ALU.mult, ALU.mult,
                        [To32[hh], Trs6[hh], Tprm], [To32[hh]])
                    tt(mix[:, hh, 0:T_], bfa[:, hh, 0:T_], gate[:, hh, 0:T_], ALU.mult, [To32[hh], Tgate[hh]], [Tmix[hh]])
                P.barrier()
            chk(35)
            out_proj_ffn(l, T_)

        def c_block(lf_ap, Tlf, n, cT_ap, TcT, first_col_prev):
            b = ps()
            mm(pb[b][0:12, 0:n], lf_ap, tri_f[0:n, 0:n], True, True, [Tlf, Tcst], [Tpb[b]])
            tsc(cT_ap, pb[b][0:12, 0:n], first_col_prev, ALU.add, [Tpb[b], TcT, TcTl], [TcT])

        def split_and_store(si, cT, TcT, n, t0, arena, qa=None):
            n0 = sb(arena, "n0", [12, 512], F32)
            r1 = sb(arena, "r1", [12, 512], F32)
            parts = sb(arena, "parts", [12, 3, 512], BF16)
            qparts = sb(arena, "qparts", [12, 3, 512], BF16)
            onesr = sb(arena, "onesr", [12, 3, 512], BF16)
            Tn0, Tr1, Tparts, Tqp, Tones = T(), T(), T(), T(), T()
            mset(onesr[:], 1.0, [Tones], eng="gpsimd")
            tsc(n0[:, 0:n], cT, -1.0, ALU.mult, [TcT], [Tn0])
            cp(parts[:, 0, 0:n], n0[:, 0:n], [Tn0], [Tparts])
            tt(r1[:, 0:n], n0[:, 0:n], parts[:, 0, 0:n], ALU.subtract, [Tn0, Tparts], [Tr1])
            cp(parts[:, 1, 0:n], r1[:, 0:n], [Tr1], [Tparts])
            tt(n0[:, 0:n], r1[:, 0:n], parts[:, 1, 0:n], ALU.subtract, [Tr1, Tparts], [Tn0])
            cp(parts[:, 2, 0:n], n0[:, 0:n], [Tn0], [Tparts])
            dma(KA[si][:, 64:67, t0:t0 + n], parts[:, :, 0:n], [Tparts], [], q="gpsimd", acc=[TKA[si]])
            dma(KA[si][:, 67:70, t0:t0 + n], onesr[:, :, 0:n], [Tones], [], q="gpsimd", acc=[TKA[si]])
            if qa is not None:
                tsc(qparts[:, :, 0:n], parts[:, :, 0:n], -1.0, ALU.mult, [Tparts], [Tqp])
                dma(QA[qa][:, 64:67, 0:n], onesr[:, :, 0:n], [Tones], [], q="gpsimd", acc=[TQA[qa]])
                dma(QA[qa][:, 67:70, 0:n], qparts[:, :, 0:n], [Tqp], [], q="gpsimd", acc=[TQA[qa]])

        TKA = [T(f"KA{i}") for i in range(NS + NSS)]
        TVS = [T(f"VS{i}") for i in range(NS + NSS)]
        TQA = [T("QA0"), T("QA1")]

        def kv_stage(si, sl_, t0, T_, qa, kout, vout, lfout, o0):
            GT = min(128, T_)
            NTB = T_ // GT
            with ExitStack() as arena:
                rms_xn(P_NKV, T_, arena)
                kvt = [sb(arena, "kvt", [128, KV_OUT], F32) for _ in range(NTB)]
                Tkvt = [T() for _ in range(NTB)]
                kT = sb(arena, "kT", [128, 6, 512], BF16)
                TkT = [T() for _ in range(6)]
                vb = sb(arena, "vb", [128, MAIN], BF16)
                Tvb = T()
                sqk = sb(arena, "sqk", [128, MAIN], F32)
                Tsqk = T()
                ss = sb(arena, "ss", [128, 12], F32)
                Tss = T()
                lft = sb(arena, "lft", [128, 12], F32)
                Tlft = T()
                cT = sb(arena, "cT", [12, 512], F32)
                TcT = T()
                for spec in plan_cols("w_kv", 0, KV_OUT):
                    wv, Tw = w_next(spec)
                    c0, ncols = spec[4], spec[5]
                    for tb in range(NTB):
                        b = ps()
                        for k in range(8):
                            mm(pb[b][0:GT, 0:ncols], xn[:, k, tb * GT:(tb + 1) * GT], wv[:, k, 0:ncols], k == 0, k == 7,
                               [Tw, Txn[k]], [Tpb[b]])
                        cp(kvt[tb][0:GT, c0:c0 + ncols], pb[b][0:GT, 0:ncols], [Tpb[b]], [Tkvt[tb]], eng="scalar")
                for tb in range(NTB):
                    ki = tb
                    rows = slice(t0 + tb * GT, t0 + (tb + 1) * GT)
                    orows = slice(o0 + tb * GT, o0 + (tb + 1) * GT)
                    dma(vout[sl_, orows, :], kvt[ki][0:GT, MAIN:2 * MAIN], [Tkvt[ki]], [], q="gpsimd")
                    cp(vb[0:GT, :], kvt[ki][0:GT, MAIN:2 * MAIN], [Tkvt[ki]], [Tvb], eng="gpsimd")
                    dma(VS[si][rows, :], vb[0:GT, :], [Tvb], [], q="gpsimd", acc=[TVS[si]])
                    k3 = kvt[ki][0:GT, 0:MAIN].rearrange("p (h d) -> p h d", d=64)
                    tt(sqk[0:GT, :], kvt[ki][0:GT, 0:MAIN], kvt[ki][0:GT, 0:MAIN], ALU.mult, [Tkvt[ki]], [Tsqk])
                    P.op("vector", lambda e, GT=GT: e.tensor_reduce(out=ss[0:GT, :], in_=sqk[0:GT, :].rearrange("p (h d) -> p h d", d=64),
                                                                     axis=AX.X, op=ALU.add), reads=[Tsqk], writes=[Tss])
                    act(ss[0:GT, :], ss[0:GT, :], AF.Sqrt, [Tss], [Tss], scale=1.0 / 64, bias=EPS)
                    recip(ss[0:GT, :], ss[0:GT, :], [Tss], [Tss])
                    tt(k3, k3, ss[0:GT, :].rearrange("p (h o) -> p h o", o=1).broadcast_to([GT, 12, 64]), ALU.mult,
                       [Tkvt[ki], Tss], [Tkvt[ki]])
                    tt(k3, k3, prm[0:GT, P_FGK:P_FGK + 64].rearrange("p (o d) -> p o d", o=1).broadcast_to([GT, 12, 64]), ALU.mult,
                       [Tkvt[ki], Tprm], [Tkvt[ki]])
                    dma(kout[sl_, orows, :], kvt[ki][0:GT, 0:MAIN], [Tkvt[ki]], [], q="gpsimd")
                    for cc in range(6):
                        b = ps()
                        tr(pb[b][:, 0:GT], kvt[ki][0:GT, cc * 128:(cc + 1) * 128], id_f[0:GT, 0:GT], [Tkvt[ki], Tcst], [Tpb[b]])
                        cp(kT[:, cc, tb * GT:(tb + 1) * GT], pb[b][:, 0:GT], [Tpb[b]], [TkT[cc]], eng=("scalar" if cc % 2 else "vector"))
                    tt(lft[0:GT, :], kvt[ki][0:GT, 2 * MAIN:2 * MAIN + 12], prm[0:GT, P_BF:P_BF + 12], ALU.add,
                       [Tkvt[ki], Tprm], [Tlft])
                    act(lft[0:GT, :], lft[0:GT, :], AF.Exp, [Tlft], [Tlft], scale=-1.0)
                    act(lft[0:GT, :], lft[0:GT, :], AF.Ln, [Tlft], [Tlft], bias=1.0)
                    tsc(lft[0:GT, :], lft[0:GT, :], -1.0, ALU.mult, [Tlft], [Tlft])
                    dma(lfout[sl_, orows, :], lft[0:GT, :], [Tlft], [], q="gpsimd")
                    prev = cTlast[:, 0:1] if tb == 0 else cT[:, tb * GT - 1:tb * GT]
                    c_block(lft[0:GT, :], Tlft, GT, cT[:, tb * GT:(tb + 1) * GT], TcT, prev)
                cp(cTlast[:, 0:1], cT[:, T_ - 1:T_], [TcT], [TcTl])
                for cc in range(6):
                    for hh in range(2):
                        dma(KA[si][2 * cc + hh, 0:64, t0:t0 + T_], kT[hh * 64:(hh + 1) * 64, cc, 0:T_], [TkT[cc]], [], q="gpsimd", acc=[TKA[si]])
                split_and_store(si, cT[:, 0:T_], TcT, T_, t0, arena, qa=qa)
                P.barrier()

        def layer_b(l, si, t0, T_, qa):
            j2 = l - 2
            L = t0 + T_
            NBF = L // 128
            rem = L - NBF * 128
            NB = NBF + (1 if rem else 0)
            with ExitStack() as arena:
                rms_xn(P_NMIX + 8 * l, T_, arena)
                bufs = attn_bufs(arena)
                KAb = [[sb(arena, "KAb", [70, L], BF16) for _ in range(2)], None]
                QAb = [[sb(arena, "QAb", [70, 512], BF16) for _ in range(2)] for _ in range(2)]
                Vb = [sb(arena, "Vb", [128, NB, 256], BF16), None]
                TKAb = [[T(), T()], [T(), T()]]
                TQAb = [[T(), T()], [T(), T()]]
                TVb = [T(), T()]
                Qn = sb(arena, "Qn", [128, 6, 512], BF16)
                TQn = [T() for _ in range(6)]
                qm = sb(arena, "qm", [128, 2, 512], F32)
                Tqm = [T(), T()]
                mset(Vb[0][:, :, 64:192], 1.0, [TVb[0]], eng="gpsimd")

                def issue_kv(cc):
                    st_ = cc % 2
                    for hh in range(2):
                        dma(KAb[st_][hh][:, :], KA[si][2 * cc + hh, :, 0:L], [TKA[si]], [TKAb[st_][hh]], q="sync", key=f"kab{st_}{hh}")
                    for u in range(2):
                        vc = cc * 128 + u * 64
                        dc = u * 192
                        if NBF:
                            dma(Vb[st_][:, 0:NBF, dc:dc + 64], VS[si][0:NBF * 128, vc:vc + 64].rearrange("(b p) d -> p b d", p=128),
                                [TVS[si]], [], q="sync", key=f"vb{st_}{u}", acc=[TVb[st_]])
                        if rem:
                            dma(Vb[st_][0:rem, NBF, dc:dc + 64], VS[si][NBF * 128:L, vc:vc + 64],
                                [TVS[si]], [], q="sync", key=f"vb{st_}{u}", acc=[TVb[st_]])

                def issue_q(cc):
                    st_ = cc % 2
                    for hh in range(2):
                        dma(QAb[st_][hh][:, 0:T_], QA[qa][2 * cc + hh, :, 0:T_], [TQA[qa]], [TQAb[st_][hh]], q="sync", key=f"qab{st_}{hh}")

                issue_kv(0)
                with ExitStack() as a1:
                    qf6 = sb(a1, "qf6", [128, 6, 512], F32)
                    Tqf = [T() for _ in range(6)]
                    sq6 = sb(a1, "sq6", [128, 6, 512], BF16)
                    Tsq6 = [T() for _ in range(6)]
                    rs6 = sb(a1, "rs6b", [128, 6, 512], F32)
                    Trs6 = [T() for _ in range(6)]
                    for (name, ll, r0, nk, c0, ncols) in plan_cols("w_in_b", j2, D):
                        wv, Tw = w_next((name, ll, r0, nk, c0, ncols))
                        for jj in range(ncols // 128):
                            j = c0 // 128 + jj
                            b = ps()
                            for k in range(8):
                                mm(pb[b][:, 0:T_], wv[:, k, jj * 128:(jj + 1) * 128], xn[:, k, 0:T_], k == 0, k == 7,
                                   [Tw, Txn[k]], [Tpb[b]])
                            if j < 6:
                                cp(qf6[:, j, 0:T_], pb[b][:, 0:T_], [Tpb[b]], [Tqf[j]], eng="scalar")
                                act(sq6[:, j, 0:T_], pb[b][:, 0:T_], AF.Square, [Tpb[b]], [Tsq6[j]])
                            else:
                                cp(qm[:, j - 6, 0:T_], pb[b][:, 0:T_], [Tpb[b]], [Tqm[j - 6]], eng="scalar")
                    nb_ = []
                    for j in range(6):
                        b = ps()
                        nb_.append(b)
                        mm(pb[b][:, 0:T_], bd_b, sq6[:, j, 0:T_], True, True, [Tsq6[j], Tcb], [Tpb[b]])
                    for j in range(6):
                        b = nb_[j]
                        act(rs6[:, j, 0:T_], pb[b][:, 0:T_], AF.Ln, [Tpb[b]], [Trs6[j]], scale=1.0 / 64, bias=EPS)
                        act(rs6[:, j, 0:T_], rs6[:, j, 0:T_], AF.Exp, [Trs6[j]], [Trs6[j]], scale=-0.5)
                        stt(Qn[:, j, 0:T_], qf6[:, j, 0:T_], drv[:, 12 + j2:13 + j2], rs6[:, j, 0:T_], ALU.mult, ALU.mult,
                            [Tqf[j], Trs6[j], Tdrv], [TQn[j]])
                        for hh in range(2):
                            dma(QA[qa][2 * j + hh, 0:64, 0:T_], Qn[hh * 64:(hh + 1) * 64, j, 0:T_], [TQn[j]], [], q="gpsimd", acc=[TQA[qa]])
                    P.barrier()
                issue_q(0)
                issue_q(1)
                with ExitStack() as a2:
                    mem_attend(l, qm, Tqm, T_, a2)
                    P.barrier()
                KAb[1] = [sb(arena, "KAb", [70, L], BF16) for _ in range(2)]
                Vb[1] = sb(arena, "Vb", [128, NB, 256], BF16)
                mset(Vb[1][:, :, 64:192], 1.0, [TVb[1]], eng="gpsimd")
                issue_kv(1)
                blocks = []
                for j in range(NB):
                    nk = 128 if j < NBF else rem
                    if j * 128 < t0:
                        blocks.append((j, nk, 0, False))
                    else:
                        blocks.append((j, nk, j * 128 - t0, True))
                for cc in range(6):
                    st_ = cc % 2

                    def lhsK(hh, j, nk, st_=st_):
                        return KAb[st_][hh][:, j * 128:j * 128 + nk]

                    def rhsQ(hh, q0, st_=st_):
                        return QAb[st_][hh][:, q0:T_]

                    def lhsV(hh, j, nk, st_=st_):
                        return Vb[st_][0:nk, j, 0:128] if hh == 0 else Vb[st_][0:nk, j, 128:256]
                    attn_pair(lhsK, rhsQ, lhsV, blocks, T_, mix[:, cc, 0:T_], Tmix[cc], TKAb[st_], TQAb[st_], [TVb[st_]], bufs)
                    if cc + 2 < 6:
                        issue_kv(cc + 2)
                        issue_q(cc + 2)
                P.barrier()
            out_proj_ffn(l, T_)

        def run_tile(si, sl_, t0, T_, xin_d, yout_d, kout, vout, lfout, qa, sample):
            GT = min(128, T_)
            NTB = T_ // GT
            with ExitStack() as arena:
                xin = sb(arena, "xin", [128, 4, D], F32)
                Txin = T()
                dma(xin[0:GT, 0:NTB, :], xin_d[sl_, t0 - (PAST if sample else 0):t0 - (PAST if sample else 0) + T_, :]
                    .rearrange("(b p) d -> p b d", p=GT), [], [Txin], q="sync", key="xin")
                for c in range(8):
                    b = ps()
                    for tb in range(NTB):
                        tr(pb[b][:, tb * GT:(tb + 1) * GT], xin[0:GT, tb, c * 128:(c + 1) * 128], id_f[0:GT, 0:GT], [Txin, Tcst], [Tpb[b]])
                    cp(h[:, c, 0:T_], pb[b][:, 0:T_], [Tpb[b]], [Th[c]], eng=("scalar" if c % 2 else "vector"))
                P.barrier()
            chk(2)
            layer_a(0, T_, sample)
            chk(3)
            layer_a(1, T_, sample)
            chk(4)
            kv_stage(si, sl_, t0, T_, qa, kout, vout, lfout, t0 - (PAST if sample else 0))
            chk(5)
            layer_b(2, si, t0, T_, qa)
            chk(6)
            layer_b(3, si, t0, T_, qa)
            chk(7)
            with ExitStack() as arena:
                yo = sb(arena, "yo", [128, 4, D], F32)
                Tyo = T()
                for tb in range(NTB):
                    for c4 in range(2):
                        b = ps()
                        for c in range(4):
                            cc = c4 * 4 + c
                            tr(pb[b][0:GT, c * 128:(c + 1) * 128], h[:, cc, tb * GT:(tb + 1) * GT], id_f, [Th[cc], Tcst], [Tpb[b]])
                        cp(yo[0:GT, tb, c4 * 512:(c4 + 1) * 512], pb[b][0:GT, :], [Tpb[b]], [Tyo], eng=("scalar" if c4 else "vector"))
                tq = t0 - (PAST if sample else 0)
                dma(yout_d[sl_, tq:tq + T_, :].rearrange("(b p) d -> p b d", p=GT), yo[0:GT, 0:NTB, :], [Tyo], [], q="gpsimd")
                P.barrier()

        def mem_prologue(sl_, sample):
            with ExitStack() as arena:
                mset(MVb[:, :, :, :, 64:192], 1.0, [TMV], eng="gpsimd")
                kvm = [sb(arena, "kvm", [128, 512], F32) for _ in range(2)]
                Tkvm = [T(), T()]
                sqk = sb(arena, "msq", [128, 256], F32)
                ss = sb(arena, "mss", [128, 4], F32)
                Tsqk, Tss = T(), T()
                if not sample:
                    mt = sb(arena, "mt", [128, 2, D], F32)
                    Tmt = T()
                    dma(mt[:, :, :], memp[sl_].rearrange("(b p) d -> p b d", p=128), [], [Tmt], q="sync", key="xin")
                    for c in range(8):
                        b = ps()
                        for tb in range(2):
                            tr(pb[b][:, tb * 128:(tb + 1) * 128], mt[:, tb, c * 128:(c + 1) * 128], id_f, [Tmt, Tcst], [Tpb[b]])
                        cp(h[:, c, 0:256], pb[b][:, 0:256], [Tpb[b]], [Th[c]], eng=("scalar" if c % 2 else "vector"))
                n = 0
                for l in range(4):
                    if not sample:
                        with ExitStack() as a2:
                            rms_xn(P_NMEM + 8 * l, 256, a2)
                            P.barrier()
                        spec = ("w_mem_kv", l, 0, 8, 0, 512)
                        wv, Tw = w_next(spec)
                    for mb in range(2):
                        ki = n % 2
                        n += 1
                        if not sample:
                            b = ps()
                            for k in range(8):
                                mm(pb[b][:, 0:512], xn[:, k, mb * 128:(mb + 1) * 128], wv[:, k, :], k == 0, k == 7, [Tw, Txn[k]], [Tpb[b]])
                            cp(kvm[ki][:, :], pb[b][:, :], [Tpb[b]], [Tkvm[ki]], eng="scalar")
                            k3 = kvm[ki][:, 0:256].rearrange("p (h d) -> p h d", d=64)
                            tt(sqk[:, :], kvm[ki][:, 0:256], kvm[ki][:, 0:256], ALU.mult, [Tkvm[ki]], [Tsqk])
                            P.op("vector", lambda e: e.tensor_reduce(out=ss[:, :], in_=sqk[:, :].rearrange("p (h d) -> p h d", d=64),
                                                                     axis=AX.X, op=ALU.add), reads=[Tsqk], writes=[Tss])
                            act(ss[:, :], ss[:, :], AF.Sqrt, [Tss], [Tss], scale=1.0 / 64, bias=EPS)
                            recip(ss[:, :], ss[:, :], [Tss], [Tss])
                            tt(k3, k3, ss[:, :].rearrange("p (h o) -> p h o", o=1).broadcast_to([128, 4, 64]), ALU.mult, [Tkvm[ki], Tss], [Tkvm[ki]])
                            tt(k3, k3, prm[:, P_MGK + 64 * l:P_MGK + 64 * (l + 1)].rearrange("p (o d) -> p o d", o=1).broadcast_to([128, 4, 64]),
                               ALU.mult, [Tkvm[ki], Tprm], [Tkvm[ki]])
                            dma(pmk[l, sl_, mb * 128:(mb + 1) * 128, :], kvm[ki][:, 0:256], [Tkvm[ki]], [], q="gpsimd")
                            dma(pmv[l, sl_, mb * 128:(mb + 1) * 128, :], kvm[ki][:, 256:512], [Tkvm[ki]], [], q="gpsimd")
                        else:
                            dma(kvm[ki][:, 0:256], cmk[l, sl_, mb * 128:(mb + 1) * 128, :], [], [Tkvm[ki]], q="sync", key=f"kvm{ki}")
                            dma(kvm[ki][:, 256:512], cmv[l, sl_, mb * 128:(mb + 1) * 128, :], [], [Tkvm[ki]], q="sync", key=f"kvm{ki}")
                        for cc in range(2):
                            b = ps()
                            tr(pb[b][:, 0:128], kvm[ki][:, cc * 128:(cc + 1) * 128], id_f, [Tkvm[ki], Tcst], [Tpb[b]])
                            cp(MKT[:, l, cc, mb * 128:(mb + 1) * 128], pb[b][:, 0:128], [Tpb[b]], [TMK], eng=("scalar" if cc else "vector"))
                        cp(MVb[:, l, mb, :, :].rearrange("p c (s d) -> p c s d", d=64)[:, :, 0::3, :], kvm[ki][:, 256:512].rearrange("p (c u d) -> p c u d", c=2, u=2), [Tkvm[ki]], [TMV])
                P.barrier()

        def cache_import(si, sl_):
            with ExitStack() as arena:
                ckt = [sb(arena, "ckt", [128, MAIN], F32) for _ in range(2)]
                Tckt = [T(), T()]
                kT = sb(arena, "kTc", [128, 6, 512], BF16)
                TkT = [T() for _ in range(6)]
                lft = sb(arena, "lftc", [128, 4, 12], F32)
                Tlft = T()
                cT = sb(arena, "cTc", [12, 512], F32)
                TcT = T()
                for r in range(0, PAST, 1024):
                    r1 = min(PAST, r + 1024)
                    dma(VS[si][r:r1, :], cv[sl_, r:r1, :], [], [], q="gpsimd", acc=[TVS[si]])
                mset(cTlast[:, 0:1], 0.0, [TcTl], eng="vector")
                for g0 in range(0, PAST, 512):
                    n = min(512, PAST - g0)
                    nb = n // 128
                    dma(lft[:, 0:nb, :], clf[sl_, g0:g0 + n, :].rearrange("(b p) h -> p b h", p=128), [], [Tlft], q="sync", key="lftc")
                    for tb in range(nb):
                        ki = tb % 2
                        dma(ckt[ki][:, :], ck[sl_, g0 + tb * 128:g0 + (tb + 1) * 128, :], [], [Tckt[ki]], q="sync", key=f"ckt{ki}")
                        for cc in range(6):
                            b = ps()
                            tr(pb[b][:, 0:128], ckt[ki][:, cc * 128:(cc + 1) * 128], id_f, [Tckt[ki], Tcst], [Tpb[b]])
                            cp(kT[:, cc, tb * 128:(tb + 1) * 128], pb[b][:, 0:128], [Tpb[b]], [TkT[cc]], eng=("scalar" if cc % 2 else "vector"))
                        prev = cTlast[:, 0:1] if tb == 0 else cT[:, tb * 128 - 1:tb * 128]
                        c_block(lft[:, tb, :], Tlft, 128, cT[:, tb * 128:(tb + 1) * 128], TcT, prev)
                    cp(cTlast[:, 0:1], cT[:, n - 1:n], [TcT], [TcTl])
                    for cc in range(6):
                        for hh in range(2):
                            dma(KA[si][2 * cc + hh, 0:64, g0:g0 + n], kT[hh * 64:(hh + 1) * 64, cc, 0:n], [TkT[cc]], [], q="gpsimd", acc=[TKA[si]])
                    with ExitStack() as a2:
                        split_and_store(si, cT[:, 0:n], TcT, n, g0, a2, qa=None)
                        P.barrier()
                P.barrier()

        try:
            dma(cst[:, :], consts_d, [], [Tcst], q="sync", key="cst")
            dma(prm[:, :], prm_d, [], [Tprm], q="sync", key="prm")
            for l in range(4):
                cast_weight("w_mem_kv", l)
            for l in range(4):
                if l < 2:
                    cast_weight("w_in_a", l)
                else:
                    cast_weight("w_in_b", l - 2)
                cast_weight("w_out", l)
                cast_weight("w_ffn_up", l)
                cast_weight("w_ffn_down", l)
                if l == 1:
                    cast_weight("w_kv", 0)
            cp(cb[:, 0, :], cst[:, C_ONES:C_ONES + 128], [Tcst], [Tcb])
            cp(cb[:, 1, :], cst[:, C_BD:C_BD + 128], [Tcst], [Tcb])
            cp(cb[:, 2, :], cst[:, C_ID:C_ID + 128], [Tcst], [Tcb])
            cp(cb[:, 3, :], cst[:, C_TRI:C_TRI + 128], [Tcst], [Tcb])
            mset(drv[:, :], 1.0, [Tdrv], eng="vector")
            tt(drv[:, 6:12], prm[:, P_LB:P_LB + 6], prm[:, P_LB + 6:P_LB + 12], ALU.subtract, [Tprm], [Tdrv])
            act(drv[:, 6:12], drv[:, 6:12], AF.Sigmoid, [Tdrv], [Tdrv])
            tsc(drv[:, 12:14], prm[:, P_FGQ:P_FGQ + 2], 0.125, ALU.mult, [Tprm], [Tdrv])
            tsc(drv[:, 16:20], prm[:, P_MGQ:P_MGQ + 4], 0.125, ALU.mult, [Tprm], [Tdrv])

            NT = SEQ // 512
            for s in range(NS):
                wplan.extend([("w_mem_kv", l, 0, 8, 0, 512) for l in range(4)])
                for _ in range(NT):
                    wplan.extend(plan_tile())
            for s in range(NSS):
                wplan.extend(plan_tile())

            chk(0)
            for s in range(NS):
                mem_prologue(s, False)
                chk(1)
                for l in range(2):
                    mset(S32[l][:], 0.0, TS32[l], eng="vector")
                    mset(Sbf[l][:], 0.0, TSbf[l], eng="gpsimd")
                mset(cTlast[:, 0:1], 0.0, [TcTl], eng="vector")
                for ti in range(NT):
                    run_tile(s, s, ti * 512, 512, xp, yp, pk, pv, plf, ti % 2, False)
                for l in range(2):
                    dma(pst[l][s].rearrange("h k v -> k h v"), S32[l][:, :, :], TS32[l], [], q="gpsimd")
                P.barrier()
            for s in range(NSS):
                si = NS + s
                mem_prologue(s, True)
                for l in range(2):
                    dma(S32[l][:, :, :], st_in[l][s].rearrange("h k v -> k h v"), [], TS32[l], q="sync", key="stin")
                    for hh in range(6):
                        cp(Sbf[l][:, hh, :], S32[l][:, hh, :], [TS32[l][hh]], [TSbf[l][hh]], eng="gpsimd")
                cache_import(si, s)
                run_tile(si, s, PAST, TS, xs, ys, sk, sv, slf, 0, True)
                for l in range(2):
                    dma(sst[l][s].rearrange("h k v -> k h v"), S32[l][:, :, :], TS32[l], [], q="gpsimd")
                P.barrier()
        except _Stop:
            wstate["used"] = len(wplan)
        assert P.dead or wstate["used"] == len(wplan), (wstate, len(wplan))
        P.emit()
    return nc, P


def make_consts():
    c = np.zeros((128, CW), np.float32)
    c[:, C_ONES:C_ONES + 128] = 1.0
    c[:64, C_BD:C_BD + 64] = 1.0
    c[64:, C_BD + 64:C_BD + 128] = 1.0
    c[:, C_ID:C_ID + 128] = np.eye(128, dtype=np.float32)
    c[:, C_TRI:C_TRI + 128] = np.triu(np.ones((128, 128), np.float32))
    sw = np.zeros((128, 128), np.float32)
    for k in range(128):
        sw[k, (k + 64) % 128] = 1.0
    c[:, C_SWAP:C_SWAP + 128] = sw
    r = np.ones(512, np.float32)
    r[0::64] = 0.0
    c[:, C_RESET:C_RESET + 512] = r[None, :]
    return c


def make_params(norm_mix, norm_ffn, norm_mem, norm_kv, lb_logits, hg_gnorm, fox_gq, mem_gq, fox_gk, mem_gk, b_f):
    p = np.zeros((128, PW), np.float32)

    def fm(v):
        v = np.asarray(v, np.float32).reshape(-1, 8, 128)
        return v.transpose(2, 0, 1).reshape(128, -1)
    p[:, P_NMIX:P_NMIX + 32] = fm(norm_mix)
    p[:, P_NFFN:P_NFFN + 32] = fm(norm_ffn)
    p[:, P_NMEM:P_NMEM + 32] = fm(norm_mem)
    p[:, P_NKV:P_NKV + 8] = fm(np.asarray(norm_kv)[None])
    p[:, P_LB:P_LB + 12] = np.asarray(lb_logits, np.float32).reshape(2, 6, 128).transpose(2, 0, 1).reshape(128, 12)
    p[:, P_GN:P_GN + 2] = np.asarray(hg_gnorm, np.float32).T
    p[:, P_FGQ:P_FGQ + 2] = np.tile(np.asarray(fox_gq, np.float32).T, (2, 1))
    p[:, P_MGQ:P_MGQ + 4] = np.tile(np.asarray(mem_gq, np.float32).T, (2, 1))
    p[:, P_FGK:P_FGK + 64] = np.asarray(fox_gk, np.float32)[None, :]
    p[:, P_MGK:P_MGK + 256] = np.asarray(mem_gk, np.float32).reshape(1, 256)
    p[:, P_BF:P_BF + 12] = np.asarray(b_f, np.float32)[None, :]
    return p


_CACHE = {}


def run(inputs, n_cores, NS, SEQ, NSS, PAST):
    key = (NS, SEQ, NSS, PAST)
    if key not in _CACHE:
        _CACHE[key] = build(NS, SEQ, NSS, PAST)[0]
    nc = _CACHE[key]
    f = lambda a: np.ascontiguousarray(np.asarray(a, np.float32))
    consts = make_consts()
    prm = make_params(inputs["norm_mix"], inputs["norm_ffn"], inputs["norm_mem"], inputs["norm_kv"], inputs["lb_logits"],
                      inputs["hg_gnorm"], inputs["fox_gq"], inputs["mem_gq"], inputs["fox_gk"], inputs["mem_gk"], inputs["b_f"])
    shared = {"consts": consts, "prm": prm}
    for k in ("w_in_a", "w_in_b", "w_mem_kv", "w_out", "w_ffn_up", "w_ffn_down"):
        shared[k] = f(inputs[k])
    shared["w_kv"] = f(inputs["w_kv"])[None]
    in_maps = []
    for c in range(n_cores):
        ps_ = slice(c * NS, (c + 1) * NS)
        ss_ = slice(c * NSS, (c + 1) * NSS)
        m = dict(shared)
        m["xp"] = f(inputs["x_prompt"][ps_])
        m["xs"] = f(inputs["x_sample"][ss_])
        m["memp"] = f(inputs["mem_prompt"][ps_])
        m["st0"] = f(inputs["state_hgrn_0"][ss_])
        m["st1"] = f(inputs["state_hgrn_1"][ss_])
        m["ck"] = f(inputs["cache_fox_k"][ss_]).reshape(NSS, PAST, MAIN)
        m["cv"] = f(inputs["cache_fox_v"][ss_]).reshape(NSS, PAST, MAIN)
        m["clf"] = f(inputs["cache_fox_logf"][ss_])
        m["cmk"] = f(inputs["cache_mem_k"][:, ss_]).reshape(4, NSS, N_MEM, 256)
        m["cmv"] = f(inputs["cache_mem_v"][:, ss_]).reshape(4, NSS, N_MEM, 256)
        in_maps.append(m)
    res = run_bass_kernel_spmd(nc, in_maps, core_ids=list(range(n_cores)))
    R = res.results
    cat = lambda k, ax=0: np.concatenate([np.asarray(r[k]) for r in R], axis=ax)
    B = n_cores * NS
    BS = n_cores * NSS
    outs = (
        cat("yp"), cat("ys"), cat("pst0"), cat("pst1"),
        cat("pk").reshape(B, SEQ, 12, 64), cat("pv").reshape(B, SEQ, 12, 64), cat("plf"),
        cat("pmk", 1).reshape(4, B, N_MEM, 4, 64), cat("pmv", 1).reshape(4, B, N_MEM, 4, 64),
        cat("sst0"), cat("sst1"),
        cat("sk").reshape(BS, TS, 12, 64), cat("sv").reshape(BS, TS, 12, 64), cat("slf"),
    )
    return tuple(np.ascontiguousarray(o, dtype=np.float32) for o in outs)


def kernel(**inputs):
    return run(inputs, 8, 2, 4096, 2, 4096)
```

```python
import numpy as np
from contextlib import ExitStack
import concourse.bass as bass
import concourse.mybir as mybir
from concourse.bass_utils import run_bass_kernel_spmd

F32 = mybir.dt.float32
BF16 = mybir.dt.bfloat16
AF = mybir.ActivationFunctionType
ALU = mybir.AluOpType
AX = mybir.AxisListType

D = 1024
DFF = 2816
MAIN = 768
A_IN = 3328
KV_OUT = 1548
EPS = 1e-6
K_MAX = 0.999999
N_MEM = 256
TS = 16
NW = 4
WELEM = 4096

C_ONES, C_BD, C_ID, C_TRI, C_SWAP, C_RESET, CW = 0, 128, 256, 384, 512, 640, 1152
P_NMIX, P_NFFN, P_NMEM, P_NKV, P_LB, P_GN, P_FGQ, P_MGQ, P_FGK, P_MGK, P_BF, PW = 0, 32, 64, 96, 104, 116, 118, 120, 124, 188, 444, 456


class T:
    __slots__ = ("name", "w", "r", "wa", "excl")

    def __init__(self, name="", excl=False):
        self.name = name
        self.w = None
        self.r = []
        self.wa = []
        self.excl = excl


class Prog:
    ENG = ("tensor", "vector", "scalar", "gpsimd", "sync")

    def __init__(self, nc):
        self.nc = nc
        self.ops = {e: [] for e in self.ENG}
        self.cnt = {}
        self.waited = {e: {} for e in self.ENG}
        self.semkeys = []
        for e in self.ENG:
            self._newsem("E_" + e)
        self.n_ops = 0
        self.dead = False

    def _newsem(self, key):
        self.semkeys.append(key)
        self.cnt[key] = 0

    def dma_sem(self, key):
        k = "D_" + key
        if k not in self.cnt:
            self._newsem(k)
        return k

    def op(self, eng, fn, reads=(), writes=(), dma=None, writes_acc=()):
        waits = {}

        def need(dep, same_ok):
            if dep is None:
                return
            key, val, deng = dep
            if deng == eng and not same_ok and not key.startswith("D_"):
                return
            if waits.get(key, 0) < val:
                waits[key] = val

        if self.dead:
            return ("E_" + eng, self.cnt["E_" + eng], eng)
        for t in reads:
            need(t.w, True)
            for r in t.wa:
                need(r, True)
            if t.excl:
                for r in t.r:
                    need(r, False)
        for t in writes:
            need(t.w, False)
            for r in t.wa:
                need(r, False)
            for r in t.r:
                need(r, False)
        for t in writes_acc:
            for r in t.r:
                need(r, False)
        wl = []
        wd = self.waited[eng]
        for key, val in waits.items():
            if wd.get(key, 0) < val:
                wd[key] = val
                wl.append((key, val))
        if dma is None:
            key = "E_" + eng
            self.cnt[key] += 1
            inc = 1
        else:
            key = dma
            self.cnt[key] += 16
            inc = 16
        me = (key, self.cnt[key], eng)
        self.ops[eng].append((wl, fn, key, inc))
        for t in writes:
            t.w = me
            t.r = []
            t.wa = []
        for t in writes_acc:
            t.wa.append(me)
        for t in reads:
            t.r.append(me)
        self.n_ops += 1
        return me

    def barrier(self, force=False):
        if self.dead and not force:
            return
        snap = dict(self.cnt)
        for e in self.ENG:
            wl = []
            for key, val in snap.items():
                if val == 0 or key == "E_" + e:
                    continue
                if self.waited[e].get(key, 0) < val:
                    self.waited[e][key] = val
                    wl.append((key, val))
            if wl:
                self.ops[e].append((wl, None, None, 0))

    def emit(self):
        nc = self.nc
        self.barrier(force=True)
        with ExitStack() as st:
            sems = {}
            for k in self.semkeys:
                sems[k] = st.enter_context(nc.semaphore(k))
            block = st.enter_context(nc.Block())
            for e in self.ENG:
                ops = self.ops[e]
                if not ops:
                    continue

                def body(eng, ops=ops):
                    for wl, fn, key, inc in ops:
                        for wk, wv in wl:
                            eng.wait_ge(sems[wk], wv)
                        if fn is not None:
                            fn(eng).then_inc(sems[key], inc)
                getattr(block, e)(body)


class _Stop(Exception):
    pass


DBG_STOP = [None]


_PROG = [None]


def chk(n):
    if DBG_STOP[0] == n:
        _PROG[0].dead = True


def build(NS, SEQ, NSS, PAST):
    nc = bass.Bass("TRN2", target_bir_lowering=False)

    def dram(name, shape, dtype, kind):
        return nc.dram_tensor(name, list(shape), dtype, kind=kind).ap()

    I, O, S = "ExternalInput", "ExternalOutput", "Internal"
    LS = PAST + TS
    xp = dram("xp", [NS, SEQ, D], F32, I)
    xs = dram("xs", [NSS, TS, D], F32, I)
    memp = dram("memp", [NS, N_MEM, D], F32, I)
    st_in = [dram("st0", [NSS, 6, 128, 128], F32, I), dram("st1", [NSS, 6, 128, 128], F32, I)]
    ck = dram("ck", [NSS, PAST, MAIN], F32, I)
    cv = dram("cv", [NSS, PAST, MAIN], F32, I)
    clf = dram("clf", [NSS, PAST, 12], F32, I)
    cmk = dram("cmk", [4, NSS, N_MEM, 256], F32, I)
    cmv = dram("cmv", [4, NSS, N_MEM, 256], F32, I)
    consts_d = dram("consts", [128, CW], F32, I)
    prm_d = dram("prm", [128, PW], F32, I)
    wshapes = {"w_in_a": [2, D, A_IN], "w_in_b": [2, D, D], "w_kv": [1, D, KV_OUT], "w_mem_kv": [4, D, 512],
               "w_out": [4, D, D], "w_ffn_up": [4, D, 2 * DFF], "w_ffn_down": [4, DFF, D]}
    wf = {k: dram(k, v, F32, I) for k, v in wshapes.items()}
    wb = {k: dram(k + "_b", v, BF16, S) for k, v in wshapes.items()}
    yp = dram("yp", [NS, SEQ, D], F32, O)
    ys = dram("ys", [NSS, TS, D], F32, O)
    pst = [dram("pst0", [NS, 6, 128, 128], F32, O), dram("pst1", [NS, 6, 128, 128], F32, O)]
    pk = dram("pk", [NS, SEQ, MAIN], F32, O)
    pv = dram("pv", [NS, SEQ, MAIN], F32, O)
    plf = dram("plf", [NS, SEQ, 12], F32, O)
    pmk = dram("pmk", [4, NS, N_MEM, 256], F32, O)
    pmv = dram("pmv", [4, NS, N_MEM, 256], F32, O)
    sst = [dram("sst0", [NSS, 6, 128, 128], F32, O), dram("sst1", [NSS, 6, 128, 128], F32, O)]
    sk = dram("sk", [NSS, TS, MAIN], F32, O)
    sv = dram("sv", [NSS, TS, MAIN], F32, O)
    slf = dram("slf", [NSS, TS, 12], F32, O)
    KA = [dram(f"KA{i}", [12, 70, SEQ if i < NS else LS], BF16, S) for i in range(NS + NSS)]
    VS = [dram(f"VS{i}", [SEQ if i < NS else LS, MAIN], BF16, S) for i in range(NS + NSS)]
    QA = [dram(f"QA{i}", [12, 70, 512], BF16, S) for i in range(2)]

    P = Prog(nc)
    _PROG[0] = P
    uid = [0]

    with ExitStack() as top:
        def sb(st, name, shape, dtype):
            uid[0] += 1
            return st.enter_context(nc.sbuf_tensor(f"{name}_{uid[0]}", list(shape), dtype))

        cst = sb(top, "cst", [128, CW], F32)
        prm = sb(top, "prm", [128, PW], F32)
        cb = sb(top, "cb", [128, 5, 128], BF16)
        drv = sb(top, "drv", [128, 32], F32)
        h = sb(top, "h", [128, 8, 512], F32)
        xn = sb(top, "xn", [128, 8, 512], BF16)
        mix = sb(top, "mix", [128, 8, 512], BF16)
        wbuf = [sb(top, f"wbuf{i}", [128, WELEM], BF16) for i in range(NW)]
        S32 = [sb(top, f"S32_{l}", [128, 6, 128], F32) for l in range(2)]
        Sbf = [sb(top, f"Sbf_{l}", [128, 6, 128], BF16) for l in range(2)]
        MKT = sb(top, "MKT", [128, 4, 2, 256], BF16)
        MVb = sb(top, "MVb", [128, 4, 2, 2, 256], BF16)
        cTlast = sb(top, "cTlast", [12, 2], F32)
        pb = [top.enter_context(nc.psum_tensor(f"pb{i}", [128, 512], F32)) for i in range(8)]
        Tpb = [T(f"pb{i}", excl=True) for i in range(8)]
        Tst = [T(f"st{i}") for i in range(8)]
        Tcst, Tprm, Tcb, Tdrv = T("cst"), T("prm"), T("cb"), T("drv")
        Th = [T(f"h{c}") for c in range(8)]
        Txn = [T(f"xn{c}") for c in range(8)]
        Tmix = [T(f"mix{c}") for c in range(8)]
        Twbuf = [T(f"wbuf{i}") for i in range(NW)]
        TS32 = [[T() for _ in range(6)] for _ in range(2)]
        TSbf = [[T() for _ in range(6)] for _ in range(2)]
        TMK, TMV, TcTl = T("MKT"), T("MVb"), T("cTlast")
        psn = [0]

        def ps():
            psn[0] = (psn[0] + 1) % 8
            return psn[0]

        ones_b, bd_b, id_b, tri_b = cb[:, 0, :], cb[:, 1, :], cb[:, 2, :], cb[:, 3, :]
        ones_f = cst[:, C_ONES:C_ONES + 128]
        id_f = cst[:, C_ID:C_ID + 128]
        tri_f = cst[:, C_TRI:C_TRI + 128]
        swap_f = cst[:, C_SWAP:C_SWAP + 128]
        reset_f = cst[:, C_RESET:C_RESET + 512]

        def pcol(c):
            return prm[:, c:c + 1]

        def mm(out, lhsT, rhs, start, stop, reads, writes):
            P.op("tensor", lambda e: e.matmul(out, lhsT, rhs, start=start, stop=stop), reads=reads, writes=writes)

        def tr(out, in_, ident, reads, writes):
            P.op("tensor", lambda e: e.transpose(out, in_, ident), reads=reads, writes=writes)

        def act(out, in_, func, reads, writes, scale=1.0, bias=None):
            if bias is None:
                P.op("scalar", lambda e: e.activation(out=out, in_=in_, func=func, scale=scale), reads=reads, writes=writes)
            else:
                P.op("scalar", lambda e: e.activation(out=out, in_=in_, func=func, scale=scale, bias=bias), reads=reads, writes=writes)

        def tt(out, in0, in1, op, reads, writes, eng="vector"):
            P.op(eng, lambda e: e.tensor_tensor(out=out, in0=in0, in1=in1, op=op), reads=reads, writes=writes)

        def tsc(out, in0, s1, op0, reads, writes, s2=None, op1=None, eng="vector"):
            if op1 is None:
                P.op(eng, lambda e: e.tensor_scalar(out=out, in0=in0, scalar1=s1, scalar2=None, op0=op0), reads=reads, writes=writes)
            else:
                P.op(eng, lambda e: e.tensor_scalar(out=out, in0=in0, scalar1=s1, scalar2=s2, op0=op0, op1=op1), reads=reads, writes=writes)

        def stt(out, in0, scalar, in1, op0, op1, reads, writes):
            P.op("vector", lambda e: e.scalar_tensor_tensor(out=out, in0=in0, scalar=scalar, in1=in1, op0=op0, op1=op1),
                 reads=reads, writes=writes)

        def cp(out, in_, reads, writes, eng="vector"):
            if eng == "scalar":
                P.op("scalar", lambda e: e.copy(out=out, in_=in_), reads=reads, writes=writes)
            else:
                P.op(eng, lambda e: e.tensor_copy(out=out, in_=in_), reads=reads, writes=writes)

        def recip(out, in_, reads, writes):
            P.op("vector", lambda e: e.reciprocal(out=out, in_=in_), reads=reads, writes=writes)

        def mset(ap, val, writes, eng="gpsimd"):
            P.op(eng, lambda e: e.memset(ap, val), writes=writes)

        dman = [0]

        def dma(out, in_, reads, writes, q="sync", key=None, acc=()):
            if key is None:
                dman[0] += 1
                key = f"g{dman[0] % 48}_{q}"
            return P.op(q, lambda e: e.dma_start(out=out, in_=in_), reads=reads, writes=writes, dma=P.dma_sem(key),
                        writes_acc=acc)

        Twb = {}
        ncast = [0]

        def cast_weight(name, l):
            shp = wshapes[name]
            rows, cols = shp[1], shp[2]
            step = max(128, (2 * 1024 * 1024 // cols) // 128 * 128)
            ts_ = []
            r = 0
            while r < rows:
                r1 = min(rows, r + step)
                t = T(f"{name}{l}_{r}")
                ncast[0] += 1
                dma(wb[name][l, r:r1, :], wf[name][l, r:r1, :], [], [t], q="gpsimd", key=f"wcast{ncast[0] % 16}")
                ts_.append(t)
                r = r1
            Twb[(name, l)] = ts_

        wplan = []
        wstate = {"issued": 0, "used": 0}

        def w_issue():
            while wstate["issued"] < len(wplan) and wstate["issued"] < wstate["used"] + NW:
                i = wstate["issued"]
                name, l, row0, nk, c0, ncols = wplan[i]
                slot = i % NW
                dst = wbuf[slot][:, 0:nk * ncols].rearrange("p (k c) -> p k c", c=ncols)
                src = wb[name][l, row0:row0 + nk * 128, c0:c0 + ncols].rearrange("(k p) c -> p k c", p=128)
                dma(dst, src, Twb[(name, l)], [Twbuf[slot]], q="sync", key=f"w{slot}")
                wstate["issued"] += 1

        def w_next(spec):
            i = wstate["used"]
            assert wplan[i] == spec, (i, wplan[i], spec)
            w_issue()
            wstate["used"] += 1
            slot = i % NW
            nk, ncols = spec[3], spec[5]
            return wbuf[slot][:, 0:nk * ncols].rearrange("p (k c) -> p k c", c=ncols), Twbuf[slot]

        def plan_cols(name, l, total, step=512, nk=8):
            return [(name, l, 0, nk, c, min(step, total - c)) for c in range(0, total, step)]

        def plan_down(l):
            out = []
            for jp in range(4):
                for kh in range(2):
                    out.append(("w_ffn_down", l, kh * 11 * 128, 11, jp * 256, 256))
            return out

        def plan_tile():
            pl = []
            for l in range(4):
                if l < 2:
                    pl += plan_cols("w_in_a", l, A_IN)
                else:
                    pl += plan_cols("w_in_b", l - 2, D)
                pl += plan_cols("w_out", l, D)
                pl += plan_cols("w_ffn_up", l, 2 * DFF)
                pl += plan_down(l)
                if l == 1:
                    pl += plan_cols("w_kv", 0, KV_OUT)
            return pl

        def rms_xn(gcol0, T_, arena):
            sqr = [sb(arena, "sqr", [128, 512], BF16) for _ in range(2)]
            Tsq = [T(), T()]
            rs = sb(arena, "rs", [128, 512], F32)
            Trs = T()
            b = ps()
            for c in range(8):
                i = c % 2
                act(sqr[i][:, 0:T_], h[:, c, 0:T_], AF.Square, [Th[c]], [Tsq[i]])
                mm(pb[b][:, 0:T_], ones_b, sqr[i][:, 0:T_], c == 0, c == 7, [Tsq[i], Tcb], [Tpb[b]])
            act(rs[:, 0:T_], pb[b][:, 0:T_], AF.Ln, [Tpb[b]], [Trs], scale=1.0 / D, bias=EPS)
            act(rs[:, 0:T_], rs[:, 0:T_], AF.Exp, [Trs], [Trs], scale=-0.5)
            for c in range(8):
                stt(xn[:, c, 0:T_], h[:, c, 0:T_], pcol(gcol0 + c), rs[:, 0:T_], ALU.mult, ALU.mult,
                    [Th[c], Trs, Tprm], [Txn[c]])

        def rstd_bufs(arena, n=2):
            return {"i": 0, "b": [(sb(arena, "hsq", [128, 512], BF16), T(), sb(arena, "hrs", [128, 512], F32), T()) for _ in range(n)]}

        def head_rstd(src, Tsrc, n, onesap, T_, rb, bank=None):
            rb["i"] = (rb["i"] + 1) % len(rb["b"])
            sq, Tsq, rs, Trs = rb["b"][rb["i"]]
            act(sq[:, 0:T_], src, AF.Square, [Tsrc], [Tsq])
            b = ps() if bank is None else bank
            mm(pb[b][:, 0:T_], onesap, sq[:, 0:T_], True, True, [Tsq, Tcb], [Tpb[b]])
            act(rs[:, 0:T_], pb[b][:, 0:T_], AF.Ln, [Tpb[b]], [Trs], scale=1.0 / n, bias=EPS)
            act(rs[:, 0:T_], rs[:, 0:T_], AF.Exp, [Trs], [Trs], scale=-0.5)
            return rs, Trs

        def attn_pair(lhsK, rhsQ, lhsV, blocks, nq, out_ap, Tout, Kreads, Qreads, Vreads, arena_bufs):
            Pt, TPt, R, TR, comb, Tcomb = arena_bufs
            Ob = []
            for hh in range(2):
                ob = ps()
                while ob in Ob:
                    ob = ps()
                Ob.append(ob)
            nb = len(blocks)
            items = [(hh, bi) + tuple(blocks[bi]) for hh in range(2) for bi in range(nb)]
            DEPTH = 2
            NP_ = len(Pt)

            def issue_s(idx):
                hh, bi, j, nk, q0, diag = items[idx]
                w = nq - q0
                sbk = ps()
                while sbk in Ob:
                    sbk = ps()
                mm(pb[sbk][0:nk, 0:w], lhsK(hh, j, nk), rhsQ(hh, q0), True, True, Kreads + Qreads, [Tpb[sbk]])
                pi = idx % NP_
                act(Pt[pi][0:nk, 0:w], pb[sbk][0:nk, 0:w], AF.Exp, [Tpb[sbk]], [TPt[pi]])
                if diag:
                    tt(Pt[pi][0:nk, 0:nk], Pt[pi][0:nk, 0:nk], tri_b[0:nk, 0:nk], ALU.mult, [TPt[pi], Tcb], [TPt[pi]],
                       eng="gpsimd")

            def issue_pv(idx):
                hh, bi, j, nk, q0, diag = items[idx]
                w = nq - q0
                pi = idx % NP_
                mm(pb[Ob[hh]][:, q0:nq], lhsV(hh, j, nk), Pt[pi][0:nk, 0:w], bi == 0, bi == nb - 1,
                   [TPt[pi]] + Vreads, [Tpb[Ob[hh]]])

            for idx in range(len(items)):
                issue_s(idx)
                if idx >= DEPTH:
                    issue_pv(idx - DEPTH)
            for idx in range(max(0, len(items) - DEPTH), len(items)):
                issue_pv(idx)
            oa, ob_ = Ob
            act(R[0:64, 0:nq], pb[ob_][0:64, 0:nq], AF.Ln, [Tpb[ob_]], [TR])
            act(R[64:128, 0:nq], pb[oa][64:128, 0:nq], AF.Ln, [Tpb[oa]], [TR])
            act(R[:, 0:nq], R[:, 0:nq], AF.Exp, [TR], [TR], scale=-1.0)
            pr = ps()
            while pr in Ob:
                pr = ps()
            mm(pb[pr][:, 0:nq], swap_f, R[:, 0:nq], True, True, [TR, Tcst], [Tpb[pr]])
            cp(comb[0:64, 0:nq], pb[oa][0:64, 0:nq], [Tpb[oa]], [Tcomb], eng="scalar")
            cp(comb[64:128, 0:nq], pb[ob_][64:128, 0:nq], [Tpb[ob_]], [Tcomb], eng="scalar")
            tt(out_ap, comb[:, 0:nq], pb[pr][:, 0:nq], ALU.mult, [Tcomb, Tpb[pr]], [Tout])

        def attn_bufs(arena):
            Pt = [sb(arena, "Pt", [128, 512], BF16) for _ in range(4)]
            R = sb(arena, "R", [128, 512], F32)
            comb = sb(arena, "comb", [128, 512], F32)
            return Pt, [T(), T(), T(), T()], R, T(), comb, T()

        def mem_attend(l, qm, Tqm, T_, arena):
            bufs = attn_bufs(arena)
            Qm = sb(arena, "Qm", [128, 2, 512], BF16)
            TQm = [T(), T()]
            rb = rstd_bufs(arena)
            for cc in range(2):
                rs, Trs = head_rstd(qm[:, cc, 0:T_], Tqm[cc], 64, bd_b, T_, rb)
                stt(Qm[:, cc, 0:T_], qm[:, cc, 0:T_], drv[:, 16 + l:17 + l], rs[:, 0:T_], ALU.mult, ALU.mult,
                    [Tqm[cc], Trs, Tdrv], [TQm[cc]])
            blocks = [(0, 128, 0, False), (1, 128, 0, False)]
            for cc in range(2):
                def lhsK(hh, j, nk, cc=cc):
                    return MKT[hh * 64:(hh + 1) * 64, l, cc, j * 128:(j + 1) * 128]

                def rhsQ(hh, q0, cc=cc):
                    return Qm[hh * 64:(hh + 1) * 64, cc, 0:T_]

                def lhsV(hh, j, nk, cc=cc):
                    return MVb[:, l, j, cc, 0:128] if hh == 0 else MVb[:, l, j, cc, 128:256]
                attn_pair(lhsK, rhsQ, lhsV, blocks, T_, mix[:, 6 + cc, 0:T_], Tmix[6 + cc], [TMK], [TQm[cc]], [TMV], bufs)

        def out_proj_ffn(l, T_):
            for (name, ll, r0, nk, c0, ncols) in plan_cols("w_out", l, D):
                wv, Tw = w_next((name, ll, r0, nk, c0, ncols))
                for jj in range(ncols // 128):
                    j = c0 // 128 + jj
                    b = ps()
                    for k in range(8):
                        mm(pb[b][:, 0:T_], wv[:, k, jj * 128:(jj + 1) * 128], mix[:, k, 0:T_], k == 0, k == 7,
                           [Tw, Tmix[k]], [Tpb[b]])
                    tt(h[:, j, 0:T_], h[:, j, 0:T_], pb[b][:, 0:T_], ALU.add, [Th[j], Tpb[b]], [Th[j]])
            P.barrier()
            with ExitStack() as arena:
                rms_xn(P_NFFN + 8 * l, T_, arena)
                hid = sb(arena, "hid", [128, 22, 512], BF16)
                Thid = [T() for _ in range(22)]
                for (name, ll, r0, nk, c0, ncols) in plan_cols("w_ffn_up", l, 2 * DFF):
                    wv, Tw = w_next((name, ll, r0, nk, c0, ncols))
                    for jj in range(ncols // 128):
                        j = c0 // 128 + jj
                        b = ps()
                        for k in range(8):
                            mm(pb[b][:, 0:T_], wv[:, k, jj * 128:(jj + 1) * 128], xn[:, k, 0:T_], k == 0, k == 7,
                               [Tw, Txn[k]], [Tpb[b]])
                        if j < 22:
                            act(hid[:, j, 0:T_], pb[b][:, 0:T_], AF.Silu, [Tpb[b]], [Thid[j]])
                        else:
                            tt(hid[:, j - 22, 0:T_], pb[b][:, 0:T_], hid[:, j - 22, 0:T_], ALU.mult,
                               [Tpb[b], Thid[j - 22]], [Thid[j - 22]])
                for jp in range(4):
                    bs = [ps(), ps()]
                    for kh in range(2):
                        spec = ("w_ffn_down", l, kh * 11 * 128, 11, jp * 256, 256)
                        wv, Tw = w_next(spec)
                        for jj in range(2):
                            for k in range(11):
                                kk = kh * 11 + k
                                mm(pb[bs[jj]][:, 0:T_], wv[:, k, jj * 128:(jj + 1) * 128], hid[:, kk, 0:T_],
                                   kk == 0, kk == 21, [Tw, Thid[kk]], [Tpb[bs[jj]]])
                    for jj in range(2):
                        j = jp * 2 + jj
                        tt(h[:, j, 0:T_], h[:, j, 0:T_], pb[bs[jj]][:, 0:T_], ALU.add, [Th[j], Tpb[bs[jj]]], [Th[j]])
                P.barrier()

        def layer_a(l, T_, seq_is_sample):
            CL = min(64, T_)
            NCH = T_ // CL
            G = min(2, NCH)
            GT = G * CL
            NG = NCH // G
            with ExitStack() as arena:
                rms_xn(P_NMIX + 8 * l, T_, arena)
                sqb = sb(arena, "sqb", [128, 6, 512], BF16)
                kb = sb(arena, "kb", [128, 6, 512], BF16)
                bfa = sb(arena, "bfa", [128, 6, 512], F32)
                gate = sb(arena, "gate", [128, 6, 512], BF16)
                qm = sb(arena, "qm", [128, 2, 512], F32)
                Vt = sb(arena, "Vt", [128, 4, MAIN], BF16)
                tmp = [sb(arena, "tmpa", [128, 512], F32) for _ in range(3)]
                Ttmp = [T(), T(), T()]
                Tsq_, Tkb, Tbf, Tgate, Tqm = ([T() for _ in range(6)] for _ in range(5))
                Tqm = [T(), T()]
                TVt = [T() for _ in range(4)]
                tn = [0]

                def nxt():
                    tn[0] = (tn[0] + 1) % 3
                    return tn[0]
                Qp = sb(arena, "Qp", [128, 6, 512], BF16)
                Qt = sb(arena, "Qt", [128, 6, 512], BF16)
                Kt = sb(arena, "Kt", [128, 6, 512], BF16)
                KhT = sb(arena, "KhT", [128, 6, 512], BF16)
                Kh = sb(arena, "Kh", [128, 4, MAIN], BF16)
                dec = sb(arena, "dec", [128, 6, 8], F32)
                em = sb(arena, "em", [128, 6, 8], F32)
                edl = sb(arena, "edl", [128, 6, 8], F32)
                Tem = [T() for _ in range(6)]
                Tedl = [T() for _ in range(6)]
                ATt = sb(arena, "ATt", [128, 6, 4, 128], BF16)
                rs6 = sb(arena, "rs6", [128, 6, 512], F32)
                TQp, TQt, TKt, TKhT, Tdec = ([T() for _ in range(6)] for _ in range(5))
                TKh = [T() for _ in range(4)]
                TAT = [[T() for _ in range(4)] for _ in range(6)]
                mset(ATt[:], 0.0, [t for row in TAT for t in row], eng="gpsimd")
                def hgrn_elem(hh):
                    P.op("vector", lambda e, hh=hh: e.tensor_tensor_scan(out=bfa[:, hh, 0:T_], data0=reset_f[:, 0:T_],
                                                                          data1=bfa[:, hh, 0:T_], initial=0.0,
                                                                          op0=ALU.mult, op1=ALU.add),
                         reads=[Tbf[hh], Tcst], writes=[Tbf[hh]])
                    b3 = bfa[:, hh, 0:T_].rearrange("p (c l) -> p c l", l=CL)
                    mid = b3[:, :, CL // 2 - 1:CL // 2].broadcast_to([128, NCH, CL])
                    act(em[:, hh, 0:NCH], b3[:, :, CL // 2 - 1], AF.Exp, [Tbf[hh]], [Tem[hh]])
                    act(dec[:, hh, 0:NCH], b3[:, :, CL - 1], AF.Exp, [Tbf[hh]], [Tdec[hh]])
                    i1 = nxt()
                    t3 = tmp[i1][:, 0:T_].rearrange("p (c l) -> p c l", l=CL)
                    tt(t3, b3, mid, ALU.subtract, [Tbf[hh]], [Ttmp[i1]])
                    i2 = nxt()
                    act(tmp[i2][:, 0:T_], tmp[i1][:, 0:T_], AF.Exp, [Ttmp[i1]], [Ttmp[i2]])
                    tt(Qt[:, hh, 0:T_], sqb[:, hh, 0:T_], tmp[i2][:, 0:T_], ALU.mult, [Tsq_[hh], Ttmp[i2]], [TQt[hh]])
                    e3 = tmp[i2][:, 0:T_].rearrange("p (c l) -> p c l", l=CL)
                    cp(edl[:, hh, 0:NCH], e3[:, :, CL - 1], [Ttmp[i2]], [Tedl[hh]])
                    act(tmp[i1][:, 0:T_], tmp[i1][:, 0:T_], AF.Exp, [Ttmp[i1]], [Ttmp[i1]], scale=-1.0)
                    tt(Kt[:, hh, 0:T_], kb[:, hh, 0:T_], tmp[i1][:, 0:T_], ALU.mult, [Tkb[hh], Ttmp[i1]], [TKt[hh]])
                    q3 = Qt[:, hh, 0:T_].rearrange("p (c l) -> p c l", l=CL)
                    k3_ = Kt[:, hh, 0:T_].rearrange("p (c l) -> p c l", l=CL)
                    qp3 = Qp[:, hh, 0:T_].rearrange("p (c l) -> p c l", l=CL)
                    kh3 = KhT[:, hh, 0:T_].rearrange("p (c l) -> p c l", l=CL)
                    em_bc = em[:, hh, 0:NCH].rearrange("p (c o) -> p c o", o=1).broadcast_to([128, NCH, CL])
                    edl_bc = edl[:, hh, 0:NCH].rearrange("p (c o) -> p c o", o=1).broadcast_to([128, NCH, CL])
                    tt(qp3, q3, em_bc, ALU.mult, [TQt[hh], Tem[hh]], [TQp[hh]])
                    tt(kh3, k3_, edl_bc, ALU.mult, [TKt[hh], Tedl[hh]], [TKhT[hh]])
                for (name, ll, r0, nk, c0, ncols) in plan_cols("w_in_a", l, A_IN):
                    wv, Tw = w_next((name, ll, r0, nk, c0, ncols))
                    bidx = c0 // 512
                    tok = {3: (0, 512, 0), 4: (0, 256, 512)}.get(bidx)
                    if tok is not None:
                        wc0, wn, vc0 = tok
                        for tb in range(NG):
                            b = ps()
                            for k in range(8):
                                mm(pb[b][0:GT, 0:wn], xn[:, k, tb * GT:(tb + 1) * GT], wv[:, k, wc0:wc0 + wn], k == 0, k == 7,
                                   [Tw, Txn[k]], [Tpb[b]])
                            cp(Vt[0:GT, tb, vc0:vc0 + wn], pb[b][0:GT, 0:wn], [Tpb[b]], [TVt[tb]], eng="scalar")
                    for jj in range(ncols // 128):
                        j = c0 // 128 + jj
                        if 12 <= j < 18:
                            continue
                        b = ps()
                        for k in range(8):
                            mm(pb[b][:, 0:T_], wv[:, k, jj * 128:(jj + 1) * 128], xn[:, k, 0:T_], k == 0, k == 7,
                               [Tw, Txn[k]], [Tpb[b]])
                        if j < 6:
                            act(sqb[:, j, 0:T_], pb[b][:, 0:T_], AF.Silu, [Tpb[b]], [Tsq_[j]])
                        elif j < 12:
                            hh = j - 6
                            i0 = nxt()
                            act(tmp[i0][:, 0:T_], pb[b][:, 0:T_], AF.Exp, [Tpb[b]], [Ttmp[i0]], scale=-1.0)
                            act(tmp[i0][:, 0:T_], tmp[i0][:, 0:T_], AF.Ln, [Ttmp[i0]], [Ttmp[i0]], bias=1.0)
                            act(tmp[i0][:, 0:T_], tmp[i0][:, 0:T_], AF.Exp, [Ttmp[i0]], [Ttmp[i0]], scale=-1.0)
                            tsc(tmp[i0][:, 0:T_], tmp[i0][:, 0:T_], drv[:, 20 + l * 6 + hh:21 + l * 6 + hh], ALU.mult,
                                [Ttmp[i0], Tdrv], [Ttmp[i0]], s2=drv[:, l * 6 + hh:l * 6 + hh + 1], op1=ALU.add)
                            tsc(tmp[i0][:, 0:T_], tmp[i0][:, 0:T_], K_MAX, ALU.min, [Ttmp[i0]], [Ttmp[i0]])
                            cp(kb[:, hh, 0:T_], tmp[i0][:, 0:T_], [Ttmp[i0]], [Tkb[hh]], eng="gpsimd")
                            act(bfa[:, hh, 0:T_], tmp[i0][:, 0:T_], AF.Ln, [Ttmp[i0]], [Tbf[hh]], scale=-1.0, bias=1.0)
                            hgrn_elem(hh)
                        elif j < 24:
                            act(gate[:, j - 18, 0:T_], pb[b][:, 0:T_], AF.Silu, [Tpb[b]], [Tgate[j - 18]])
                        else:
                            cp(qm[:, j - 24, 0:T_], pb[b][:, 0:T_], [Tpb[b]], [Tqm[j - 24]], eng="scalar")
                chk(30)
                with ExitStack() as a2:
                    mem_attend(l, qm, Tqm, T_, a2)
                chk(31)
                chk(32)
                for tb in range(NG):
                    b = ps()
                    pbv = pb[b][:].bitcast(BF16)
                    for hh in range(6):
                        tr(pbv[0:GT, hh * 128:(hh + 1) * 128], KhT[:, hh, tb * GT:(tb + 1) * GT], id_b, [TKhT[hh], Tcb], [Tpb[b]])
                    cp(Kh[0:GT, tb, :], pbv[0:GT, 0:MAIN], [Tpb[b]], [TKh[tb]])
                for hh in range(6):
                    for gi in range(NG):
                        b = ps()
                        r = 0
                        sl = slice(gi * GT, (gi + 1) * GT)
                        mm(pb[b][0:GT, r * 128:r * 128 + GT], Kt[:, hh, sl], Qt[:, hh, sl], True, True,
                           [TKt[hh], TQt[hh]], [Tpb[b]])
                        for ci in range(G):
                            ps_ = slice(ci * CL, (ci + 1) * CL)
                            tt(ATt[ps_, hh, gi, ci * CL:(ci + 1) * CL], pb[b][ps_, r * 128 + ci * CL:r * 128 + (ci + 1) * CL],
                               tri_f[ps_, ci * CL:(ci + 1) * CL], ALU.mult, [Tpb[b], Tcst], [TAT[hh][gi]])
                P.barrier()
                chk(33)
                srn = [0]
                for gi in range(NG):
                    sl = slice(gi * GT, (gi + 1) * GT)
                    for hh in range(6):
                        mm(pb[hh][:, sl], Vt[0:GT, gi, hh * 128:(hh + 1) * 128], ATt[0:GT, hh, gi, 0:GT], True, False,
                           [TVt[gi], TAT[hh][gi]], [Tpb[hh]])
                    for ci in range(G):
                        c = gi * G + ci
                        cs = slice(c * CL, (c + 1) * CL)
                        for hh in range(6):
                            mm(pb[hh][:, cs], Sbf[l][:, hh, :], Qp[:, hh, cs], False, ci == G - 1,
                               [TSbf[l][hh], TQp[hh]], [Tpb[hh]])
                        prow = slice(ci * CL, (ci + 1) * CL)
                        for hh in range(6):
                            srn[0] = (srn[0] + 1) % 2
                            r = srn[0]
                            reg = pb[6 + r][:, 0:128]
                            mm(reg, Kh[prow, gi, hh * 128:(hh + 1) * 128], Vt[prow, gi, hh * 128:(hh + 1) * 128], True, True,
                               [TKh[gi], TVt[gi]], [Tpb[6 + r]])
                            stt(S32[l][:, hh, :], S32[l][:, hh, :], dec[:, hh, c:c + 1], reg, ALU.mult, ALU.add,
                                [TS32[l][hh], Tdec[hh], Tpb[6 + r]], [TS32[l][hh]])
                            cp(Sbf[l][:, hh, :], S32[l][:, hh, :], [TS32[l][hh]], [TSbf[l][hh]], eng="gpsimd")
                P.barrier()
                chk(34)
                To32 = [T() for _ in range(6)]
                Tsq6 = [T() for _ in range(6)]
                Trs6 = [T() for _ in range(6)]
                for hh in range(6):
                    cp(bfa[:, hh, 0:T_], pb[hh][:, 0:T_], [Tpb[hh]], [To32[hh]], eng="scalar")
                    act(Qt[:, hh, 0:T_], pb[hh][:, 0:T_], AF.Square, [Tpb[hh]], [Tsq6[hh]])
                for hh in range(6):
                    mm(pb[hh][:, 0:T_], ones_b, Qt[:, hh, 0:T_], True, True, [Tsq6[hh], Tcb], [Tpb[hh]])
                for hh in range(6):
                    act(rs6[:, hh, 0:T_], pb[hh][:, 0:T_], AF.Ln, [Tpb[hh]], [Trs6[hh]], scale=1.0 / 128, bias=EPS)
                    act(rs6[:, hh, 0:T_], rs6[:, hh, 0:T_], AF.Exp, [Trs6[hh]], [Trs6[hh]], scale=-0.5)
                    stt(bfa[:, hh, 0:T_], bfa[:, hh, 0:T_], pcol(P_GN + l), rs6[:, hh, 0:T_], ALU.mult, ALU.mult,
                        [To32[hh], Trs6[hh], Tprm], [To32[hh]])
                    tt(mix[:, hh, 0:T_], bfa[:, hh, 0:T_], gate[:, hh, 0:T_], ALU.mult, [To32[hh], Tgate[hh]], [Tmix[hh]])
                P.barrier()
            chk(35)
            out_proj_ffn(l, T_)

        def c_block(lf_ap, Tlf, n, cT_ap, TcT, first_col_prev):
            b = ps()
            mm(pb[b][0:12, 0:n], lf_ap, tri_f[0:n, 0:n], True, True, [Tlf, Tcst], [Tpb[b]])
            tsc(cT_ap, pb[b][0:12, 0:n], first_col_prev, ALU.add, [Tpb[b], TcT, TcTl], [TcT])

        def split_and_store(si, cT, TcT, n, t0, arena, qa=None):
            n0 = sb(arena, "n0", [12, 512], F32)
            r1 = sb(arena, "r1", [12, 512], F32)
            parts = sb(arena, "parts", [12, 3, 512], BF16)
            qparts = sb(arena, "qparts", [12, 3, 512], BF16)
            onesr = sb(arena, "onesr", [12, 3, 512], BF16)
            Tn0, Tr1, Tparts, Tqp, Tones = T(), T(), T(), T(), T()
            mset(onesr[:], 1.0, [Tones], eng="gpsimd")
            tsc(n0[:, 0:n], cT, -1.0, ALU.mult, [TcT], [Tn0])
            cp(parts[:, 0, 0:n], n0[:, 0:n], [Tn0], [Tparts])
            tt(r1[:, 0:n], n0[:, 0:n], parts[:, 0, 0:n], ALU.subtract, [Tn0, Tparts], [Tr1])
            cp(parts[:, 1, 0:n], r1[:, 0:n], [Tr1], [Tparts])
            tt(n0[:, 0:n], r1[:, 0:n], parts[:, 1, 0:n], ALU.subtract, [Tr1, Tparts], [Tn0])
            cp(parts[:, 2, 0:n], n0[:, 0:n], [Tn0], [Tparts])
            dma(KA[si][:, 64:67, t0:t0 + n], parts[:, :, 0:n], [Tparts], [], q="gpsimd", acc=[TKA[si]])
            dma(KA[si][:, 67:70, t0:t0 + n], onesr[:, :, 0:n], [Tones], [], q="gpsimd", acc=[TKA[si]])
            if qa is not None:
                tsc(qparts[:, :, 0:n], parts[:, :, 0:n], -1.0, ALU.mult, [Tparts], [Tqp])
                dma(QA[qa][:, 64:67, 0:n], onesr[:, :, 0:n], [Tones], [], q="gpsimd", acc=[TQA[qa]])
                dma(QA[qa][:, 67:70, 0:n], qparts[:, :, 0:n], [Tqp], [], q="gpsimd", acc=[TQA[qa]])

        TKA = [T(f"KA{i}") for i in range(NS + NSS)]
        TVS = [T(f"VS{i}") for i in range(NS + NSS)]
        TQA = [T("QA0"), T("QA1")]

        def kv_stage(si, sl_, t0, T_, qa, kout, vout, lfout, o0):
            GT = min(128, T_)
            NTB = T_ // GT
            with ExitStack() as arena:
                rms_xn(P_NKV, T_, arena)
                kvt = [sb(arena, "kvt", [128, KV_OUT], F32) for _ in range(NTB)]
                Tkvt = [T() for _ in range(NTB)]
                kT = sb(arena, "kT", [128, 6, 512], BF16)
                TkT = [T() for _ in range(6)]
                vb = sb(arena, "vb", [128, 4, MAIN], BF16)
                Tvb = [T() for _ in range(4)]
                sqk = sb(arena, "sqk", [128, MAIN], F32)
                Tsqk = T()
                cT = sb(arena, "cT", [12, 512], F32)
                TcT = T()
                for spec in plan_cols("w_kv", 0, KV_OUT):
                    wv, Tw = w_next(spec)
                    c0, ncols = spec[4], spec[5]
                    for tb in range(NTB):
                        b = ps()
                        for k in range(8):
                            mm(pb[b][0:GT, 0:ncols], xn[:, k, tb * GT:(tb + 1) * GT], wv[:, k, 0:ncols], k == 0, k == 7,
                               [Tw, Txn[k]], [Tpb[b]])
                        cp(kvt[tb][0:GT, c0:c0 + ncols], pb[b][0:GT, 0:ncols], [Tpb[b]], [Tkvt[tb]], eng="scalar")
                lft4 = sb(arena, "lft4", [128, 4, 12], F32)
                Tlft4 = [T() for _ in range(4)]
                ss4 = sb(arena, "ss4", [128, 4, 12], F32)
                Tss4 = [T() for _ in range(4)]
                for tb in range(NTB):
                    ki = tb
                    rows = slice(t0 + tb * GT, t0 + (tb + 1) * GT)
                    orows = slice(o0 + tb * GT, o0 + (tb + 1) * GT)
                    dma(vout[sl_, orows, :], kvt[ki][0:GT, MAIN:2 * MAIN], [Tkvt[ki]], [], q="gpsimd")
                    cp(vb[0:GT, tb, :], kvt[ki][0:GT, MAIN:2 * MAIN], [Tkvt[ki]], [Tvb[tb]], eng="gpsimd")
                    dma(VS[si][rows, :], vb[0:GT, tb, :], [Tvb[tb]], [], q="gpsimd", acc=[TVS[si]])
                    k3 = kvt[ki][0:GT, 0:MAIN].rearrange("p (h d) -> p h d", d=64)
                    tt(sqk[0:GT, :], kvt[ki][0:GT, 0:MAIN], kvt[ki][0:GT, 0:MAIN], ALU.mult, [Tkvt[ki]], [Tsqk])
                    P.op("vector", lambda e, GT=GT, tb=tb: e.tensor_reduce(out=ss4[0:GT, tb, :], in_=sqk[0:GT, :].rearrange("p (h d) -> p h d", d=64),
                                                                            axis=AX.X, op=ALU.add), reads=[Tsqk], writes=[Tss4[tb]])
                    act(ss4[0:GT, tb, :], ss4[0:GT, tb, :], AF.Ln, [Tss4[tb]], [Tss4[tb]], scale=1.0 / 64, bias=EPS)
                    act(ss4[0:GT, tb, :], ss4[0:GT, tb, :], AF.Exp, [Tss4[tb]], [Tss4[tb]], scale=-0.5)
                    tt(k3, k3, ss4[0:GT, tb, :].rearrange("p (h o) -> p h o", o=1).broadcast_to([GT, 12, 64]), ALU.mult,
                       [Tkvt[ki], Tss4[tb]], [Tkvt[ki]])
                    tt(k3, k3, prm[0:GT, P_FGK:P_FGK + 64].rearrange("p (o d) -> p o d", o=1).broadcast_to([GT, 12, 64]), ALU.mult,
                       [Tkvt[ki], Tprm], [Tkvt[ki]])
                    dma(kout[sl_, orows, :], kvt[ki][0:GT, 0:MAIN], [Tkvt[ki]], [], q="gpsimd")
                    tt(lft4[0:GT, tb, :], kvt[ki][0:GT, 2 * MAIN:2 * MAIN + 12], prm[0:GT, P_BF:P_BF + 12], ALU.add,
                       [Tkvt[ki], Tprm], [Tlft4[tb]])
                    act(lft4[0:GT, tb, :], lft4[0:GT, tb, :], AF.Exp, [Tlft4[tb]], [Tlft4[tb]], scale=-1.0)
                    act(lft4[0:GT, tb, :], lft4[0:GT, tb, :], AF.Ln, [Tlft4[tb]], [Tlft4[tb]], bias=1.0)
                    tsc(lft4[0:GT, tb, :], lft4[0:GT, tb, :], -1.0, ALU.mult, [Tlft4[tb]], [Tlft4[tb]])
                    dma(lfout[sl_, orows, :], lft4[0:GT, tb, :], [Tlft4[tb]], [], q="gpsimd")
                for tb in range(NTB):
                    ki = tb
                    for cc in range(6):
                        b = ps()
                        tr(pb[b][:, 0:GT], kvt[ki][0:GT, cc * 128:(cc + 1) * 128], id_f[0:GT, 0:GT], [Tkvt[ki], Tcst], [Tpb[b]])
                        cp(kT[:, cc, tb * GT:(tb + 1) * GT], pb[b][:, 0:GT], [Tpb[b]], [TkT[cc]], eng=("scalar" if cc % 2 else "vector"))
                    prev = cTlast[:, 0:1] if tb == 0 else cT[:, tb * GT - 1:tb * GT]
                    c_block(lft4[0:GT, tb, :], Tlft4[tb], GT, cT[:, tb * GT:(tb + 1) * GT], TcT, prev)
                cp(cTlast[:, 0:1], cT[:, T_ - 1:T_], [TcT], [TcTl])
                for cc in range(6):
                    for hh in range(2):
                        dma(KA[si][2 * cc + hh, 0:64, t0:t0 + T_], kT[hh * 64:(hh + 1) * 64, cc, 0:T_], [TkT[cc]], [], q="gpsimd", acc=[TKA[si]])
                split_and_store(si, cT[:, 0:T_], TcT, T_, t0, arena, qa=qa)
                P.barrier()

        def layer_b(l, si, t0, T_, qa):
            j2 = l - 2
            L = t0 + T_
            NBF = L // 128
            rem = L - NBF * 128
            NB = NBF + (1 if rem else 0)
            with ExitStack() as arena:
                rms_xn(P_NMIX + 8 * l, T_, arena)
                bufs = attn_bufs(arena)
                KAb = [[sb(arena, "KAb", [70, L], BF16) for _ in range(2)], None]
                QAb = [[sb(arena, "QAb", [70, 512], BF16) for _ in range(2)] for _ in range(2)]
                Vb = [sb(arena, "Vb", [128, NB, 256], BF16), None]
                TKAb = [[T(), T()], [T(), T()]]
                TQAb = [[T(), T()], [T(), T()]]
                TVb = [T(), T()]
                Qn = sb(arena, "Qn", [128, 6, 512], BF16)
                TQn = [T() for _ in range(6)]
                qm = sb(arena, "qm", [128, 2, 512], F32)
                Tqm = [T(), T()]
                mset(Vb[0][:, :, 64:192], 1.0, [TVb[0]], eng="gpsimd")

                def issue_kv(cc):
                    st_ = cc % 2
                    for hh in range(2):
                        dma(KAb[st_][hh][:, :], KA[si][2 * cc + hh, :, 0:L], [TKA[si]], [TKAb[st_][hh]], q="sync", key=f"kab{st_}{hh}")
                    for u in range(2):
                        vc = cc * 128 + u * 64
                        dc = u * 192
                        if NBF:
                            dma(Vb[st_][:, 0:NBF, dc:dc + 64], VS[si][0:NBF * 128, vc:vc + 64].rearrange("(b p) d -> p b d", p=128),
                                [TVS[si]], [], q="sync", key=f"vb{st_}{u}", acc=[TVb[st_]])
                        if rem:
                            dma(Vb[st_][0:rem, NBF, dc:dc + 64], VS[si][NBF * 128:L, vc:vc + 64],
                                [TVS[si]], [], q="sync", key=f"vb{st_}{u}", acc=[TVb[st_]])

                def issue_q(cc):
                    st_ = cc % 2
                    for hh in range(2):
                        dma(QAb[st_][hh][:, 0:T_], QA[qa][2 * cc + hh, :, 0:T_], [TQA[qa]], [TQAb[st_][hh]], q="sync", key=f"qab{st_}{hh}")

                issue_kv(0)
                with ExitStack() as a1:
                    qf6 = sb(a1, "qf6", [128, 6, 512], F32)
                    Tqf = [T() for _ in range(6)]
                    sq6 = sb(a1, "sq6", [128, 6, 512], BF16)
                    Tsq6 = [T() for _ in range(6)]
                    rs6 = sb(a1, "rs6b", [128, 6, 512], F32)
                    Trs6 = [T() for _ in range(6)]
                    for (name, ll, r0, nk, c0, ncols) in plan_cols("w_in_b", j2, D):
                        wv, Tw = w_next((name, ll, r0, nk, c0, ncols))
                        for jj in range(ncols // 128):
                            j = c0 // 128 + jj
                            b = ps()
                            for k in range(8):
                                mm(pb[b][:, 0:T_], wv[:, k, jj * 128:(jj + 1) * 128], xn[:, k, 0:T_], k == 0, k == 7,
                                   [Tw, Txn[k]], [Tpb[b]])
                            if j < 6:
                                cp(qf6[:, j, 0:T_], pb[b][:, 0:T_], [Tpb[b]], [Tqf[j]], eng="scalar")
                                act(sq6[:, j, 0:T_], pb[b][:, 0:T_], AF.Square, [Tpb[b]], [Tsq6[j]])
                            else:
                                cp(qm[:, j - 6, 0:T_], pb[b][:, 0:T_], [Tpb[b]], [Tqm[j - 6]], eng="scalar")
                    nb_ = []
                    for j in range(6):
                        b = ps()
                        nb_.append(b)
                        mm(pb[b][:, 0:T_], bd_b, sq6[:, j, 0:T_], True, True, [Tsq6[j], Tcb], [Tpb[b]])
                    for j in range(6):
                        b = nb_[j]
                        act(rs6[:, j, 0:T_], pb[b][:, 0:T_], AF.Ln, [Tpb[b]], [Trs6[j]], scale=1.0 / 64, bias=EPS)
                        act(rs6[:, j, 0:T_], rs6[:, j, 0:T_], AF.Exp, [Trs6[j]], [Trs6[j]], scale=-0.5)
                        stt(Qn[:, j, 0:T_], qf6[:, j, 0:T_], drv[:, 12 + j2:13 + j2], rs6[:, j, 0:T_], ALU.mult, ALU.mult,
                            [Tqf[j], Trs6[j], Tdrv], [TQn[j]])
                        for hh in range(2):
                            dma(QA[qa][2 * j + hh, 0:64, 0:T_], Qn[hh * 64:(hh + 1) * 64, j, 0:T_], [TQn[j]], [], q="gpsimd", acc=[TQA[qa]])
                    P.barrier()
                issue_q(0)
                issue_q(1)
                with ExitStack() as a2:
                    mem_attend(l, qm, Tqm, T_, a2)
                    P.barrier()
                KAb[1] = [sb(arena, "KAb", [70, L], BF16) for _ in range(2)]
                Vb[1] = sb(arena, "Vb", [128, NB, 256], BF16)
                mset(Vb[1][:, :, 64:192], 1.0, [TVb[1]], eng="gpsimd")
                issue_kv(1)
                blocks = []
                for j in range(NB):
                    nk = 128 if j < NBF else rem
                    if j * 128 < t0:
                        blocks.append((j, nk, 0, False))
                    else:
                        blocks.append((j, nk, j * 128 - t0, True))
                for cc in range(6):
                    st_ = cc % 2

                    def lhsK(hh, j, nk, st_=st_):
                        return KAb[st_][hh][:, j * 128:j * 128 + nk]

                    def rhsQ(hh, q0, st_=st_):
                        return QAb[st_][hh][:, q0:T_]

                    def lhsV(hh, j, nk, st_=st_):
                        return Vb[st_][0:nk, j, 0:128] if hh == 0 else Vb[st_][0:nk, j, 128:256]
                    attn_pair(lhsK, rhsQ, lhsV, blocks, T_, mix[:, cc, 0:T_], Tmix[cc], TKAb[st_], TQAb[st_], [TVb[st_]], bufs)
                    if cc + 2 < 6:
                        issue_kv(cc + 2)
                        issue_q(cc + 2)
                P.barrier()
            out_proj_ffn(l, T_)

        def run_tile(si, sl_, t0, T_, xin_d, yout_d, kout, vout, lfout, qa, sample):
            GT = min(128, T_)
            NTB = T_ // GT
            with ExitStack() as arena:
                xin = sb(arena, "xin", [128, 4, D], F32)
                Txin = T()
                dma(xin[0:GT, 0:NTB, :], xin_d[sl_, t0 - (PAST if sample else 0):t0 - (PAST if sample else 0) + T_, :]
                    .rearrange("(b p) d -> p b d", p=GT), [], [Txin], q="sync", key="xin")
                for c in range(8):
                    b = ps()
                    for tb in range(NTB):
                        tr(pb[b][:, tb * GT:(tb + 1) * GT], xin[0:GT, tb, c * 128:(c + 1) * 128], id_f[0:GT, 0:GT], [Txin, Tcst], [Tpb[b]])
                    cp(h[:, c, 0:T_], pb[b][:, 0:T_], [Tpb[b]], [Th[c]], eng=("scalar" if c % 2 else "vector"))
                P.barrier()
            chk(2)
            layer_a(0, T_, sample)
            chk(3)
            layer_a(1, T_, sample)
            chk(4)
            kv_stage(si, sl_, t0, T_, qa, kout, vout, lfout, t0 - (PAST if sample else 0))
            chk(5)
            layer_b(2, si, t0, T_, qa)
            chk(6)
            layer_b(3, si, t0, T_, qa)
            chk(7)
            with ExitStack() as arena:
                yo = sb(arena, "yo", [128, 4, D], F32)
                Tyo = T()
                for tb in range(NTB):
                    for c4 in range(2):
                        b = ps()
                        for c in range(4):
                            cc = c4 * 4 + c
                            tr(pb[b][0:GT, c * 128:(c + 1) * 128], h[:, cc, tb * GT:(tb + 1) * GT], id_f, [Th[cc], Tcst], [Tpb[b]])
                        cp(yo[0:GT, tb, c4 * 512:(c4 + 1) * 512], pb[b][0:GT, :], [Tpb[b]], [Tyo], eng=("scalar" if c4 else "vector"))
                tq = t0 - (PAST if sample else 0)
                dma(yout_d[sl_, tq:tq + T_, :].rearrange("(b p) d -> p b d", p=GT), yo[0:GT, 0:NTB, :], [Tyo], [], q="gpsimd")
                P.barrier()

        def mem_prologue(sl_, sample):
            with ExitStack() as arena:
                mset(MVb[:, :, :, :, 64:192], 1.0, [TMV], eng="gpsimd")
                kvm = [sb(arena, "kvm", [128, 512], F32) for _ in range(2)]
                Tkvm = [T(), T()]
                sqk = sb(arena, "msq", [128, 256], F32)
                ss = sb(arena, "mss", [128, 4], F32)
                Tsqk, Tss = T(), T()
                if not sample:
                    mt = sb(arena, "mt", [128, 2, D], F32)
                    Tmt = T()
                    dma(mt[:, :, :], memp[sl_].rearrange("(b p) d -> p b d", p=128), [], [Tmt], q="sync", key="xin")
                    for c in range(8):
                        b = ps()
                        for tb in range(2):
                            tr(pb[b][:, tb * 128:(tb + 1) * 128], mt[:, tb, c * 128:(c + 1) * 128], id_f, [Tmt, Tcst], [Tpb[b]])
                        cp(h[:, c, 0:256], pb[b][:, 0:256], [Tpb[b]], [Th[c]], eng=("scalar" if c % 2 else "vector"))
                n = 0
                for l in range(4):
                    if not sample:
                        with ExitStack() as a2:
                            rms_xn(P_NMEM + 8 * l, 256, a2)
                            P.barrier()
                        spec = ("w_mem_kv", l, 0, 8, 0, 512)
                        wv, Tw = w_next(spec)
                    for mb in range(2):
                        ki = n % 2
                        n += 1
                        if not sample:
                            b = ps()
                            for k in range(8):
                                mm(pb[b][:, 0:512], xn[:, k, mb * 128:(mb + 1) * 128], wv[:, k, :], k == 0, k == 7, [Tw, Txn[k]], [Tpb[b]])
                            cp(kvm[ki][:, :], pb[b][:, :], [Tpb[b]], [Tkvm[ki]], eng="scalar")
                            k3 = kvm[ki][:, 0:256].rearrange("p (h d) -> p h d", d=64)
                            tt(sqk[:, :], kvm[ki][:, 0:256], kvm[ki][:, 0:256], ALU.mult, [Tkvm[ki]], [Tsqk])
                            P.op("vector", lambda e: e.tensor_reduce(out=ss[:, :], in_=sqk[:, :].rearrange("p (h d) -> p h d", d=64),
                                                                     axis=AX.X, op=ALU.add), reads=[Tsqk], writes=[Tss])
                            act(ss[:, :], ss[:, :], AF.Sqrt, [Tss], [Tss], scale=1.0 / 64, bias=EPS)
                            recip(ss[:, :], ss[:, :], [Tss], [Tss])
                            tt(k3, k3, ss[:, :].rearrange("p (h o) -> p h o", o=1).broadcast_to([128, 4, 64]), ALU.mult, [Tkvm[ki], Tss], [Tkvm[ki]])
                            tt(k3, k3, prm[:, P_MGK + 64 * l:P_MGK + 64 * (l + 1)].rearrange("p (o d) -> p o d", o=1).broadcast_to([128, 4, 64]),
                               ALU.mult, [Tkvm[ki], Tprm], [Tkvm[ki]])
                            dma(pmk[l, sl_, mb * 128:(mb + 1) * 128, :], kvm[ki][:, 0:256], [Tkvm[ki]], [], q="gpsimd")
                            dma(pmv[l, sl_, mb * 128:(mb + 1) * 128, :], kvm[ki][:, 256:512], [Tkvm[ki]], [], q="gpsimd")
                        else:
                            dma(kvm[ki][:, 0:256], cmk[l, sl_, mb * 128:(mb + 1) * 128, :], [], [Tkvm[ki]], q="sync", key=f"kvm{ki}")
                            dma(kvm[ki][:, 256:512], cmv[l, sl_, mb * 128:(mb + 1) * 128, :], [], [Tkvm[ki]], q="sync", key=f"kvm{ki}")
                        for cc in range(2):
                            b = ps()
                            tr(pb[b][:, 0:128], kvm[ki][:, cc * 128:(cc + 1) * 128], id_f, [Tkvm[ki], Tcst], [Tpb[b]])
                            cp(MKT[:, l, cc, mb * 128:(mb + 1) * 128], pb[b][:, 0:128], [Tpb[b]], [TMK], eng=("scalar" if cc else "vector"))
                        cp(MVb[:, l, mb, :, :].rearrange("p c (s d) -> p c s d", d=64)[:, :, 0::3, :], kvm[ki][:, 256:512].rearrange("p (c u d) -> p c u d", c=2, u=2), [Tkvm[ki]], [TMV])
                P.barrier()

        def cache_import(si, sl_):
            with ExitStack() as arena:
                ckt = [sb(arena, "ckt", [128, MAIN], F32) for _ in range(2)]
                Tckt = [T(), T()]
                kT = sb(arena, "kTc", [128, 6, 512], BF16)
                TkT = [T() for _ in range(6)]
                lft = sb(arena, "lftc", [128, 4, 12], F32)
                Tlft = T()
                cT = sb(arena, "cTc", [12, 512], F32)
                TcT = T()
                for r in range(0, PAST, 1024):
                    r1 = min(PAST, r + 1024)
                    dma(VS[si][r:r1, :], cv[sl_, r:r1, :], [], [], q="gpsimd", acc=[TVS[si]])
                mset(cTlast[:, 0:1], 0.0, [TcTl], eng="vector")
                for g0 in range(0, PAST, 512):
                    n = min(512, PAST - g0)
                    nb = n // 128
                    dma(lft[:, 0:nb, :], clf[sl_, g0:g0 + n, :].rearrange("(b p) h -> p b h", p=128), [], [Tlft], q="sync", key="lftc")
                    for tb in range(nb):
                        ki = tb % 2
                        dma(ckt[ki][:, :], ck[sl_, g0 + tb * 128:g0 + (tb + 1) * 128, :], [], [Tckt[ki]], q="sync", key=f"ckt{ki}")
                        for cc in range(6):
                            b = ps()
                            tr(pb[b][:, 0:128], ckt[ki][:, cc * 128:(cc + 1) * 128], id_f, [Tckt[ki], Tcst], [Tpb[b]])
                            cp(kT[:, cc, tb * 128:(tb + 1) * 128], pb[b][:, 0:128], [Tpb[b]], [TkT[cc]], eng=("scalar" if cc % 2 else "vector"))
                        prev = cTlast[:, 0:1] if tb == 0 else cT[:, tb * 128 - 1:tb * 128]
                        c_block(lft[:, tb, :], Tlft, 128, cT[:, tb * 128:(tb + 1) * 128], TcT, prev)
                    cp(cTlast[:, 0:1], cT[:, n - 1:n], [TcT], [TcTl])
                    for cc in range(6):
                        for hh in range(2):
                            dma(KA[si][2 * cc + hh, 0:64, g0:g0 + n], kT[hh * 64:(hh + 1) * 64, cc, 0:n], [TkT[cc]], [], q="gpsimd", acc=[TKA[si]])
                    with ExitStack() as a2:
                        split_and_store(si, cT[:, 0:n], TcT, n, g0, a2, qa=None)
                        P.barrier()
                P.barrier()

        try:
            dma(cst[:, :], consts_d, [], [Tcst], q="sync", key="cst")
            dma(prm[:, :], prm_d, [], [Tprm], q="sync", key="prm")
            for l in range(4):
                cast_weight("w_mem_kv", l)
            for l in range(4):
                if l < 2:
                    cast_weight("w_in_a", l)
                else:
                    cast_weight("w_in_b", l - 2)
                cast_weight("w_out", l)
                cast_weight("w_ffn_up", l)
                cast_weight("w_ffn_down", l)
                if l == 1:
                    cast_weight("w_kv", 0)
            cp(cb[:, 0, :], cst[:, C_ONES:C_ONES + 128], [Tcst], [Tcb])
            cp(cb[:, 1, :], cst[:, C_BD:C_BD + 128], [Tcst], [Tcb])
            cp(cb[:, 2, :], cst[:, C_ID:C_ID + 128], [Tcst], [Tcb])
            cp(cb[:, 3, :], cst[:, C_TRI:C_TRI + 128], [Tcst], [Tcb])
            mset(drv[:, :], 1.0, [Tdrv], eng="vector")
            tt(drv[:, 6:12], prm[:, P_LB:P_LB + 6], prm[:, P_LB + 6:P_LB + 12], ALU.subtract, [Tprm], [Tdrv])
            act(drv[:, 6:12], drv[:, 6:12], AF.Sigmoid, [Tdrv], [Tdrv])
            tsc(drv[:, 20:32], drv[:, 0:12], -1.0, ALU.mult, [Tdrv], [Tdrv])
            tsc(drv[:, 12:14], prm[:, P_FGQ:P_FGQ + 2], 0.125, ALU.mult, [Tprm], [Tdrv])
            tsc(drv[:, 16:20], prm[:, P_MGQ:P_MGQ + 4], 0.125, ALU.mult, [Tprm], [Tdrv])

            NT = SEQ // 512
            for s in range(NS):
                wplan.extend([("w_mem_kv", l, 0, 8, 0, 512) for l in range(4)])
                for _ in range(NT):
                    wplan.extend(plan_tile())
            for s in range(NSS):
                wplan.extend(plan_tile())

            chk(0)
            for s in range(NS):
                mem_prologue(s, False)
                chk(1)
                for l in range(2):
                    mset(S32[l][:], 0.0, TS32[l], eng="vector")
                    mset(Sbf[l][:], 0.0, TSbf[l], eng="gpsimd")
                mset(cTlast[:, 0:1], 0.0, [TcTl], eng="vector")
                for ti in range(NT):
                    run_tile(s, s, ti * 512, 512, xp, yp, pk, pv, plf, ti % 2, False)
                for l in range(2):
                    dma(pst[l][s].rearrange("h k v -> k h v"), S32[l][:, :, :], TS32[l], [], q="gpsimd")
                P.barrier()
            for s in range(NSS):
                si = NS + s
                mem_prologue(s, True)
                for l in range(2):
                    dma(S32[l][:, :, :], st_in[l][s].rearrange("h k v -> k h v"), [], TS32[l], q="sync", key="stin")
                    for hh in range(6):
                        cp(Sbf[l][:, hh, :], S32[l][:, hh, :], [TS32[l][hh]], [TSbf[l][hh]], eng="gpsimd")
                cache_import(si, s)
                run_tile(si, s, PAST, TS, xs, ys, sk, sv, slf, 0, True)
                for l in range(2):
                    dma(sst[l][s].rearrange("h k v -> k h v"), S32[l][:, :, :], TS32[l], [], q="gpsimd")
                P.barrier()
        except _Stop:
            wstate["used"] = len(wplan)
        assert P.dead or wstate["used"] == len(wplan), (wstate, len(wplan))
        P.emit()
    return nc, P


def make_consts():
    c = np.zeros((128, CW), np.float32)
    c[:, C_ONES:C_ONES + 128] = 1.0
    c[:64, C_BD:C_BD + 64] = 1.0
    c[64:, C_BD + 64:C_BD + 128] = 1.0
    c[:, C_ID:C_ID + 128] = np.eye(128, dtype=np.float32)
    c[:, C_TRI:C_TRI + 128] = np.triu(np.ones((128, 128), np.float32))
    sw = np.zeros((128, 128), np.float32)
    for k in range(128):
        sw[k, (k + 64) % 128] = 1.0
    c[:, C_SWAP:C_SWAP + 128] = sw
    r = np.ones(512, np.float32)
    r[0::64] = 0.0
    c[:, C_RESET:C_RESET + 512] = r[None, :]
    return c


def make_params(norm_mix, norm_ffn, norm_mem, norm_kv, lb_logits, hg_gnorm, fox_gq, mem_gq, fox_gk, mem_gk, b_f):
    p = np.zeros((128, PW), np.float32)

    def fm(v):
        v = np.asarray(v, np.float32).reshape(-1, 8, 128)
        return v.transpose(2, 0, 1).reshape(128, -1)
    p[:, P_NMIX:P_NMIX + 32] = fm(norm_mix)
    p[:, P_NFFN:P_NFFN + 32] = fm(norm_ffn)
    p[:, P_NMEM:P_NMEM + 32] = fm(norm_mem)
    p[:, P_NKV:P_NKV + 8] = fm(np.asarray(norm_kv)[None])
    p[:, P_LB:P_LB + 12] = np.asarray(lb_logits, np.float32).reshape(2, 6, 128).transpose(2, 0, 1).reshape(128, 12)
    p[:, P_GN:P_GN + 2] = np.asarray(hg_gnorm, np.float32).T
    p[:, P_FGQ:P_FGQ + 2] = np.tile(np.asarray(fox_gq, np.float32).T, (2, 1))
    p[:, P_MGQ:P_MGQ + 4] = np.tile(np.asarray(mem_gq, np.float32).T, (2, 1))
    p[:, P_FGK:P_FGK + 64] = np.asarray(fox_gk, np.float32)[None, :]
    p[:, P_MGK:P_MGK + 256] = np.asarray(mem_gk, np.float32).reshape(1, 256)
    p[:, P_BF:P_BF + 12] = np.asarray(b_f, np.float32)[None, :]
    return p


_CACHE = {}


def run(inputs, n_cores, NS, SEQ, NSS, PAST):
    key = (NS, SEQ, NSS, PAST)
    if key not in _CACHE:
        _CACHE[key] = build(NS, SEQ, NSS, PAST)[0]
    nc = _CACHE[key]
    f = lambda a: np.ascontiguousarray(np.asarray(a, np.float32))
    consts = make_consts()
    prm = make_params(inputs["norm_mix"], inputs["norm_ffn"], inputs["norm_mem"], inputs["norm_kv"], inputs["lb_logits"],
                      inputs["hg_gnorm"], inputs["fox_gq"], inputs["mem_gq"], inputs["fox_gk"], inputs["mem_gk"], inputs["b_f"])
    shared = {"consts": consts, "prm": prm}
    for k in ("w_in_a", "w_in_b", "w_mem_kv", "w_out", "w_ffn_up", "w_ffn_down"):
        shared[k] = f(inputs[k])
    shared["w_kv"] = f(inputs["w_kv"])[None]
    in_maps = []
    for c in range(n_cores):
        ps_ = slice(c * NS, (c + 1) * NS)
        ss_ = slice(c * NSS, (c + 1) * NSS)
        m = dict(shared)
        m["xp"] = f(inputs["x_prompt"][ps_])
        m["xs"] = f(inputs["x_sample"][ss_])
        m["memp"] = f(inputs["mem_prompt"][ps_])
        m["st0"] = f(inputs["state_hgrn_0"][ss_])
        m["st1"] = f(inputs["state_hgrn_1"][ss_])
        m["ck"] = f(inputs["cache_fox_k"][ss_]).reshape(NSS, PAST, MAIN)
        m["cv"] = f(inputs["cache_fox_v"][ss_]).reshape(NSS, PAST, MAIN)
        m["clf"] = f(inputs["cache_fox_logf"][ss_])
        m["cmk"] = f(inputs["cache_mem_k"][:, ss_]).reshape(4, NSS, N_MEM, 256)
        m["cmv"] = f(inputs["cache_mem_v"][:, ss_]).reshape(4, NSS, N_MEM, 256)
        in_maps.append(m)
    res = run_bass_kernel_spmd(nc, in_maps, core_ids=list(range(n_cores)))
    R = res.results
    cat = lambda k, ax=0: np.concatenate([np.asarray(r[k]) for r in R], axis=ax)
    B = n_cores * NS
    BS = n_cores * NSS
    outs = (
        cat("yp"), cat("ys"), cat("pst0"), cat("pst1"),
        cat("pk").reshape(B, SEQ, 12, 64), cat("pv").reshape(B, SEQ, 12, 64), cat("plf"),
        cat("pmk", 1).reshape(4, B, N_MEM, 4, 64), cat("pmv", 1).reshape(4, B, N_MEM, 4, 64),
        cat("sst0"), cat("sst1"),
        cat("sk").reshape(BS, TS, 12, 64), cat("sv").reshape(BS, TS, 12, 64), cat("slf"),
    )
    return tuple(np.ascontiguousarray(o, dtype=np.float32) for o in outs)


def kernel(**inputs):
    return run(inputs, 8, 2, 4096, 2, 4096)
```

```python
import numpy as np
from contextlib import ExitStack
import concourse.bass as bass
import concourse.mybir as mybir
from concourse.bass_utils import run_bass_kernel_spmd

F32 = mybir.dt.float32
BF16 = mybir.dt.bfloat16
AF = mybir.ActivationFunctionType
ALU = mybir.AluOpType
AX = mybir.AxisListType

D = 1024
DFF = 2816
MAIN = 768
A_IN = 3328
KV_OUT = 1548
EPS = 1e-6
K_MAX = 0.999999
N_MEM = 256
TS = 16
NW = 4
WELEM = 4096

C_ONES, C_BD, C_ID, C_TRI, C_SWAP, C_RESET, CW = 0, 128, 256, 384, 512, 640, 1152
P_NMIX, P_NFFN, P_NMEM, P_NKV, P_LB, P_GN, P_FGQ, P_MGQ, P_FGK, P_MGK, P_BF, PW = 0, 32, 64, 96, 104, 116, 118, 120, 124, 188, 444, 456


class T:
    __slots__ = ("name", "w", "r", "wa", "excl")

    def __init__(self, name="", excl=False):
        self.name = name
        self.w = None
        self.r = []
        self.wa = []
        self.excl = excl


class Prog:
    ENG = ("tensor", "vector", "scalar", "gpsimd", "sync")

    def __init__(self, nc):
        self.nc = nc
        self.ops = {e: [] for e in self.ENG}
        self.cnt = {}
        self.waited = {e: {} for e in self.ENG}
        self.semkeys = []
        for e in self.ENG:
            self._newsem("E_" + e)
        self.n_ops = 0
        self.dead = False

    def _newsem(self, key):
        self.semkeys.append(key)
        self.cnt[key] = 0

    def dma_sem(self, key):
        k = "D_" + key
        if k not in self.cnt:
            self._newsem(k)
        return k

    def op(self, eng, fn, reads=(), writes=(), dma=None, writes_acc=()):
        waits = {}

        def need(dep, same_ok):
            if dep is None:
                return
            key, val, deng = dep
            if deng == eng and not same_ok and not key.startswith("D_"):
                return
            if waits.get(key, 0) < val:
                waits[key] = val

        if self.dead:
            return ("E_" + eng, self.cnt["E_" + eng], eng)
        for t in reads:
            need(t.w, True)
            for r in t.wa:
                need(r, True)
            if t.excl:
                for r in t.r:
                    need(r, False)
        for t in writes:
            need(t.w, False)
            for r in t.wa:
                need(r, False)
            for r in t.r:
                need(r, False)
        for t in writes_acc:
            for r in t.r:
                need(r, False)
        wl = []
        wd = self.waited[eng]
        for key, val in waits.items():
            if wd.get(key, 0) < val:
                wd[key] = val
                wl.append((key, val))
        if dma is None:
            key = "E_" + eng
            self.cnt[key] += 1
            inc = 1
        else:
            key = dma
            self.cnt[key] += 16
            inc = 16
        me = (key, self.cnt[key], eng)
        self.ops[eng].append((wl, fn, key, inc))
        for t in writes:
            t.w = me
            t.r = []
            t.wa = []
        for t in writes_acc:
            t.wa.append(me)
        for t in reads:
            t.r.append(me)
        self.n_ops += 1
        return me

    def barrier(self, force=False):
        if self.dead and not force:
            return
        snap = dict(self.cnt)
        for e in self.ENG:
            wl = []
            for key, val in snap.items():
                if val == 0 or key == "E_" + e:
                    continue
                if self.waited[e].get(key, 0) < val:
                    self.waited[e][key] = val
                    wl.append((key, val))
            if wl:
                self.ops[e].append((wl, None, None, 0))

    def emit(self):
        nc = self.nc
        self.barrier(force=True)
        with ExitStack() as st:
            sems = {}
            for k in self.semkeys:
                sems[k] = st.enter_context(nc.semaphore(k))
            block = st.enter_context(nc.Block())
            for e in self.ENG:
                ops = self.ops[e]
                if not ops:
                    continue

                def body(eng, ops=ops):
                    for wl, fn, key, inc in ops:
                        for wk, wv in wl:
                            eng.wait_ge(sems[wk], wv)
                        if fn is not None:
                            fn(eng).then_inc(sems[key], inc)
                getattr(block, e)(body)


class _Stop(Exception):
    pass


DBG_STOP = [None]


_PROG = [None]


def chk(n):
    if DBG_STOP[0] == n:
        _PROG[0].dead = True


def build(NS, SEQ, NSS, PAST):
    nc = bass.Bass("TRN2", target_bir_lowering=False)

    def dram(name, shape, dtype, kind):
        return nc.dram_tensor(name, list(shape), dtype, kind=kind).ap()

    I, O, S = "ExternalInput", "ExternalOutput", "Internal"
    LS = PAST + TS
    xp = dram("xp", [NS, SEQ, D], F32, I)
    xs = dram("xs", [NSS, TS, D], F32, I)
    memp = dram("memp", [NS, N_MEM, D], F32, I)
    st_in = [dram("st0", [NSS, 6, 128, 128], F32, I), dram("st1", [NSS, 6, 128, 128], F32, I)]
    ck = dram("ck", [NSS, PAST, MAIN], F32, I)
    cv = dram("cv", [NSS, PAST, MAIN], F32, I)
    clf = dram("clf", [NSS, PAST, 12], F32, I)
    cmk = dram("cmk", [4, NSS, N_MEM, 256], F32, I)
    cmv = dram("cmv", [4, NSS, N_MEM, 256], F32, I)
    consts_d = dram("consts", [128, CW], F32, I)
    prm_d = dram("prm", [128, PW], F32, I)
    wshapes = {"w_in_a": [2, D, A_IN], "w_in_b": [2, D, D], "w_kv": [1, D, KV_OUT], "w_mem_kv": [4, D, 512],
               "w_out": [4, D, D], "w_ffn_up": [4, D, 2 * DFF], "w_ffn_down": [4, DFF, D]}
    wf = {k: dram(k, v, F32, I) for k, v in wshapes.items()}
    wb = {k: dram(k + "_b", v, BF16, S) for k, v in wshapes.items()}
    yp = dram("yp", [NS, SEQ, D], F32, O)
    ys = dram("ys", [NSS, TS, D], F32, O)
    pst = [dram("pst0", [NS, 6, 128, 128], F32, O), dram("pst1", [NS, 6, 128, 128], F32, O)]
    pk = dram("pk", [NS, SEQ, MAIN], F32, O)
    pv = dram("pv", [NS, SEQ, MAIN], F32, O)
    plf = dram("plf", [NS, SEQ, 12], F32, O)
    pmk = dram("pmk", [4, NS, N_MEM, 256], F32, O)
    pmv = dram("pmv", [4, NS, N_MEM, 256], F32, O)
    sst = [dram("sst0", [NSS, 6, 128, 128], F32, O), dram("sst1", [NSS, 6, 128, 128], F32, O)]
    sk = dram("sk", [NSS, TS, MAIN], F32, O)
    sv = dram("sv", [NSS, TS, MAIN], F32, O)
    slf = dram("slf", [NSS, TS, 12], F32, O)
    KA = [dram(f"KA{i}", [12, 70, SEQ if i < NS else LS], BF16, S) for i in range(NS + NSS)]
    VS = [dram(f"VS{i}", [SEQ if i < NS else LS, MAIN], BF16, S) for i in range(NS + NSS)]
    QA = [dram(f"QA{i}", [12, 70, 512], BF16, S) for i in range(2)]

    P = Prog(nc)
    _PROG[0] = P
    uid = [0]

    with ExitStack() as top:
        def sb(st, name, shape, dtype):
            uid[0] += 1
            return st.enter_context(nc.sbuf_tensor(f"{name}_{uid[0]}", list(shape), dtype))

        cst = sb(top, "cst", [128, CW], F32)
        prm = sb(top, "prm", [128, PW], F32)
        cb = sb(top, "cb", [128, 5, 128], BF16)
        drv = sb(top, "drv", [128, 32], F32)
        h = sb(top, "h", [128, 8, 512], F32)
        xn = sb(top, "xn", [128, 8, 512], BF16)
        mix = sb(top, "mix", [128, 8, 512], BF16)
        wbuf = [sb(top, f"wbuf{i}", [128, WELEM], BF16) for i in range(NW)]
        S32 = [sb(top, f"S32_{l}", [128, 6, 128], F32) for l in range(2)]
        Sbf = [sb(top, f"Sbf_{l}", [128, 6, 128], BF16) for l in range(2)]
        MKT = sb(top, "MKT", [128, 4, 2, 256], BF16)
        MVb = sb(top, "MVb", [128, 4, 2, 2, 256], BF16)
        cTlast = sb(top, "cTlast", [12, 2], F32)
        pb = [top.enter_context(nc.psum_tensor(f"pb{i}", [128, 512], F32)) for i in range(8)]
        Tpb = [T(f"pb{i}", excl=True) for i in range(8)]
        Tst = [T(f"st{i}") for i in range(8)]
        Tcst, Tprm, Tcb, Tdrv = T("cst"), T("prm"), T("cb"), T("drv")
        Th = [T(f"h{c}") for c in range(8)]
        Txn = [T(f"xn{c}") for c in range(8)]
        Tmix = [T(f"mix{c}") for c in range(8)]
        Twbuf = [T(f"wbuf{i}") for i in range(NW)]
        TS32 = [[T() for _ in range(6)] for _ in range(2)]
        TSbf = [[T() for _ in range(6)] for _ in range(2)]
        TMK, TMV, TcTl = T("MKT"), T("MVb"), T("cTlast")
        psn = [0]

        def ps():
            psn[0] = (psn[0] + 1) % 8
            return psn[0]

        ones_b, bd_b, id_b, tri_b = cb[:, 0, :], cb[:, 1, :], cb[:, 2, :], cb[:, 3, :]
        ones_f = cst[:, C_ONES:C_ONES + 128]
        id_f = cst[:, C_ID:C_ID + 128]
        tri_f = cst[:, C_TRI:C_TRI + 128]
        swap_f = cst[:, C_SWAP:C_SWAP + 128]
        reset_f = cst[:, C_RESET:C_RESET + 512]

        def pcol(c):
            return prm[:, c:c + 1]

        def mm(out, lhsT, rhs, start, stop, reads, writes):
            P.op("tensor", lambda e: e.matmul(out, lhsT, rhs, start=start, stop=stop), reads=reads, writes=writes)

        def tr(out, in_, ident, reads, writes):
            P.op("tensor", lambda e: e.transpose(out, in_, ident), reads=reads, writes=writes)

        def act(out, in_, func, reads, writes, scale=1.0, bias=None):
            if bias is None:
                P.op("scalar", lambda e: e.activation(out=out, in_=in_, func=func, scale=scale), reads=reads, writes=writes)
            else:
                P.op("scalar", lambda e: e.activation(out=out, in_=in_, func=func, scale=scale, bias=bias), reads=reads, writes=writes)

        def tt(out, in0, in1, op, reads, writes, eng="vector"):
            P.op(eng, lambda e: e.tensor_tensor(out=out, in0=in0, in1=in1, op=op), reads=reads, writes=writes)

        def tsc(out, in0, s1, op0, reads, writes, s2=None, op1=None, eng="vector"):
            if op1 is None:
                P.op(eng, lambda e: e.tensor_scalar(out=out, in0=in0, scalar1=s1, scalar2=None, op0=op0), reads=reads, writes=writes)
            else:
                P.op(eng, lambda e: e.tensor_scalar(out=out, in0=in0, scalar1=s1, scalar2=s2, op0=op0, op1=op1), reads=reads, writes=writes)

        def stt(out, in0, scalar, in1, op0, op1, reads, writes):
            P.op("vector", lambda e: e.scalar_tensor_tensor(out=out, in0=in0, scalar=scalar, in1=in1, op0=op0, op1=op1),
                 reads=reads, writes=writes)

        def cp(out, in_, reads, writes, eng="vector"):
            if eng == "scalar":
                P.op("scalar", lambda e: e.copy(out=out, in_=in_), reads=reads, writes=writes)
            else:
                P.op(eng, lambda e: e.tensor_copy(out=out, in_=in_), reads=reads, writes=writes)

        def recip(out, in_, reads, writes):
            P.op("vector", lambda e: e.reciprocal(out=out, in_=in_), reads=reads, writes=writes)

        def mset(ap, val, writes, eng="gpsimd"):
            P.op(eng, lambda e: e.memset(ap, val), writes=writes)

        dman = [0]

        def dma(out, in_, reads, writes, q="sync", key=None, acc=()):
            if key is None:
                dman[0] += 1
                key = f"g{dman[0] % 48}_{q}"
            return P.op(q, lambda e: e.dma_start(out=out, in_=in_), reads=reads, writes=writes, dma=P.dma_sem(key),
                        writes_acc=acc)

        Twb = {}
        ncast = [0]

        def cast_weight(name, l):
            shp = wshapes[name]
            rows, cols = shp[1], shp[2]
            step = max(128, (2 * 1024 * 1024 // cols) // 128 * 128)
            ts_ = []
            r = 0
            while r < rows:
                r1 = min(rows, r + step)
                t = T(f"{name}{l}_{r}")
                ncast[0] += 1
                dma(wb[name][l, r:r1, :], wf[name][l, r:r1, :], [], [t], q="gpsimd", key=f"wcast{ncast[0] % 16}")
                ts_.append(t)
                r = r1
            Twb[(name, l)] = ts_

        wplan = []
        wstate = {"issued": 0, "used": 0}

        def w_issue():
            while wstate["issued"] < len(wplan) and wstate["issued"] < wstate["used"] + NW:
                i = wstate["issued"]
                name, l, row0, nk, c0, ncols = wplan[i]
                slot = i % NW
                dst = wbuf[slot][:, 0:nk * ncols].rearrange("p (k c) -> p k c", c=ncols)
                src = wb[name][l, row0:row0 + nk * 128, c0:c0 + ncols].rearrange("(k p) c -> p k c", p=128)
                dma(dst, src, Twb[(name, l)], [Twbuf[slot]], q="sync", key=f"w{slot}")
                wstate["issued"] += 1

        def w_next(spec):
            i = wstate["used"]
            assert wplan[i] == spec, (i, wplan[i], spec)
            w_issue()
            wstate["used"] += 1
            slot = i % NW
            nk, ncols = spec[3], spec[5]
            return wbuf[slot][:, 0:nk * ncols].rearrange("p (k c) -> p k c", c=ncols), Twbuf[slot]

        def plan_cols(name, l, total, step=512, nk=8):
            return [(name, l, 0, nk, c, min(step, total - c)) for c in range(0, total, step)]

        def plan_down(l):
            out = []
            for jp in range(4):
                for kh in range(2):
                    out.append(("w_ffn_down", l, kh * 11 * 128, 11, jp * 256, 256))
            return out

        def plan_tile():
            pl = []
            for l in range(4):
                if l < 2:
                    pl += plan_cols("w_in_a", l, A_IN)
                else:
                    pl += plan_cols("w_in_b", l - 2, D)
                pl += plan_cols("w_out", l, D)
                pl += plan_cols("w_ffn_up", l, 2 * DFF)
                pl += plan_down(l)
                if l == 1:
                    pl += plan_cols("w_kv", 0, KV_OUT)
            return pl

        nsq = [sb(top, "nsq", [128, 512], BF16) for _ in range(2)]
        Tnsq = [T(), T()]
        nrs = sb(top, "nrs", [128, 512], F32)
        Tnrs = T()
        prenorm = {"done": False}

        def rms_xn(gcol0, T_, arena=None):
            if prenorm["done"]:
                prenorm["done"] = False
                return
            sqr, Tsq, rs, Trs = nsq, Tnsq, nrs, Tnrs
            b = ps()
            for c in range(8):
                i = c % 2
                act(sqr[i][:, 0:T_], h[:, c, 0:T_], AF.Square, [Th[c]], [Tsq[i]])
                mm(pb[b][:, 0:T_], ones_b, sqr[i][:, 0:T_], c == 0, c == 7, [Tsq[i], Tcb], [Tpb[b]])
            act(rs[:, 0:T_], pb[b][:, 0:T_], AF.Ln, [Tpb[b]], [Trs], scale=1.0 / D, bias=EPS)
            act(rs[:, 0:T_], rs[:, 0:T_], AF.Exp, [Trs], [Trs], scale=-0.5)
            for c in range(8):
                stt(xn[:, c, 0:T_], h[:, c, 0:T_], pcol(gcol0 + c), rs[:, 0:T_], ALU.mult, ALU.mult,
                    [Th[c], Trs, Tprm], [Txn[c]])

        def rstd_bufs(arena, n=2):
            return {"i": 0, "b": [(sb(arena, "hsq", [128, 512], BF16), T(), sb(arena, "hrs", [128, 512], F32), T()) for _ in range(n)]}

        def head_rstd(src, Tsrc, n, onesap, T_, rb, bank=None):
            rb["i"] = (rb["i"] + 1) % len(rb["b"])
            sq, Tsq, rs, Trs = rb["b"][rb["i"]]
            act(sq[:, 0:T_], src, AF.Square, [Tsrc], [Tsq])
            b = ps() if bank is None else bank
            mm(pb[b][:, 0:T_], onesap, sq[:, 0:T_], True, True, [Tsq, Tcb], [Tpb[b]])
            act(rs[:, 0:T_], pb[b][:, 0:T_], AF.Ln, [Tpb[b]], [Trs], scale=1.0 / n, bias=EPS)
            act(rs[:, 0:T_], rs[:, 0:T_], AF.Exp, [Trs], [Trs], scale=-0.5)
            return rs, Trs

        def attn_pair(lhsK, rhsQ, lhsV, blocks, nq, out_ap, Tout, Kreads, Qreads, Vreads, arena_bufs):
            Pt, TPt, R, TR, comb, Tcomb = arena_bufs
            Ob = []
            for hh in range(2):
                ob = ps()
                while ob in Ob:
                    ob = ps()
                Ob.append(ob)
            nb = len(blocks)
            items = [(hh, bi) + tuple(blocks[bi]) for hh in range(2) for bi in range(nb)]
            DEPTH = 2
            NP_ = len(Pt)

            def issue_s(idx):
                hh, bi, j, nk, q0, diag = items[idx]
                w = nq - q0
                sbk = ps()
                while sbk in Ob:
                    sbk = ps()
                mm(pb[sbk][0:nk, 0:w], lhsK(hh, j, nk), rhsQ(hh, q0), True, True, Kreads + Qreads, [Tpb[sbk]])
                pi = idx % NP_
                act(Pt[pi][0:nk, 0:w], pb[sbk][0:nk, 0:w], AF.Exp, [Tpb[sbk]], [TPt[pi]])
                if diag:
                    tt(Pt[pi][0:nk, 0:nk], Pt[pi][0:nk, 0:nk], tri_b[0:nk, 0:nk], ALU.mult, [TPt[pi], Tcb], [TPt[pi]],
                       eng="gpsimd")

            def issue_pv(idx):
                hh, bi, j, nk, q0, diag = items[idx]
                w = nq - q0
                pi = idx % NP_
                mm(pb[Ob[hh]][:, q0:nq], lhsV(hh, j, nk), Pt[pi][0:nk, 0:w], bi == 0, bi == nb - 1,
                   [TPt[pi]] + Vreads, [Tpb[Ob[hh]]])

            for idx in range(len(items)):
                issue_s(idx)
                if idx >= DEPTH:
                    issue_pv(idx - DEPTH)
            for idx in range(max(0, len(items) - DEPTH), len(items)):
                issue_pv(idx)
            oa, ob_ = Ob
            act(R[0:64, 0:nq], pb[ob_][0:64, 0:nq], AF.Ln, [Tpb[ob_]], [TR])
            act(R[64:128, 0:nq], pb[oa][64:128, 0:nq], AF.Ln, [Tpb[oa]], [TR])
            act(R[:, 0:nq], R[:, 0:nq], AF.Exp, [TR], [TR], scale=-1.0)
            pr = ps()
            while pr in Ob:
                pr = ps()
            mm(pb[pr][:, 0:nq], swap_f, R[:, 0:nq], True, True, [TR, Tcst], [Tpb[pr]])
            cp(comb[0:64, 0:nq], pb[oa][0:64, 0:nq], [Tpb[oa]], [Tcomb], eng="scalar")
            cp(comb[64:128, 0:nq], pb[ob_][64:128, 0:nq], [Tpb[ob_]], [Tcomb], eng="scalar")
            tt(out_ap, comb[:, 0:nq], pb[pr][:, 0:nq], ALU.mult, [Tcomb, Tpb[pr]], [Tout])

        def attn_bufs(arena):
            Pt = [sb(arena, "Pt", [128, 512], BF16) for _ in range(4)]
            R = sb(arena, "R", [128, 512], F32)
            comb = sb(arena, "comb", [128, 512], F32)
            return Pt, [T(), T(), T(), T()], R, T(), comb, T()

        def mem_attend(l, qm, Tqm, T_, arena):
            bufs = attn_bufs(arena)
            Qm = sb(arena, "Qm", [128, 2, 512], BF16)
            TQm = [T(), T()]
            rb = rstd_bufs(arena)
            for cc in range(2):
                rs, Trs = head_rstd(qm[:, cc, 0:T_], Tqm[cc], 64, bd_b, T_, rb)
                stt(Qm[:, cc, 0:T_], qm[:, cc, 0:T_], drv[:, 16 + l:17 + l], rs[:, 0:T_], ALU.mult, ALU.mult,
                    [Tqm[cc], Trs, Tdrv], [TQm[cc]])
            blocks = [(0, 128, 0, False), (1, 128, 0, False)]
            for cc in range(2):
                def lhsK(hh, j, nk, cc=cc):
                    return MKT[hh * 64:(hh + 1) * 64, l, cc, j * 128:(j + 1) * 128]

                def rhsQ(hh, q0, cc=cc):
                    return Qm[hh * 64:(hh + 1) * 64, cc, 0:T_]

                def lhsV(hh, j, nk, cc=cc):
                    return MVb[:, l, j, cc, 0:128] if hh == 0 else MVb[:, l, j, cc, 128:256]
                attn_pair(lhsK, rhsQ, lhsV, blocks, T_, mix[:, 6 + cc, 0:T_], Tmix[6 + cc], [TMK], [TQm[cc]], [TMV], bufs)

        def out_proj_ffn(l, T_, next_gcol=None):
            for (name, ll, r0, nk, c0, ncols) in plan_cols("w_out", l, D):
                wv, Tw = w_next((name, ll, r0, nk, c0, ncols))
                for jj in range(ncols // 128):
                    j = c0 // 128 + jj
                    b = ps()
                    for k in range(8):
                        mm(pb[b][:, 0:T_], wv[:, k, jj * 128:(jj + 1) * 128], mix[:, k, 0:T_], k == 0, k == 7,
                           [Tw, Tmix[k]], [Tpb[b]])
                    tt(h[:, j, 0:T_], h[:, j, 0:T_], pb[b][:, 0:T_], ALU.add, [Th[j], Tpb[b]], [Th[j]])
            with ExitStack() as arena:
                rms_xn(P_NFFN + 8 * l, T_, arena)
                hid = sb(arena, "hid", [128, 22, 512], BF16)
                Thid = [T() for _ in range(22)]
                for (name, ll, r0, nk, c0, ncols) in plan_cols("w_ffn_up", l, 2 * DFF):
                    wv, Tw = w_next((name, ll, r0, nk, c0, ncols))
                    for jj in range(ncols // 128):
                        j = c0 // 128 + jj
                        b = ps()
                        for k in range(8):
                            mm(pb[b][:, 0:T_], wv[:, k, jj * 128:(jj + 1) * 128], xn[:, k, 0:T_], k == 0, k == 7,
                               [Tw, Txn[k]], [Tpb[b]])
                        if j < 22:
                            act(hid[:, j, 0:T_], pb[b][:, 0:T_], AF.Silu, [Tpb[b]], [Thid[j]])
                        else:
                            tt(hid[:, j - 22, 0:T_], pb[b][:, 0:T_], hid[:, j - 22, 0:T_], ALU.mult,
                               [Tpb[b], Thid[j - 22]], [Thid[j - 22]])
                for jp in range(4):
                    bs = [ps(), ps()]
                    for kh in range(2):
                        spec = ("w_ffn_down", l, kh * 11 * 128, 11, jp * 256, 256)
                        wv, Tw = w_next(spec)
                        for jj in range(2):
                            for k in range(11):
                                kk = kh * 11 + k
                                mm(pb[bs[jj]][:, 0:T_], wv[:, k, jj * 128:(jj + 1) * 128], hid[:, kk, 0:T_],
                                   kk == 0, kk == 21, [Tw, Thid[kk]], [Tpb[bs[jj]]])
                    for jj in range(2):
                        j = jp * 2 + jj
                        tt(h[:, j, 0:T_], h[:, j, 0:T_], pb[bs[jj]][:, 0:T_], ALU.add, [Th[j], Tpb[bs[jj]]], [Th[j]])
                if next_gcol is not None:
                    rms_xn(next_gcol, T_)
                    prenorm["done"] = True
                P.barrier()

        def layer_a(l, T_, seq_is_sample):
            CL = min(64, T_)
            NCH = T_ // CL
            G = min(2, NCH)
            GT = G * CL
            NG = NCH // G
            with ExitStack() as arena:
                rms_xn(P_NMIX + 8 * l, T_, arena)
                sqb = sb(arena, "sqb", [128, 6, 512], BF16)
                kb = sb(arena, "kb", [128, 6, 512], BF16)
                bfa = sb(arena, "bfa", [128, 6, 512], F32)
                gate = sb(arena, "gate", [128, 6, 512], BF16)
                qm = sb(arena, "qm", [128, 2, 512], F32)
                Vt = sb(arena, "Vt", [128, 4, MAIN], BF16)
                tmp = [sb(arena, "tmpa", [128, 512], F32) for _ in range(3)]
                Ttmp = [T(), T(), T()]
                Tsq_, Tkb, Tbf, Tgate, Tqm = ([T() for _ in range(6)] for _ in range(5))
                Tqm = [T(), T()]
                TVt = [T() for _ in range(4)]
                tn = [0]

                def nxt():
                    tn[0] = (tn[0] + 1) % 3
                    return tn[0]
                Qp = sb(arena, "Qp", [128, 6, 512], BF16)
                Qt = sb(arena, "Qt", [128, 6, 512], BF16)
                Kt = sb(arena, "Kt", [128, 6, 512], BF16)
                KhT = sb(arena, "KhT", [128, 6, 512], BF16)
                Kh = sb(arena, "Kh", [128, 4, MAIN], BF16)
                dec = sb(arena, "dec", [128, 6, 8], F32)
                em = sb(arena, "em", [128, 6, 8], F32)
                edl = sb(arena, "edl", [128, 6, 8], F32)
                Tem = [T() for _ in range(6)]
                Tedl = [T() for _ in range(6)]
                ATt = sb(arena, "ATt", [128, 6, 4, 128], BF16)
                rs6 = sb(arena, "rs6", [128, 6, 512], F32)
                TQp, TQt, TKt, TKhT, Tdec = ([T() for _ in range(6)] for _ in range(5))
                TKh = [T() for _ in range(4)]
                TAT = [[T() for _ in range(4)] for _ in range(6)]
                mset(ATt[:], 0.0, [t for row in TAT for t in row], eng="gpsimd")
                def hgrn_elem(hh):
                    P.op("vector", lambda e, hh=hh: e.tensor_tensor_scan(out=bfa[:, hh, 0:T_], data0=reset_f[:, 0:T_],
                                                                          data1=bfa[:, hh, 0:T_], initial=0.0,
                                                                          op0=ALU.mult, op1=ALU.add),
                         reads=[Tbf[hh], Tcst], writes=[Tbf[hh]])
                    b3 = bfa[:, hh, 0:T_].rearrange("p (c l) -> p c l", l=CL)
                    mid = b3[:, :, CL // 2 - 1:CL // 2].broadcast_to([128, NCH, CL])
                    act(em[:, hh, 0:NCH], b3[:, :, CL // 2 - 1], AF.Exp, [Tbf[hh]], [Tem[hh]])
                    act(dec[:, hh, 0:NCH], b3[:, :, CL - 1], AF.Exp, [Tbf[hh]], [Tdec[hh]])
                    i1 = nxt()
                    t3 = tmp[i1][:, 0:T_].rearrange("p (c l) -> p c l", l=CL)
                    tt(t3, b3, mid, ALU.subtract, [Tbf[hh]], [Ttmp[i1]])
                    i2 = nxt()
                    act(tmp[i2][:, 0:T_], tmp[i1][:, 0:T_], AF.Exp, [Ttmp[i1]], [Ttmp[i2]])
                    tt(Qt[:, hh, 0:T_], sqb[:, hh, 0:T_], tmp[i2][:, 0:T_], ALU.mult, [Tsq_[hh], Ttmp[i2]], [TQt[hh]])
                    e3 = tmp[i2][:, 0:T_].rearrange("p (c l) -> p c l", l=CL)
                    cp(edl[:, hh, 0:NCH], e3[:, :, CL - 1], [Ttmp[i2]], [Tedl[hh]])
                    act(tmp[i1][:, 0:T_], tmp[i1][:, 0:T_], AF.Exp, [Ttmp[i1]], [Ttmp[i1]], scale=-1.0)
                    tt(Kt[:, hh, 0:T_], kb[:, hh, 0:T_], tmp[i1][:, 0:T_], ALU.mult, [Tkb[hh], Ttmp[i1]], [TKt[hh]])
                    q3 = Qt[:, hh, 0:T_].rearrange("p (c l) -> p c l", l=CL)
                    k3_ = Kt[:, hh, 0:T_].rearrange("p (c l) -> p c l", l=CL)
                    qp3 = Qp[:, hh, 0:T_].rearrange("p (c l) -> p c l", l=CL)
                    kh3 = KhT[:, hh, 0:T_].rearrange("p (c l) -> p c l", l=CL)
                    em_bc = em[:, hh, 0:NCH].rearrange("p (c o) -> p c o", o=1).broadcast_to([128, NCH, CL])
                    edl_bc = edl[:, hh, 0:NCH].rearrange("p (c o) -> p c o", o=1).broadcast_to([128, NCH, CL])
                    tt(qp3, q3, em_bc, ALU.mult, [TQt[hh], Tem[hh]], [TQp[hh]])
                    tt(kh3, k3_, edl_bc, ALU.mult, [TKt[hh], Tedl[hh]], [TKhT[hh]])
                for (name, ll, r0, nk, c0, ncols) in plan_cols("w_in_a", l, A_IN):
                    wv, Tw = w_next((name, ll, r0, nk, c0, ncols))
                    bidx = c0 // 512
                    tok = {3: (0, 512, 0), 4: (0, 256, 512)}.get(bidx)
                    if tok is not None:
                        wc0, wn, vc0 = tok
                        for tb in range(NG):
                            b = ps()
                            for k in range(8):
                                mm(pb[b][0:GT, 0:wn], xn[:, k, tb * GT:(tb + 1) * GT], wv[:, k, wc0:wc0 + wn], k == 0, k == 7,
                                   [Tw, Txn[k]], [Tpb[b]])
                            cp(Vt[0:GT, tb, vc0:vc0 + wn], pb[b][0:GT, 0:wn], [Tpb[b]], [TVt[tb]], eng="scalar")
                    for jj in range(ncols // 128):
                        j = c0 // 128 + jj
                        if 12 <= j < 18:
                            continue
                        b = ps()
                        for k in range(8):
                            mm(pb[b][:, 0:T_], wv[:, k, jj * 128:(jj + 1) * 128], xn[:, k, 0:T_], k == 0, k == 7,
                               [Tw, Txn[k]], [Tpb[b]])
                        if j < 6:
                            act(sqb[:, j, 0:T_], pb[b][:, 0:T_], AF.Silu, [Tpb[b]], [Tsq_[j]])
                        elif j < 12:
                            hh = j - 6
                            i0 = nxt()
                            act(tmp[i0][:, 0:T_], pb[b][:, 0:T_], AF.Exp, [Tpb[b]], [Ttmp[i0]], scale=-1.0)
                            act(tmp[i0][:, 0:T_], tmp[i0][:, 0:T_], AF.Ln, [Ttmp[i0]], [Ttmp[i0]], bias=1.0)
                            act(tmp[i0][:, 0:T_], tmp[i0][:, 0:T_], AF.Exp, [Ttmp[i0]], [Ttmp[i0]], scale=-1.0)
                            tsc(tmp[i0][:, 0:T_], tmp[i0][:, 0:T_], drv[:, 20 + l * 6 + hh:21 + l * 6 + hh], ALU.mult,
                                [Ttmp[i0], Tdrv], [Ttmp[i0]], s2=drv[:, l * 6 + hh:l * 6 + hh + 1], op1=ALU.add)
                            tsc(tmp[i0][:, 0:T_], tmp[i0][:, 0:T_], K_MAX, ALU.min, [Ttmp[i0]], [Ttmp[i0]])
                            cp(kb[:, hh, 0:T_], tmp[i0][:, 0:T_], [Ttmp[i0]], [Tkb[hh]], eng="gpsimd")
                            act(bfa[:, hh, 0:T_], tmp[i0][:, 0:T_], AF.Ln, [Ttmp[i0]], [Tbf[hh]], scale=-1.0, bias=1.0)
                            hgrn_elem(hh)
                        elif j < 24:
                            act(gate[:, j - 18, 0:T_], pb[b][:, 0:T_], AF.Silu, [Tpb[b]], [Tgate[j - 18]])
                        else:
                            cp(qm[:, j - 24, 0:T_], pb[b][:, 0:T_], [Tpb[b]], [Tqm[j - 24]], eng="scalar")
                chk(30)
                with ExitStack() as a2:
                    mem_attend(l, qm, Tqm, T_, a2)
                chk(31)
                chk(32)
                for tb in range(NG):
                    b = ps()
                    pbv = pb[b][:].bitcast(BF16)
                    for hh in range(6):
                        tr(pbv[0:GT, hh * 128:(hh + 1) * 128], KhT[:, hh, tb * GT:(tb + 1) * GT], id_b, [TKhT[hh], Tcb], [Tpb[b]])
                    cp(Kh[0:GT, tb, :], pbv[0:GT, 0:MAIN], [Tpb[b]], [TKh[tb]])
                for hh in range(6):
                    for gi in range(NG):
                        b = ps()
                        r = 0
                        sl = slice(gi * GT, (gi + 1) * GT)
                        mm(pb[b][0:GT, r * 128:r * 128 + GT], Kt[:, hh, sl], Qt[:, hh, sl], True, True,
                           [TKt[hh], TQt[hh]], [Tpb[b]])
                        for ci in range(G):
                            ps_ = slice(ci * CL, (ci + 1) * CL)
                            tt(ATt[ps_, hh, gi, ci * CL:(ci + 1) * CL], pb[b][ps_, r * 128 + ci * CL:r * 128 + (ci + 1) * CL],
                               tri_f[ps_, ci * CL:(ci + 1) * CL], ALU.mult, [Tpb[b], Tcst], [TAT[hh][gi]])
                P.barrier()
                chk(33)
                srn = [0]
                for gi in range(NG):
                    sl = slice(gi * GT, (gi + 1) * GT)
                    for hh in range(6):
                        mm(pb[hh][:, sl], Vt[0:GT, gi, hh * 128:(hh + 1) * 128], ATt[0:GT, hh, gi, 0:GT], True, False,
                           [TVt[gi], TAT[hh][gi]], [Tpb[hh]])
                    for ci in range(G):
                        c = gi * G + ci
                        cs = slice(c * CL, (c + 1) * CL)
                        for hh in range(6):
                            mm(pb[hh][:, cs], Sbf[l][:, hh, :], Qp[:, hh, cs], False, ci == G - 1,
                               [TSbf[l][hh], TQp[hh]], [Tpb[hh]])
                        prow = slice(ci * CL, (ci + 1) * CL)
                        for hh in range(6):
                            srn[0] = (srn[0] + 1) % 2
                            r = srn[0]
                            reg = pb[6 + r][:, 0:128]
                            mm(reg, Kh[prow, gi, hh * 128:(hh + 1) * 128], Vt[prow, gi, hh * 128:(hh + 1) * 128], True, True,
                               [TKh[gi], TVt[gi]], [Tpb[6 + r]])
                            stt(S32[l][:, hh, :], S32[l][:, hh, :], dec[:, hh, c:c + 1], reg, ALU.mult, ALU.add,
                                [TS32[l][hh], Tdec[hh], Tpb[6 + r]], [TS32[l][hh]])
                            cp(Sbf[l][:, hh, :], S32[l][:, hh, :], [TS32[l][hh]], [TSbf[l][hh]], eng="gpsimd")
                P.barrier()
                chk(34)
                To32 = [T() for _ in range(6)]
                Tsq6 = [T() for _ in range(6)]
                Trs6 = [T() for _ in range(6)]
                for hh in range(6):
                    cp(bfa[:, hh, 0:T_], pb[hh][:, 0:T_], [Tpb[hh]], [To32[hh]], eng="scalar")
                    act(Qt[:, hh, 0:T_], pb[hh][:, 0:T_], AF.Square, [Tpb[hh]], [Tsq6[hh]])
                for hh in range(6):
                    mm(pb[hh][:, 0:T_], ones_b, Qt[:, hh, 0:T_], True, True, [Tsq6[hh], Tcb], [Tpb[hh]])
                for hh in range(6):
                    act(rs6[:, hh, 0:T_], pb[hh][:, 0:T_], AF.Ln, [Tpb[hh]], [Trs6[hh]], scale=1.0 / 128, bias=EPS)
                    act(rs6[:, hh, 0:T_], rs6[:, hh, 0:T_], AF.Exp, [Trs6[hh]], [Trs6[hh]], scale=-0.5)
                    stt(bfa[:, hh, 0:T_], bfa[:, hh, 0:T_], pcol(P_GN + l), rs6[:, hh, 0:T_], ALU.mult, ALU.mult,
                        [To32[hh], Trs6[hh], Tprm], [To32[hh]])
                    tt(mix[:, hh, 0:T_], bfa[:, hh, 0:T_], gate[:, hh, 0:T_], ALU.mult, [To32[hh], Tgate[hh]], [Tmix[hh]])
                P.barrier()
            chk(35)
            out_proj_ffn(l, T_, (P_NMIX + 8) if l == 0 else P_NKV)

        def c_block(lf_ap, Tlf, n, cT_ap, TcT, first_col_prev):
            b = ps()
            mm(pb[b][0:12, 0:n], lf_ap, tri_f[0:n, 0:n], True, True, [Tlf, Tcst], [Tpb[b]])
            tsc(cT_ap, pb[b][0:12, 0:n], first_col_prev, ALU.add, [Tpb[b], TcT, TcTl], [TcT])

        def split_and_store(si, cT, TcT, n, t0, arena, qa=None):
            n0 = sb(arena, "n0", [12, 512], F32)
            r1 = sb(arena, "r1", [12, 512], F32)
            parts = sb(arena, "parts", [12, 3, 512], BF16)
            qparts = sb(arena, "qparts", [12, 3, 512], BF16)
            onesr = sb(arena, "onesr", [12, 3, 512], BF16)
            Tn0, Tr1, Tparts, Tqp, Tones = T(), T(), T(), T(), T()
            mset(onesr[:], 1.0, [Tones], eng="gpsimd")
            tsc(n0[:, 0:n], cT, -1.0, ALU.mult, [TcT], [Tn0])
            cp(parts[:, 0, 0:n], n0[:, 0:n], [Tn0], [Tparts])
            tt(r1[:, 0:n], n0[:, 0:n], parts[:, 0, 0:n], ALU.subtract, [Tn0, Tparts], [Tr1])
            cp(parts[:, 1, 0:n], r1[:, 0:n], [Tr1], [Tparts])
            tt(n0[:, 0:n], r1[:, 0:n], parts[:, 1, 0:n], ALU.subtract, [Tr1, Tparts], [Tn0])
            cp(parts[:, 2, 0:n], n0[:, 0:n], [Tn0], [Tparts])
            dma(KA[si][:, 64:67, t0:t0 + n], parts[:, :, 0:n], [Tparts], [], q="gpsimd", acc=[TKA[si]])
            dma(KA[si][:, 67:70, t0:t0 + n], onesr[:, :, 0:n], [Tones], [], q="gpsimd", acc=[TKA[si]])
            if qa is not None:
                tsc(qparts[:, :, 0:n], parts[:, :, 0:n], -1.0, ALU.mult, [Tparts], [Tqp])
                dma(QA[qa][:, 64:67, 0:n], onesr[:, :, 0:n], [Tones], [], q="gpsimd", acc=[TQA[qa]])
                dma(QA[qa][:, 67:70, 0:n], qparts[:, :, 0:n], [Tqp], [], q="gpsimd", acc=[TQA[qa]])

        TKA = [T(f"KA{i}") for i in range(NS + NSS)]
        TVS = [T(f"VS{i}") for i in range(NS + NSS)]
        TQA = [T("QA0"), T("QA1")]

        def kv_stage(si, sl_, t0, T_, qa, kout, vout, lfout, o0):
            GT = min(128, T_)
            NTB = T_ // GT
            with ExitStack() as arena:
                rms_xn(P_NKV, T_, arena)
                kvt = [sb(arena, "kvt", [128, KV_OUT], F32) for _ in range(NTB)]
                Tkvt = [T() for _ in range(NTB)]
                kT = sb(arena, "kT", [128, 6, 512], BF16)
                TkT = [T() for _ in range(6)]
                vb = sb(arena, "vb", [128, 4, MAIN], BF16)
                Tvb = [T() for _ in range(4)]
                sqk = sb(arena, "sqk", [128, MAIN], F32)
                Tsqk = T()
                cT = sb(arena, "cT", [12, 512], F32)
                TcT = T()
                for spec in plan_cols("w_kv", 0, KV_OUT):
                    wv, Tw = w_next(spec)
                    c0, ncols = spec[4], spec[5]
                    for tb in range(NTB):
                        b = ps()
                        for k in range(8):
                            mm(pb[b][0:GT, 0:ncols], xn[:, k, tb * GT:(tb + 1) * GT], wv[:, k, 0:ncols], k == 0, k == 7,
                               [Tw, Txn[k]], [Tpb[b]])
                        cp(kvt[tb][0:GT, c0:c0 + ncols], pb[b][0:GT, 0:ncols], [Tpb[b]], [Tkvt[tb]], eng="scalar")
                lft4 = sb(arena, "lft4", [128, 4, 12], F32)
                Tlft4 = [T() for _ in range(4)]
                ss4 = sb(arena, "ss4", [128, 4, 12], F32)
                Tss4 = [T() for _ in range(4)]
                for tb in range(NTB):
                    ki = tb
                    rows = slice(t0 + tb * GT, t0 + (tb + 1) * GT)
                    orows = slice(o0 + tb * GT, o0 + (tb + 1) * GT)
                    dma(vout[sl_, orows, :], kvt[ki][0:GT, MAIN:2 * MAIN], [Tkvt[ki]], [], q="gpsimd")
                    cp(vb[0:GT, tb, :], kvt[ki][0:GT, MAIN:2 * MAIN], [Tkvt[ki]], [Tvb[tb]], eng="gpsimd")
                    dma(VS[si][rows, :], vb[0:GT, tb, :], [Tvb[tb]], [], q="gpsimd", acc=[TVS[si]])
                    k3 = kvt[ki][0:GT, 0:MAIN].rearrange("p (h d) -> p h d", d=64)
                    tt(sqk[0:GT, :], kvt[ki][0:GT, 0:MAIN], kvt[ki][0:GT, 0:MAIN], ALU.mult, [Tkvt[ki]], [Tsqk])
                    P.op("vector", lambda e, GT=GT, tb=tb: e.tensor_reduce(out=ss4[0:GT, tb, :], in_=sqk[0:GT, :].rearrange("p (h d) -> p h d", d=64),
                                                                            axis=AX.X, op=ALU.add), reads=[Tsqk], writes=[Tss4[tb]])
                    act(ss4[0:GT, tb, :], ss4[0:GT, tb, :], AF.Ln, [Tss4[tb]], [Tss4[tb]], scale=1.0 / 64, bias=EPS)
                    act(ss4[0:GT, tb, :], ss4[0:GT, tb, :], AF.Exp, [Tss4[tb]], [Tss4[tb]], scale=-0.5)
                    tt(k3, k3, ss4[0:GT, tb, :].rearrange("p (h o) -> p h o", o=1).broadcast_to([GT, 12, 64]), ALU.mult,
                       [Tkvt[ki], Tss4[tb]], [Tkvt[ki]])
                    tt(k3, k3, prm[0:GT, P_FGK:P_FGK + 64].rearrange("p (o d) -> p o d", o=1).broadcast_to([GT, 12, 64]), ALU.mult,
                       [Tkvt[ki], Tprm], [Tkvt[ki]])
                    dma(kout[sl_, orows, :], kvt[ki][0:GT, 0:MAIN], [Tkvt[ki]], [], q="gpsimd")
                    tt(lft4[0:GT, tb, :], kvt[ki][0:GT, 2 * MAIN:2 * MAIN + 12], prm[0:GT, P_BF:P_BF + 12], ALU.add,
                       [Tkvt[ki], Tprm], [Tlft4[tb]])
                    act(lft4[0:GT, tb, :], lft4[0:GT, tb, :], AF.Exp, [Tlft4[tb]], [Tlft4[tb]], scale=-1.0)
                    act(lft4[0:GT, tb, :], lft4[0:GT, tb, :], AF.Ln, [Tlft4[tb]], [Tlft4[tb]], bias=1.0)
                    tsc(lft4[0:GT, tb, :], lft4[0:GT, tb, :], -1.0, ALU.mult, [Tlft4[tb]], [Tlft4[tb]])
                    dma(lfout[sl_, orows, :], lft4[0:GT, tb, :], [Tlft4[tb]], [], q="gpsimd")
                for tb in range(NTB):
                    ki = tb
                    for cc in range(6):
                        b = ps()
                        tr(pb[b][:, 0:GT], kvt[ki][0:GT, cc * 128:(cc + 1) * 128], id_f[0:GT, 0:GT], [Tkvt[ki], Tcst], [Tpb[b]])
                        cp(kT[:, cc, tb * GT:(tb + 1) * GT], pb[b][:, 0:GT], [Tpb[b]], [TkT[cc]], eng=("scalar" if cc % 2 else "vector"))
                    prev = cTlast[:, 0:1] if tb == 0 else cT[:, tb * GT - 1:tb * GT]
                    c_block(lft4[0:GT, tb, :], Tlft4[tb], GT, cT[:, tb * GT:(tb + 1) * GT], TcT, prev)
                cp(cTlast[:, 0:1], cT[:, T_ - 1:T_], [TcT], [TcTl])
                for cc in range(6):
                    for hh in range(2):
                        dma(KA[si][2 * cc + hh, 0:64, t0:t0 + T_], kT[hh * 64:(hh + 1) * 64, cc, 0:T_], [TkT[cc]], [], q="gpsimd", acc=[TKA[si]])
                split_and_store(si, cT[:, 0:T_], TcT, T_, t0, arena, qa=qa)
                rms_xn(P_NMIX + 16, T_)
                prenorm["done"] = True
                P.barrier()

        def layer_b(l, si, t0, T_, qa):
            j2 = l - 2
            L = t0 + T_
            NBF = L // 128
            rem = L - NBF * 128
            NB = NBF + (1 if rem else 0)
            with ExitStack() as arena:
                rms_xn(P_NMIX + 8 * l, T_, arena)
                bufs = attn_bufs(arena)
                KAb = [[sb(arena, "KAb", [70, L], BF16) for _ in range(2)], None]
                QAb = [[sb(arena, "QAb", [70, 512], BF16) for _ in range(2)] for _ in range(2)]
                Vb = [sb(arena, "Vb", [128, NB, 256], BF16), None]
                TKAb = [[T(), T()], [T(), T()]]
                TQAb = [[T(), T()], [T(), T()]]
                TVb = [T(), T()]
                Qn = sb(arena, "Qn", [128, 6, 512], BF16)
                TQn = [T() for _ in range(6)]
                qm = sb(arena, "qm", [128, 2, 512], F32)
                Tqm = [T(), T()]
                mset(Vb[0][:, :, 64:192], 1.0, [TVb[0]], eng="gpsimd")

                def issue_kv(cc):
                    st_ = cc % 2
                    for hh in range(2):
                        dma(KAb[st_][hh][:, :], KA[si][2 * cc + hh, :, 0:L], [TKA[si]], [TKAb[st_][hh]], q="sync", key=f"kab{st_}{hh}")
                    for u in range(2):
                        vc = cc * 128 + u * 64
                        dc = u * 192
                        if NBF:
                            dma(Vb[st_][:, 0:NBF, dc:dc + 64], VS[si][0:NBF * 128, vc:vc + 64].rearrange("(b p) d -> p b d", p=128),
                                [TVS[si]], [], q="sync", key=f"vb{st_}{u}", acc=[TVb[st_]])
                        if rem:
                            dma(Vb[st_][0:rem, NBF, dc:dc + 64], VS[si][NBF * 128:L, vc:vc + 64],
                                [TVS[si]], [], q="sync", key=f"vb{st_}{u}", acc=[TVb[st_]])

                def issue_q(cc):
                    st_ = cc % 2
                    for hh in range(2):
                        dma(QAb[st_][hh][:, 0:T_], QA[qa][2 * cc + hh, :, 0:T_], [TQA[qa]], [TQAb[st_][hh]], q="sync", key=f"qab{st_}{hh}")

                issue_kv(0)
                with ExitStack() as a1:
                    qf6 = sb(a1, "qf6", [128, 6, 512], F32)
                    Tqf = [T() for _ in range(6)]
                    sq6 = sb(a1, "sq6", [128, 6, 512], BF16)
                    Tsq6 = [T() for _ in range(6)]
                    rs6 = sb(a1, "rs6b", [128, 6, 512], F32)
                    Trs6 = [T() for _ in range(6)]
                    for (name, ll, r0, nk, c0, ncols) in plan_cols("w_in_b", j2, D):
                        wv, Tw = w_next((name, ll, r0, nk, c0, ncols))
                        for jj in range(ncols // 128):
                            j = c0 // 128 + jj
                            b = ps()
                            for k in range(8):
                                mm(pb[b][:, 0:T_], wv[:, k, jj * 128:(jj + 1) * 128], xn[:, k, 0:T_], k == 0, k == 7,
                                   [Tw, Txn[k]], [Tpb[b]])
                            if j < 6:
                                cp(qf6[:, j, 0:T_], pb[b][:, 0:T_], [Tpb[b]], [Tqf[j]], eng="scalar")
                                act(sq6[:, j, 0:T_], pb[b][:, 0:T_], AF.Square, [Tpb[b]], [Tsq6[j]])
                            else:
                                cp(qm[:, j - 6, 0:T_], pb[b][:, 0:T_], [Tpb[b]], [Tqm[j - 6]], eng="scalar")
                    nb_ = []
                    for j in range(6):
                        b = ps()
                        nb_.append(b)
                        mm(pb[b][:, 0:T_], bd_b, sq6[:, j, 0:T_], True, True, [Tsq6[j], Tcb], [Tpb[b]])
                    for j in range(6):
                        b = nb_[j]
                        act(rs6[:, j, 0:T_], pb[b][:, 0:T_], AF.Ln, [Tpb[b]], [Trs6[j]], scale=1.0 / 64, bias=EPS)
                        act(rs6[:, j, 0:T_], rs6[:, j, 0:T_], AF.Exp, [Trs6[j]], [Trs6[j]], scale=-0.5)
                        stt(Qn[:, j, 0:T_], qf6[:, j, 0:T_], drv[:, 12 + j2:13 + j2], rs6[:, j, 0:T_], ALU.mult, ALU.mult,
                            [Tqf[j], Trs6[j], Tdrv], [TQn[j]])
                        for hh in range(2):
                            dma(QA[qa][2 * j + hh, 0:64, 0:T_], Qn[hh * 64:(hh + 1) * 64, j, 0:T_], [TQn[j]], [], q="gpsimd", acc=[TQA[qa]])
                    P.barrier()
                issue_q(0)
                issue_q(1)
                with ExitStack() as a2:
                    mem_attend(l, qm, Tqm, T_, a2)
                    P.barrier()
                KAb[1] = [sb(arena, "KAb", [70, L], BF16) for _ in range(2)]
                Vb[1] = sb(arena, "Vb", [128, NB, 256], BF16)
                mset(Vb[1][:, :, 64:192], 1.0, [TVb[1]], eng="gpsimd")
                issue_kv(1)
                blocks = []
                for j in range(NB):
                    nk = 128 if j < NBF else rem
                    if j * 128 < t0:
                        blocks.append((j, nk, 0, False))
                    else:
                        blocks.append((j, nk, j * 128 - t0, True))
                for cc in range(6):
                    st_ = cc % 2

                    def lhsK(hh, j, nk, st_=st_):
                        return KAb[st_][hh][:, j * 128:j * 128 + nk]

                    def rhsQ(hh, q0, st_=st_):
                        return QAb[st_][hh][:, q0:T_]

                    def lhsV(hh, j, nk, st_=st_):
                        return Vb[st_][0:nk, j, 0:128] if hh == 0 else Vb[st_][0:nk, j, 128:256]
                    attn_pair(lhsK, rhsQ, lhsV, blocks, T_, mix[:, cc, 0:T_], Tmix[cc], TKAb[st_], TQAb[st_], [TVb[st_]], bufs)
                    if cc + 2 < 6:
                        issue_kv(cc + 2)
                        issue_q(cc + 2)
                P.barrier()
            out_proj_ffn(l, T_, (P_NMIX + 24) if l == 2 else None)

        def run_tile(si, sl_, t0, T_, xin_d, yout_d, kout, vout, lfout, qa, sample):
            GT = min(128, T_)
            NTB = T_ // GT
            with ExitStack() as arena:
                xin = sb(arena, "xin", [128, 4, D], F32)
                Txin = T()
                dma(xin[0:GT, 0:NTB, :], xin_d[sl_, t0 - (PAST if sample else 0):t0 - (PAST if sample else 0) + T_, :]
                    .rearrange("(b p) d -> p b d", p=GT), [], [Txin], q="sync", key="xin")
                for c in range(8):
                    b = ps()
                    for tb in range(NTB):
                        tr(pb[b][:, tb * GT:(tb + 1) * GT], xin[0:GT, tb, c * 128:(c + 1) * 128], id_f[0:GT, 0:GT], [Txin, Tcst], [Tpb[b]])
                    cp(h[:, c, 0:T_], pb[b][:, 0:T_], [Tpb[b]], [Th[c]], eng=("scalar" if c % 2 else "vector"))
                P.barrier()
            chk(2)
            layer_a(0, T_, sample)
            chk(3)
            layer_a(1, T_, sample)
            chk(4)
            kv_stage(si, sl_, t0, T_, qa, kout, vout, lfout, t0 - (PAST if sample else 0))
            chk(5)
            layer_b(2, si, t0, T_, qa)
            chk(6)
            layer_b(3, si, t0, T_, qa)
            chk(7)
            with ExitStack() as arena:
                yo = sb(arena, "yo", [128, 4, D], F32)
                Tyo = T()
                for tb in range(NTB):
                    for c4 in range(2):
                        b = ps()
                        for c in range(4):
                            cc = c4 * 4 + c
                            tr(pb[b][0:GT, c * 128:(c + 1) * 128], h[:, cc, tb * GT:(tb + 1) * GT], id_f, [Th[cc], Tcst], [Tpb[b]])
                        cp(yo[0:GT, tb, c4 * 512:(c4 + 1) * 512], pb[b][0:GT, :], [Tpb[b]], [Tyo], eng=("scalar" if c4 else "vector"))
                tq = t0 - (PAST if sample else 0)
                dma(yout_d[sl_, tq:tq + T_, :].rearrange("(b p) d -> p b d", p=GT), yo[0:GT, 0:NTB, :], [Tyo], [], q="gpsimd")
                P.barrier()

        def mem_prologue(sl_, sample):
            with ExitStack() as arena:
                mset(MVb[:, :, :, :, 64:192], 1.0, [TMV], eng="gpsimd")
                kvm = [sb(arena, "kvm", [128, 512], F32) for _ in range(2)]
                Tkvm = [T(), T()]
                sqk = sb(arena, "msq", [128, 256], F32)
                ss = sb(arena, "mss", [128, 4], F32)
                Tsqk, Tss = T(), T()
                if not sample:
                    mt = sb(arena, "mt", [128, 2, D], F32)
                    Tmt = T()
                    dma(mt[:, :, :], memp[sl_].rearrange("(b p) d -> p b d", p=128), [], [Tmt], q="sync", key="xin")
                    for c in range(8):
                        b = ps()
                        for tb in range(2):
                            tr(pb[b][:, tb * 128:(tb + 1) * 128], mt[:, tb, c * 128:(c + 1) * 128], id_f, [Tmt, Tcst], [Tpb[b]])
                        cp(h[:, c, 0:256], pb[b][:, 0:256], [Tpb[b]], [Th[c]], eng=("scalar" if c % 2 else "vector"))
                n = 0
                for l in range(4):
                    if not sample:
                        with ExitStack() as a2:
                            rms_xn(P_NMEM + 8 * l, 256, a2)
                            P.barrier()
                        spec = ("w_mem_kv", l, 0, 8, 0, 512)
                        wv, Tw = w_next(spec)
                    for mb in range(2):
                        ki = n % 2
                        n += 1
                        if not sample:
                            b = ps()
                            for k in range(8):
                                mm(pb[b][:, 0:512], xn[:, k, mb * 128:(mb + 1) * 128], wv[:, k, :], k == 0, k == 7, [Tw, Txn[k]], [Tpb[b]])
                            cp(kvm[ki][:, :], pb[b][:, :], [Tpb[b]], [Tkvm[ki]], eng="scalar")
                            k3 = kvm[ki][:, 0:256].rearrange("p (h d) -> p h d", d=64)
                            tt(sqk[:, :], kvm[ki][:, 0:256], kvm[ki][:, 0:256], ALU.mult, [Tkvm[ki]], [Tsqk])
                            P.op("vector", lambda e: e.tensor_reduce(out=ss[:, :], in_=sqk[:, :].rearrange("p (h d) -> p h d", d=64),
                                                                     axis=AX.X, op=ALU.add), reads=[Tsqk], writes=[Tss])
                            act(ss[:, :], ss[:, :], AF.Sqrt, [Tss], [Tss], scale=1.0 / 64, bias=EPS)
                            recip(ss[:, :], ss[:, :], [Tss], [Tss])
                            tt(k3, k3, ss[:, :].rearrange("p (h o) -> p h o", o=1).broadcast_to([128, 4, 64]), ALU.mult, [Tkvm[ki], Tss], [Tkvm[ki]])
                            tt(k3, k3, prm[:, P_MGK + 64 * l:P_MGK + 64 * (l + 1)].rearrange("p (o d) -> p o d", o=1).broadcast_to([128, 4, 64]),
                               ALU.mult, [Tkvm[ki], Tprm], [Tkvm[ki]])
                            dma(pmk[l, sl_, mb * 128:(mb + 1) * 128, :], kvm[ki][:, 0:256], [Tkvm[ki]], [], q="gpsimd")
                            dma(pmv[l, sl_, mb * 128:(mb + 1) * 128, :], kvm[ki][:, 256:512], [Tkvm[ki]], [], q="gpsimd")
                        else:
                            dma(kvm[ki][:, 0:256], cmk[l, sl_, mb * 128:(mb + 1) * 128, :], [], [Tkvm[ki]], q="sync", key=f"kvm{ki}")
                            dma(kvm[ki][:, 256:512], cmv[l, sl_, mb * 128:(mb + 1) * 128, :], [], [Tkvm[ki]], q="sync", key=f"kvm{ki}")
                        for cc in range(2):
                            b = ps()
                            tr(pb[b][:, 0:128], kvm[ki][:, cc * 128:(cc + 1) * 128], id_f, [Tkvm[ki], Tcst], [Tpb[b]])
                            cp(MKT[:, l, cc, mb * 128:(mb + 1) * 128], pb[b][:, 0:128], [Tpb[b]], [TMK], eng=("scalar" if cc else "vector"))
                        cp(MVb[:, l, mb, :, :].rearrange("p c (s d) -> p c s d", d=64)[:, :, 0::3, :], kvm[ki][:, 256:512].rearrange("p (c u d) -> p c u d", c=2, u=2), [Tkvm[ki]], [TMV])
                P.barrier()

        def cache_import(si, sl_):
            with ExitStack() as arena:
                ckt = [sb(arena, "ckt", [128, MAIN], F32) for _ in range(2)]
                Tckt = [T(), T()]
                kT = sb(arena, "kTc", [128, 6, 512], BF16)
                TkT = [T() for _ in range(6)]
                lft = sb(arena, "lftc", [128, 4, 12], F32)
                Tlft = T()
                cT = sb(arena, "cTc", [12, 512], F32)
                TcT = T()
                for r in range(0, PAST, 1024):
                    r1 = min(PAST, r + 1024)
                    dma(VS[si][r:r1, :], cv[sl_, r:r1, :], [], [], q="gpsimd", acc=[TVS[si]])
                mset(cTlast[:, 0:1], 0.0, [TcTl], eng="vector")
                for g0 in range(0, PAST, 512):
                    n = min(512, PAST - g0)
                    nb = n // 128
                    dma(lft[:, 0:nb, :], clf[sl_, g0:g0 + n, :].rearrange("(b p) h -> p b h", p=128), [], [Tlft], q="sync", key="lftc")
                    for tb in range(nb):
                        ki = tb % 2
                        dma(ckt[ki][:, :], ck[sl_, g0 + tb * 128:g0 + (tb + 1) * 128, :], [], [Tckt[ki]], q="sync", key=f"ckt{ki}")
                        for cc in range(6):
                            b = ps()
                            tr(pb[b][:, 0:128], ckt[ki][:, cc * 128:(cc + 1) * 128], id_f, [Tckt[ki], Tcst], [Tpb[b]])
                            cp(kT[:, cc, tb * 128:(tb + 1) * 128], pb[b][:, 0:128], [Tpb[b]], [TkT[cc]], eng=("scalar" if cc % 2 else "vector"))
                        prev = cTlast[:, 0:1] if tb == 0 else cT[:, tb * 128 - 1:tb * 128]
                        c_block(lft[:, tb, :], Tlft, 128, cT[:, tb * 128:(tb + 1) * 128], TcT, prev)
                    cp(cTlast[:, 0:1], cT[:, n - 1:n], [TcT], [TcTl])
                    for cc in range(6):
                        for hh in range(2):
                            dma(KA[si][2 * cc + hh, 0:64, g0:g0 + n], kT[hh * 64:(hh + 1) * 64, cc, 0:n], [TkT[cc]], [], q="gpsimd", acc=[TKA[si]])
                    with ExitStack() as a2:
                        split_and_store(si, cT[:, 0:n], TcT, n, g0, a2, qa=None)
                        P.barrier()
                P.barrier()

        try:
            dma(cst[:, :], consts_d, [], [Tcst], q="sync", key="cst")
            dma(prm[:, :], prm_d, [], [Tprm], q="sync", key="prm")
            for l in range(4):
                cast_weight("w_mem_kv", l)
            for l in range(4):
                if l < 2:
                    cast_weight("w_in_a", l)
                else:
                    cast_weight("w_in_b", l - 2)
                cast_weight("w_out", l)
                cast_weight("w_ffn_up", l)
                cast_weight("w_ffn_down", l)
                if l == 1:
                    cast_weight("w_kv", 0)
            cp(cb[:, 0, :], cst[:, C_ONES:C_ONES + 128], [Tcst], [Tcb])
            cp(cb[:, 1, :], cst[:, C_BD:C_BD + 128], [Tcst], [Tcb])
            cp(cb[:, 2, :], cst[:, C_ID:C_ID + 128], [Tcst], [Tcb])
            cp(cb[:, 3, :], cst[:, C_TRI:C_TRI + 128], [Tcst], [Tcb])
            mset(drv[:, :], 1.0, [Tdrv], eng="vector")
            tt(drv[:, 6:12], prm[:, P_LB:P_LB + 6], prm[:, P_LB + 6:P_LB + 12], ALU.subtract, [Tprm], [Tdrv])
            act(drv[:, 6:12], drv[:, 6:12], AF.Sigmoid, [Tdrv], [Tdrv])
            tsc(drv[:, 20:32], drv[:, 0:12], -1.0, ALU.mult, [Tdrv], [Tdrv])
            tsc(drv[:, 12:14], prm[:, P_FGQ:P_FGQ + 2], 0.125, ALU.mult, [Tprm], [Tdrv])
            tsc(drv[:, 16:20], prm[:, P_MGQ:P_MGQ + 4], 0.125, ALU.mult, [Tprm], [Tdrv])

            NT = SEQ // 512
            for s in range(NS):
                wplan.extend([("w_mem_kv", l, 0, 8, 0, 512) for l in range(4)])
                for _ in range(NT):
                    wplan.extend(plan_tile())
            for s in range(NSS):
                wplan.extend(plan_tile())

            chk(0)
            for s in range(NS):
                mem_prologue(s, False)
                chk(1)
                for l in range(2):
                    mset(S32[l][:], 0.0, TS32[l], eng="vector")
                    mset(Sbf[l][:], 0.0, TSbf[l], eng="gpsimd")
                mset(cTlast[:, 0:1], 0.0, [TcTl], eng="vector")
                for ti in range(NT):
                    run_tile(s, s, ti * 512, 512, xp, yp, pk, pv, plf, ti % 2, False)
                for l in range(2):
                    dma(pst[l][s].rearrange("h k v -> k h v"), S32[l][:, :, :], TS32[l], [], q="gpsimd")
                P.barrier()
            for s in range(NSS):
                si = NS + s
                mem_prologue(s, True)
                for l in range(2):
                    dma(S32[l][:, :, :], st_in[l][s].rearrange("h k v -> k h v"), [], TS32[l], q="sync", key="stin")
                    for hh in range(6):
                        cp(Sbf[l][:, hh, :], S32[l][:, hh, :], [TS32[l][hh]], [TSbf[l][hh]], eng="gpsimd")
                cache_import(si, s)
                run_tile(si, s, PAST, TS, xs, ys, sk, sv, slf, 0, True)
                for l in range(2):
                    dma(sst[l][s].rearrange("h k v -> k h v"), S32[l][:, :, :], TS32[l], [], q="gpsimd")
                P.barrier()
        except _Stop:
            wstate["used"] = len(wplan)
        assert P.dead or wstate["used"] == len(wplan), (wstate, len(wplan))
        P.emit()
    return nc, P


def make_consts():
    c = np.zeros((128, CW), np.float32)
    c[:, C_ONES:C_ONES + 128] = 1.0
    c[:64, C_BD:C_BD + 64] = 1.0
    c[64:, C_BD + 64:C_BD + 128] = 1.0
    c[:, C_ID:C_ID + 128] = np.eye(128, dtype=np.float32)
    c[:, C_TRI:C_TRI + 128] = np.triu(np.ones((128, 128), np.float32))
    sw = np.zeros((128, 128), np.float32)
    for k in range(128):
        sw[k, (k + 64) % 128] = 1.0
    c[:, C_SWAP:C_SWAP + 128] = sw
    r = np.ones(512, np.float32)
    r[0::64] = 0.0
    c[:, C_RESET:C_RESET + 512] = r[None, :]
    return c


def make_params(norm_mix, norm_ffn, norm_mem, norm_kv, lb_logits, hg_gnorm, fox_gq, mem_gq, fox_gk, mem_gk, b_f):
    p = np.zeros((128, PW), np.float32)

    def fm(v):
        v = np.asarray(v, np.float32).reshape(-1, 8, 128)
        return v.transpose(2, 0, 1).reshape(128, -1)
    p[:, P_NMIX:P_NMIX + 32] = fm(norm_mix)
    p[:, P_NFFN:P_NFFN + 32] = fm(norm_ffn)
    p[:, P_NMEM:P_NMEM + 32] = fm(norm_mem)
    p[:, P_NKV:P_NKV + 8] = fm(np.asarray(norm_kv)[None])
    p[:, P_LB:P_LB + 12] = np.asarray(lb_logits, np.float32).reshape(2, 6, 128).transpose(2, 0, 1).reshape(128, 12)
    p[:, P_GN:P_GN + 2] = np.asarray(hg_gnorm, np.float32).T
    p[:, P_FGQ:P_FGQ + 2] = np.tile(np.asarray(fox_gq, np.float32).T, (2, 1))
    p[:, P_MGQ:P_MGQ + 4] = np.tile(np.asarray(mem_gq, np.float32).T, (2, 1))
    p[:, P_FGK:P_FGK + 64] = np.asarray(fox_gk, np.float32)[None, :]
    p[:, P_MGK:P_MGK + 256] = np.asarray(mem_gk, np.float32).reshape(1, 256)
    p[:, P_BF:P_BF + 12] = np.asarray(b_f, np.float32)[None, :]
    return p


_CACHE = {}


def run(inputs, n_cores, NS, SEQ, NSS, PAST):
    key = (NS, SEQ, NSS, PAST)
    if key not in _CACHE:
        _CACHE[key] = build(NS, SEQ, NSS, PAST)[0]
    nc = _CACHE[key]
    f = lambda a: np.ascontiguousarray(np.asarray(a, np.float32))
    consts = make_consts()
    prm = make_params(inputs["norm_mix"], inputs["norm_ffn"], inputs["norm_mem"], inputs["norm_kv"], inputs["lb_logits"],
                      inputs["hg_gnorm"], inputs["fox_gq"], inputs["mem_gq"], inputs["fox_gk"], inputs["mem_gk"], inputs["b_f"])
    shared = {"consts": consts, "prm": prm}
    for k in ("w_in_a", "w_in_b", "w_mem_kv", "w_out", "w_ffn_up", "w_ffn_down"):
        shared[k] = f(inputs[k])
    shared["w_kv"] = f(inputs["w_kv"])[None]
    in_maps = []
    for c in range(n_cores):
        ps_ = slice(c * NS, (c + 1) * NS)
        ss_ = slice(c * NSS, (c + 1) * NSS)
        m = dict(shared)
        m["xp"] = f(inputs["x_prompt"][ps_])
        m["xs"] = f(inputs["x_sample"][ss_])
        m["memp"] = f(inputs["mem_prompt"][ps_])
        m["st0"] = f(inputs["state_hgrn_0"][ss_])
        m["st1"] = f(inputs["state_hgrn_1"][ss_])
        m["ck"] = f(inputs["cache_fox_k"][ss_]).reshape(NSS, PAST, MAIN)
        m["cv"] = f(inputs["cache_fox_v"][ss_]).reshape(NSS, PAST, MAIN)
        m["clf"] = f(inputs["cache_fox_logf"][ss_])
        m["cmk"] = f(inputs["cache_mem_k"][:, ss_]).reshape(4, NSS, N_MEM, 256)
        m["cmv"] = f(inputs["cache_mem_v"][:, ss_]).reshape(4, NSS, N_MEM, 256)
        in_maps.append(m)
    res = run_bass_kernel_spmd(nc, in_maps, core_ids=list(range(n_cores)))
    R = res.results
    cat = lambda k, ax=0: np.concatenate([np.asarray(r[k]) for r in R], axis=ax)
    B = n_cores * NS
    BS = n_cores * NSS
    outs = (
        cat("yp"), cat("ys"), cat("pst0"), cat("pst1"),
        cat("pk").reshape(B, SEQ, 12, 64), cat("pv").reshape(B, SEQ, 12, 64), cat("plf"),
        cat("pmk", 1).reshape(4, B, N_MEM, 4, 64), cat("pmv", 1).reshape(4, B, N_MEM, 4, 64),
        cat("sst0"), cat("sst1"),
        cat("sk").reshape(BS, TS, 12, 64), cat("sv").reshape(BS, TS, 12, 64), cat("slf"),
    )
    return tuple(np.ascontiguousarray(o, dtype=np.float32) for o in outs)


def kernel(**inputs):
    return run(inputs, 8, 2, 4096, 2, 4096)
```

```python
import numpy as np
from contextlib import ExitStack
import concourse.bass as bass
import concourse.mybir as mybir
from concourse.bass_utils import run_bass_kernel_spmd

F32 = mybir.dt.float32
BF16 = mybir.dt.bfloat16
AF = mybir.ActivationFunctionType
ALU = mybir.AluOpType
AX = mybir.AxisListType

D = 1024
DFF = 2816
MAIN = 768
A_IN = 3328
KV_OUT = 1548
EPS = 1e-6
K_MAX = 0.999999
N_MEM = 256
TS = 16
NW = 4
WELEM = 4096

C_ONES, C_BD, C_ID, C_TRI, C_SWAP, C_RESET, CW = 0, 128, 256, 384, 512, 640, 1152
P_NMIX, P_NFFN, P_NMEM, P_NKV, P_LB, P_GN, P_FGQ, P_MGQ, P_FGK, P_MGK, P_BF, PW = 0, 32, 64, 96, 104, 116, 118, 120, 124, 188, 444, 456


class T:
    __slots__ = ("name", "w", "r", "wa", "excl")

    def __init__(self, name="", excl=False):
        self.name = name
        self.w = None
        self.r = []
        self.wa = []
        self.excl = excl


class Prog:
    ENG = ("tensor", "vector", "scalar", "gpsimd", "sync")

    def __init__(self, nc):
        self.nc = nc
        self.ops = {e: [] for e in self.ENG}
        self.cnt = {}
        self.waited = {e: {} for e in self.ENG}
        self.semkeys = []
        for e in self.ENG:
            self._newsem("E_" + e)
        self.n_ops = 0
        self.dead = False

    def _newsem(self, key):
        self.semkeys.append(key)
        self.cnt[key] = 0

    def dma_sem(self, key):
        k = "D_" + key
        if k not in self.cnt:
            self._newsem(k)
        return k

    def op(self, eng, fn, reads=(), writes=(), dma=None, writes_acc=()):
        waits = {}

        def need(dep, same_ok):
            if dep is None:
                return
            key, val, deng = dep
            if deng == eng and not same_ok and not key.startswith("D_"):
                return
            if waits.get(key, 0) < val:
                waits[key] = val

        if self.dead:
            return ("E_" + eng, self.cnt["E_" + eng], eng)
        for t in reads:
            need(t.w, True)
            for r in t.wa:
                need(r, True)
            if t.excl:
                for r in t.r:
                    need(r, False)
        for t in writes:
            need(t.w, False)
            for r in t.wa:
                need(r, False)
            for r in t.r:
                need(r, False)
        for t in writes_acc:
            for r in t.r:
                need(r, False)
        wl = []
        wd = self.waited[eng]
        for key, val in waits.items():
            if wd.get(key, 0) < val:
                wd[key] = val
                wl.append((key, val))
        if dma is None:
            key = "E_" + eng
            self.cnt[key] += 1
            inc = 1
        else:
            key = dma
            self.cnt[key] += 16
            inc = 16
        me = (key, self.cnt[key], eng)
        self.ops[eng].append((wl, fn, key, inc))
        for t in writes:
            t.w = me
            t.r = []
            t.wa = []
        for t in writes_acc:
            t.wa.append(me)
        for t in reads:
            t.r.append(me)
        self.n_ops += 1
        return me

    def barrier(self, force=False):
        if self.dead and not force:
            return
        snap = dict(self.cnt)
        for e in self.ENG:
            wl = []
            for key, val in snap.items():
                if val == 0 or key == "E_" + e:
                    continue
                if self.waited[e].get(key, 0) < val:
                    self.waited[e][key] = val
                    wl.append((key, val))
            if wl:
                self.ops[e].append((wl, None, None, 0))

    def emit(self):
        nc = self.nc
        self.barrier(force=True)
        with ExitStack() as st:
            sems = {}
            for k in self.semkeys:
                sems[k] = st.enter_context(nc.semaphore(k))
            block = st.enter_context(nc.Block())
            for e in self.ENG:
                ops = self.ops[e]
                if not ops:
                    continue

                def body(eng, ops=ops):
                    for wl, fn, key, inc in ops:
                        for wk, wv in wl:
                            eng.wait_ge(sems[wk], wv)
                        if fn is not None:
                            fn(eng).then_inc(sems[key], inc)
                getattr(block, e)(body)


class _Stop(Exception):
    pass


DBG_STOP = [None]


_PROG = [None]


def chk(n):
    if DBG_STOP[0] == n:
        _PROG[0].dead = True


def build(NS, SEQ, NSS, PAST):
    nc = bass.Bass("TRN2", target_bir_lowering=False)

    def dram(name, shape, dtype, kind):
        return nc.dram_tensor(name, list(shape), dtype, kind=kind).ap()

    I, O, S = "ExternalInput", "ExternalOutput", "Internal"
    LS = PAST + TS
    xp = dram("xp", [NS, SEQ, D], F32, I)
    xs = dram("xs", [NSS, TS, D], F32, I)
    memp = dram("memp", [NS, N_MEM, D], F32, I)
    st_in = [dram("st0", [NSS, 6, 128, 128], F32, I), dram("st1", [NSS, 6, 128, 128], F32, I)]
    ck = dram("ck", [NSS, PAST, MAIN], F32, I)
    cv = dram("cv", [NSS, PAST, MAIN], F32, I)
    clf = dram("clf", [NSS, PAST, 12], F32, I)
    cmk = dram("cmk", [4, NSS, N_MEM, 256], F32, I)
    cmv = dram("cmv", [4, NSS, N_MEM, 256], F32, I)
    consts_d = dram("consts", [128, CW], F32, I)
    prm_d = dram("prm", [128, PW], F32, I)
    wshapes = {"w_in_a": [2, D, A_IN], "w_in_b": [2, D, D], "w_kv": [1, D, KV_OUT], "w_mem_kv": [4, D, 512],
               "w_out": [4, D, D], "w_ffn_up": [4, D, 2 * DFF], "w_ffn_down": [4, DFF, D]}
    wf = {k: dram(k, v, F32, I) for k, v in wshapes.items()}
    wb = {k: dram(k + "_b", v, BF16, S) for k, v in wshapes.items()}
    yp = dram("yp", [NS, SEQ, D], F32, O)
    ys = dram("ys", [NSS, TS, D], F32, O)
    pst = [dram("pst0", [NS, 6, 128, 128], F32, O), dram("pst1", [NS, 6, 128, 128], F32, O)]
    pk = dram("pk", [NS, SEQ, MAIN], F32, O)
    pv = dram("pv", [NS, SEQ, MAIN], F32, O)
    plf = dram("plf", [NS, SEQ, 12], F32, O)
    pmk = dram("pmk", [4, NS, N_MEM, 256], F32, O)
    pmv = dram("pmv", [4, NS, N_MEM, 256], F32, O)
    sst = [dram("sst0", [NSS, 6, 128, 128], F32, O), dram("sst1", [NSS, 6, 128, 128], F32, O)]
    sk = dram("sk", [NSS, TS, MAIN], F32, O)
    sv = dram("sv", [NSS, TS, MAIN], F32, O)
    slf = dram("slf", [NSS, TS, 12], F32, O)
    KA = [dram(f"KA{i}", [12, 70, SEQ if i < NS else LS], BF16, S) for i in range(NS + NSS)]
    VS = [dram(f"VS{i}", [SEQ if i < NS else LS, MAIN], BF16, S) for i in range(NS + NSS)]
    QA = [dram(f"QA{i}", [12, 70, 512], BF16, S) for i in range(2)]

    P = Prog(nc)
    _PROG[0] = P
    uid = [0]

    with ExitStack() as top:
        def sb(st, name, shape, dtype):
            uid[0] += 1
            return st.enter_context(nc.sbuf_tensor(f"{name}_{uid[0]}", list(shape), dtype))

        cst = sb(top, "cst", [128, CW], F32)
        prm = sb(top, "prm", [128, PW], F32)
        cb = sb(top, "cb", [128, 5, 128], BF16)
        drv = sb(top, "drv", [128, 32], F32)
        h = sb(top, "h", [128, 8, 512], F32)
        xn = sb(top, "xn", [128, 8, 512], BF16)
        mix = sb(top, "mix", [128, 8, 512], BF16)
        wbuf = [sb(top, f"wbuf{i}", [128, WELEM], BF16) for i in range(NW)]
        S32 = [sb(top, f"S32_{l}", [128, 6, 128], F32) for l in range(2)]
        Sbf = [sb(top, f"Sbf_{l}", [128, 6, 128], BF16) for l in range(2)]
        MKT = sb(top, "MKT", [128, 4, 2, 256], BF16)
        MVb = sb(top, "MVb", [128, 4, 2, 2, 256], BF16)
        cTlast = sb(top, "cTlast", [12, 2], F32)
        pb = [top.enter_context(nc.psum_tensor(f"pb{i}", [128, 512], F32)) for i in range(8)]
        Tpb = [T(f"pb{i}", excl=True) for i in range(8)]
        Tst = [T(f"st{i}") for i in range(8)]
        Tcst, Tprm, Tcb, Tdrv = T("cst"), T("prm"), T("cb"), T("drv")
        Th = [T(f"h{c}") for c in range(8)]
        Txn = [T(f"xn{c}") for c in range(8)]
        Tmix = [T(f"mix{c}") for c in range(8)]
        Twbuf = [T(f"wbuf{i}") for i in range(NW)]
        TS32 = [[T() for _ in range(6)] for _ in range(2)]
        TSbf = [[T() for _ in range(6)] for _ in range(2)]
        TMK, TMV, TcTl = T("MKT"), T("MVb"), T("cTlast")
        psn = [0]

        def ps():
            psn[0] = (psn[0] + 1) % 8
            return psn[0]

        ones_b, bd_b, id_b, tri_b = cb[:, 0, :], cb[:, 1, :], cb[:, 2, :], cb[:, 3, :]
        ones_f = cst[:, C_ONES:C_ONES + 128]
        id_f = cst[:, C_ID:C_ID + 128]
        tri_f = cst[:, C_TRI:C_TRI + 128]
        swap_f = cst[:, C_SWAP:C_SWAP + 128]
        reset_f = cst[:, C_RESET:C_RESET + 512]

        def pcol(c):
            return prm[:, c:c + 1]

        def mm(out, lhsT, rhs, start, stop, reads, writes):
            P.op("tensor", lambda e: e.matmul(out, lhsT, rhs, start=start, stop=stop), reads=reads, writes=writes)

        def tr(out, in_, ident, reads, writes):
            P.op("tensor", lambda e: e.transpose(out, in_, ident), reads=reads, writes=writes)

        def act(out, in_, func, reads, writes, scale=1.0, bias=None):
            if bias is None:
                P.op("scalar", lambda e: e.activation(out=out, in_=in_, func=func, scale=scale), reads=reads, writes=writes)
            else:
                P.op("scalar", lambda e: e.activation(out=out, in_=in_, func=func, scale=scale, bias=bias), reads=reads, writes=writes)

        def tt(out, in0, in1, op, reads, writes, eng="vector"):
            P.op(eng, lambda e: e.tensor_tensor(out=out, in0=in0, in1=in1, op=op), reads=reads, writes=writes)

        def tsc(out, in0, s1, op0, reads, writes, s2=None, op1=None, eng="vector"):
            if op1 is None:
                P.op(eng, lambda e: e.tensor_scalar(out=out, in0=in0, scalar1=s1, scalar2=None, op0=op0), reads=reads, writes=writes)
            else:
                P.op(eng, lambda e: e.tensor_scalar(out=out, in0=in0, scalar1=s1, scalar2=s2, op0=op0, op1=op1), reads=reads, writes=writes)

        def stt(out, in0, scalar, in1, op0, op1, reads, writes):
            P.op("vector", lambda e: e.scalar_tensor_tensor(out=out, in0=in0, scalar=scalar, in1=in1, op0=op0, op1=op1),
                 reads=reads, writes=writes)

        def cp(out, in_, reads, writes, eng="vector"):
            if eng == "scalar":
                P.op("scalar", lambda e: e.copy(out=out, in_=in_), reads=reads, writes=writes)
            else:
                P.op(eng, lambda e: e.tensor_copy(out=out, in_=in_), reads=reads, writes=writes)

        def recip(out, in_, reads, writes):
            P.op("vector", lambda e: e.reciprocal(out=out, in_=in_), reads=reads, writes=writes)

        def mset(ap, val, writes, eng="gpsimd"):
            P.op(eng, lambda e: e.memset(ap, val), writes=writes)

        dman = [0]

        def dma(out, in_, reads, writes, q="sync", key=None, acc=()):
            if key is None:
                dman[0] += 1
                key = f"g{dman[0] % 24}_{q}"
            return P.op(q, lambda e: e.dma_start(out=out, in_=in_), reads=reads, writes=writes, dma=P.dma_sem(key),
                        writes_acc=acc)

        Twb = {}
        ncast = [0]

        def cast_weight(name, l):
            shp = wshapes[name]
            rows, cols = shp[1], shp[2]
            step = max(128, (2 * 1024 * 1024 // cols) // 128 * 128)
            ts_ = []
            r = 0
            while r < rows:
                r1 = min(rows, r + step)
                t = T(f"{name}{l}_{r}")
                ncast[0] += 1
                dma(wb[name][l, r:r1, :], wf[name][l, r:r1, :], [], [t], q="gpsimd", key=f"wcast{ncast[0] % 16}")
                ts_.append(t)
                r = r1
            Twb[(name, l)] = ts_

        wplan = []
        wstate = {"issued": 0, "used": 0}

        def w_issue():
            while wstate["issued"] < len(wplan) and wstate["issued"] < wstate["used"] + NW:
                i = wstate["issued"]
                name, l, row0, nk, c0, ncols = wplan[i]
                slot = i % NW
                dst = wbuf[slot][:, 0:nk * ncols].rearrange("p (k c) -> p k c", c=ncols)
                src = wb[name][l, row0:row0 + nk * 128, c0:c0 + ncols].rearrange("(k p) c -> p k c", p=128)
                dma(dst, src, Twb[(name, l)], [Twbuf[slot]], q="sync", key=f"w{slot}")
                wstate["issued"] += 1

        def w_next(spec):
            i = wstate["used"]
            assert wplan[i] == spec, (i, wplan[i], spec)
            w_issue()
            wstate["used"] += 1
            slot = i % NW
            nk, ncols = spec[3], spec[5]
            return wbuf[slot][:, 0:nk * ncols].rearrange("p (k c) -> p k c", c=ncols), Twbuf[slot]

        def plan_cols(name, l, total, step=512, nk=8):
            return [(name, l, 0, nk, c, min(step, total - c)) for c in range(0, total, step)]

        def plan_down(l):
            out = []
            for jp in range(4):
                for kh in range(2):
                    out.append(("w_ffn_down", l, kh * 11 * 128, 11, jp * 256, 256))
            return out

        def plan_tile():
            pl = []
            for l in range(4):
                if l < 2:
                    pl += plan_cols("w_in_a", l, A_IN)
                else:
                    pl += plan_cols("w_in_b", l - 2, D)
                pl += plan_cols("w_out", l, D)
                pl += plan_cols("w_ffn_up", l, 2 * DFF)
                pl += plan_down(l)
                if l == 1:
                    pl += plan_cols("w_kv", 0, KV_OUT)
            return pl

        nsq = [sb(top, "nsq", [128, 512], BF16) for _ in range(2)]
        Tnsq = [T(), T()]
        nrs = sb(top, "nrs", [128, 512], F32)
        Tnrs = T()
        prenorm = {"done": False}

        def rms_xn(gcol0, T_, arena=None):
            if prenorm["done"]:
                prenorm["done"] = False
                return
            sqr, Tsq, rs, Trs = nsq, Tnsq, nrs, Tnrs
            b = ps()
            for c in range(8):
                i = c % 2
                act(sqr[i][:, 0:T_], h[:, c, 0:T_], AF.Square, [Th[c]], [Tsq[i]])
                mm(pb[b][:, 0:T_], ones_b, sqr[i][:, 0:T_], c == 0, c == 7, [Tsq[i], Tcb], [Tpb[b]])
            act(rs[:, 0:T_], pb[b][:, 0:T_], AF.Ln, [Tpb[b]], [Trs], scale=1.0 / D, bias=EPS)
            act(rs[:, 0:T_], rs[:, 0:T_], AF.Exp, [Trs], [Trs], scale=-0.5)
            for c in range(8):
                stt(xn[:, c, 0:T_], h[:, c, 0:T_], pcol(gcol0 + c), rs[:, 0:T_], ALU.mult, ALU.mult,
                    [Th[c], Trs, Tprm], [Txn[c]])

        def rstd_bufs(arena, n=2):
            return {"i": 0, "b": [(sb(arena, "hsq", [128, 512], BF16), T(), sb(arena, "hrs", [128, 512], F32), T()) for _ in range(n)]}

        def head_rstd(src, Tsrc, n, onesap, T_, rb, bank=None):
            rb["i"] = (rb["i"] + 1) % len(rb["b"])
            sq, Tsq, rs, Trs = rb["b"][rb["i"]]
            act(sq[:, 0:T_], src, AF.Square, [Tsrc], [Tsq])
            b = ps() if bank is None else bank
            mm(pb[b][:, 0:T_], onesap, sq[:, 0:T_], True, True, [Tsq, Tcb], [Tpb[b]])
            act(rs[:, 0:T_], pb[b][:, 0:T_], AF.Ln, [Tpb[b]], [Trs], scale=1.0 / n, bias=EPS)
            act(rs[:, 0:T_], rs[:, 0:T_], AF.Exp, [Trs], [Trs], scale=-0.5)
            return rs, Trs

        def attn_pair(lhsK, rhsQ, lhsV, blocks, nq, out_ap, Tout, Kreads, Qreads, Vreads, arena_bufs, pending=None):
            Pt, TPt, R, TR, comb, Tcomb = arena_bufs
            prevOb = list(pending[0]) if pending else []
            Ob = []
            for hh in range(2):
                ob = ps()
                while ob in Ob or ob in prevOb:
                    ob = ps()
                Ob.append(ob)
            busy = Ob + prevOb
            nb = len(blocks)
            items = [(hh, bi) + tuple(blocks[bi]) for hh in range(2) for bi in range(nb)]
            DEPTH = 2
            NP_ = len(Pt)

            def issue_s(idx):
                hh, bi, j, nk, q0, diag = items[idx]
                w = nq - q0
                sbk = ps()
                while sbk in busy:
                    sbk = ps()
                mm(pb[sbk][0:nk, 0:w], lhsK(hh, j, nk), rhsQ(hh, q0), True, True, Kreads + Qreads, [Tpb[sbk]])
                pi = idx % NP_
                act(Pt[pi][0:nk, 0:w], pb[sbk][0:nk, 0:w], AF.Exp, [Tpb[sbk]], [TPt[pi]])
                if diag:
                    tt(Pt[pi][0:nk, 0:nk], Pt[pi][0:nk, 0:nk], tri_b[0:nk, 0:nk], ALU.mult, [TPt[pi], Tcb], [TPt[pi]],
                       eng="gpsimd")

            def issue_pv(idx):
                hh, bi, j, nk, q0, diag = items[idx]
                w = nq - q0
                pi = idx % NP_
                mm(pb[Ob[hh]][:, q0:nq], lhsV(hh, j, nk), Pt[pi][0:nk, 0:w], bi == 0, bi == nb - 1,
                   [TPt[pi]] + Vreads, [Tpb[Ob[hh]]])

            for idx in range(len(items)):
                issue_s(idx)
                if idx >= DEPTH:
                    issue_pv(idx - DEPTH)
                if idx == 3 and pending:
                    pending[1](Ob)
                    pending = None
                    busy[:] = Ob
            if pending:
                pending[1](Ob)
                busy[:] = Ob
            for idx in range(max(0, len(items) - DEPTH), len(items)):
                issue_pv(idx)

            def finalize(avoid=()):
                _attn_finalize(Ob, nq, out_ap, Tout, R, TR, comb, Tcomb, list(avoid))
            return (Ob, finalize)

        def _attn_finalize(Ob, nq, out_ap, Tout, R, TR, comb, Tcomb, avoid):
            oa, ob_ = Ob
            act(R[0:64, 0:nq], pb[ob_][0:64, 0:nq], AF.Ln, [Tpb[ob_]], [TR])
            act(R[64:128, 0:nq], pb[oa][64:128, 0:nq], AF.Ln, [Tpb[oa]], [TR])
            act(R[:, 0:nq], R[:, 0:nq], AF.Exp, [TR], [TR], scale=-1.0)
            pr = ps()
            while pr in Ob or pr in avoid:
                pr = ps()
            mm(pb[pr][:, 0:nq], swap_f, R[:, 0:nq], True, True, [TR, Tcst], [Tpb[pr]])
            cp(comb[0:64, 0:nq], pb[oa][0:64, 0:nq], [Tpb[oa]], [Tcomb], eng="scalar")
            cp(comb[64:128, 0:nq], pb[ob_][64:128, 0:nq], [Tpb[ob_]], [Tcomb], eng="scalar")
            tt(out_ap, comb[:, 0:nq], pb[pr][:, 0:nq], ALU.mult, [Tcomb, Tpb[pr]], [Tout])

        def attn_bufs(arena):
            Pt = [sb(arena, "Pt", [128, 512], BF16) for _ in range(4)]
            R = sb(arena, "R", [128, 512], F32)
            comb = sb(arena, "comb", [128, 512], F32)
            return Pt, [T(), T(), T(), T()], R, T(), comb, T()

        def mem_attend(l, qm, Tqm, T_, arena):
            bufs = attn_bufs(arena)
            Qm = sb(arena, "Qm", [128, 2, 512], BF16)
            TQm = [T(), T()]
            rb = rstd_bufs(arena)
            for cc in range(2):
                rs, Trs = head_rstd(qm[:, cc, 0:T_], Tqm[cc], 64, bd_b, T_, rb)
                stt(Qm[:, cc, 0:T_], qm[:, cc, 0:T_], drv[:, 16 + l:17 + l], rs[:, 0:T_], ALU.mult, ALU.mult,
                    [Tqm[cc], Trs, Tdrv], [TQm[cc]])
            blocks = [(0, 128, 0, False), (1, 128, 0, False)]
            pend = None
            for cc in range(2):
                def lhsK(hh, j, nk, cc=cc):
                    return MKT[hh * 64:(hh + 1) * 64, l, cc, j * 128:(j + 1) * 128]

                def rhsQ(hh, q0, cc=cc):
                    return Qm[hh * 64:(hh + 1) * 64, cc, 0:T_]

                def lhsV(hh, j, nk, cc=cc):
                    return MVb[:, l, j, cc, 0:128] if hh == 0 else MVb[:, l, j, cc, 128:256]
                pend = attn_pair(lhsK, rhsQ, lhsV, blocks, T_, mix[:, 6 + cc, 0:T_], Tmix[6 + cc], [TMK], [TQm[cc]], [TMV], bufs,
                                 pending=pend)
            pend[1]()

        def out_proj_ffn(l, T_, next_gcol=None):
            for (name, ll, r0, nk, c0, ncols) in plan_cols("w_out", l, D):
                wv, Tw = w_next((name, ll, r0, nk, c0, ncols))
                for jj in range(ncols // 128):
                    j = c0 // 128 + jj
                    b = ps()
                    for k in range(8):
                        mm(pb[b][:, 0:T_], wv[:, k, jj * 128:(jj + 1) * 128], mix[:, k, 0:T_], k == 0, k == 7,
                           [Tw, Tmix[k]], [Tpb[b]])
                    tt(h[:, j, 0:T_], h[:, j, 0:T_], pb[b][:, 0:T_], ALU.add, [Th[j], Tpb[b]], [Th[j]])
            with ExitStack() as arena:
                rms_xn(P_NFFN + 8 * l, T_, arena)
                hid = sb(arena, "hid", [128, 22, 512], BF16)
                Thid = [T() for _ in range(22)]
                for (name, ll, r0, nk, c0, ncols) in plan_cols("w_ffn_up", l, 2 * DFF):
                    wv, Tw = w_next((name, ll, r0, nk, c0, ncols))
                    for jj in range(ncols // 128):
                        j = c0 // 128 + jj
                        b = ps()
                        for k in range(8):
                            mm(pb[b][:, 0:T_], wv[:, k, jj * 128:(jj + 1) * 128], xn[:, k, 0:T_], k == 0, k == 7,
                               [Tw, Txn[k]], [Tpb[b]])
                        if j < 22:
                            act(hid[:, j, 0:T_], pb[b][:, 0:T_], AF.Silu, [Tpb[b]], [Thid[j]])
                        else:
                            tt(hid[:, j - 22, 0:T_], pb[b][:, 0:T_], hid[:, j - 22, 0:T_], ALU.mult,
                               [Tpb[b], Thid[j - 22]], [Thid[j - 22]])
                for jp in range(4):
                    bs = [ps(), ps()]
                    for kh in range(2):
                        spec = ("w_ffn_down", l, kh * 11 * 128, 11, jp * 256, 256)
                        wv, Tw = w_next(spec)
                        for jj in range(2):
                            for k in range(11):
                                kk = kh * 11 + k
                                mm(pb[bs[jj]][:, 0:T_], wv[:, k, jj * 128:(jj + 1) * 128], hid[:, kk, 0:T_],
                                   kk == 0, kk == 21, [Tw, Thid[kk]], [Tpb[bs[jj]]])
                    for jj in range(2):
                        j = jp * 2 + jj
                        tt(h[:, j, 0:T_], h[:, j, 0:T_], pb[bs[jj]][:, 0:T_], ALU.add, [Th[j], Tpb[bs[jj]]], [Th[j]])
                if next_gcol is not None:
                    rms_xn(next_gcol, T_)
                    prenorm["done"] = True
                P.barrier()

        def layer_a(l, T_, seq_is_sample):
            CL = min(64, T_)
            NCH = T_ // CL
            G = min(2, NCH)
            GT = G * CL
            NG = NCH // G
            with ExitStack() as arena:
                rms_xn(P_NMIX + 8 * l, T_, arena)
                sqb = sb(arena, "sqb", [128, 6, 512], BF16)
                kb = sb(arena, "kb", [128, 6, 512], BF16)
                bfa = sb(arena, "bfa", [128, 6, 512], F32)
                gate = sb(arena, "gate", [128, 6, 512], BF16)
                qm = sb(arena, "qm", [128, 2, 512], F32)
                Vt = sb(arena, "Vt", [128, 4, MAIN], BF16)
                tmp = [sb(arena, "tmpa", [128, 512], F32) for _ in range(3)]
                Ttmp = [T(), T(), T()]
                Tsq_, Tkb, Tbf, Tgate, Tqm = ([T() for _ in range(6)] for _ in range(5))
                Tqm = [T(), T()]
                TVt = [T() for _ in range(4)]
                tn = [0]

                def nxt():
                    tn[0] = (tn[0] + 1) % 3
                    return tn[0]
                Qp = sb(arena, "Qp", [128, 6, 512], BF16)
                Qt = sb(arena, "Qt", [128, 6, 512], BF16)
                Kt = sb(arena, "Kt", [128, 6, 512], BF16)
                KhT = sb(arena, "KhT", [128, 6, 512], BF16)
                Kh = sb(arena, "Kh", [128, 4, MAIN], BF16)
                dec = sb(arena, "dec", [128, 6, 8], F32)
                em = sb(arena, "em", [128, 6, 8], F32)
                edl = sb(arena, "edl", [128, 6, 8], F32)
                Tem = [T() for _ in range(6)]
                Tedl = [T() for _ in range(6)]
                ATt = sb(arena, "ATt", [128, 6, 4, 128], BF16)
                rs6 = sb(arena, "rs6", [128, 6, 512], F32)
                TQp, TQt, TKt, TKhT, Tdec = ([T() for _ in range(6)] for _ in range(5))
                TKh = [T() for _ in range(4)]
                TAT = [[T() for _ in range(4)] for _ in range(6)]
                mset(ATt[:], 0.0, [t for row in TAT for t in row], eng="gpsimd")
                def hgrn_elem(hh):
                    P.op("vector", lambda e, hh=hh: e.tensor_tensor_scan(out=bfa[:, hh, 0:T_], data0=reset_f[:, 0:T_],
                                                                          data1=bfa[:, hh, 0:T_], initial=0.0,
                                                                          op0=ALU.mult, op1=ALU.add),
                         reads=[Tbf[hh], Tcst], writes=[Tbf[hh]])
                    b3 = bfa[:, hh, 0:T_].rearrange("p (c l) -> p c l", l=CL)
                    mid = b3[:, :, CL // 2 - 1:CL // 2].broadcast_to([128, NCH, CL])
                    act(em[:, hh, 0:NCH], b3[:, :, CL // 2 - 1], AF.Exp, [Tbf[hh]], [Tem[hh]])
                    act(dec[:, hh, 0:NCH], b3[:, :, CL - 1], AF.Exp, [Tbf[hh]], [Tdec[hh]])
                    i1 = nxt()
                    t3 = tmp[i1][:, 0:T_].rearrange("p (c l) -> p c l", l=CL)
                    tt(t3, b3, mid, ALU.subtract, [Tbf[hh]], [Ttmp[i1]])
                    i2 = nxt()
                    act(tmp[i2][:, 0:T_], tmp[i1][:, 0:T_], AF.Exp, [Ttmp[i1]], [Ttmp[i2]])
                    tt(Qt[:, hh, 0:T_], sqb[:, hh, 0:T_], tmp[i2][:, 0:T_], ALU.mult, [Tsq_[hh], Ttmp[i2]], [TQt[hh]])
                    e3 = tmp[i2][:, 0:T_].rearrange("p (c l) -> p c l", l=CL)
                    cp(edl[:, hh, 0:NCH], e3[:, :, CL - 1], [Ttmp[i2]], [Tedl[hh]])
                    act(tmp[i1][:, 0:T_], tmp[i1][:, 0:T_], AF.Exp, [Ttmp[i1]], [Ttmp[i1]], scale=-1.0)
                    tt(Kt[:, hh, 0:T_], kb[:, hh, 0:T_], tmp[i1][:, 0:T_], ALU.mult, [Tkb[hh], Ttmp[i1]], [TKt[hh]])
                    q3 = Qt[:, hh, 0:T_].rearrange("p (c l) -> p c l", l=CL)
                    k3_ = Kt[:, hh, 0:T_].rearrange("p (c l) -> p c l", l=CL)
                    qp3 = Qp[:, hh, 0:T_].rearrange("p (c l) -> p c l", l=CL)
                    kh3 = KhT[:, hh, 0:T_].rearrange("p (c l) -> p c l", l=CL)
                    em_bc = em[:, hh, 0:NCH].rearrange("p (c o) -> p c o", o=1).broadcast_to([128, NCH, CL])
                    edl_bc = edl[:, hh, 0:NCH].rearrange("p (c o) -> p c o", o=1).broadcast_to([128, NCH, CL])
                    tt(qp3, q3, em_bc, ALU.mult, [TQt[hh], Tem[hh]], [TQp[hh]])
                    tt(kh3, k3_, edl_bc, ALU.mult, [TKt[hh], Tedl[hh]], [TKhT[hh]])
                for (name, ll, r0, nk, c0, ncols) in plan_cols("w_in_a", l, A_IN):
                    wv, Tw = w_next((name, ll, r0, nk, c0, ncols))
                    bidx = c0 // 512
                    tok = {3: (0, 512, 0), 4: (0, 256, 512)}.get(bidx)
                    if tok is not None:
                        wc0, wn, vc0 = tok
                        for tb in range(NG):
                            b = ps()
                            for k in range(8):
                                mm(pb[b][0:GT, 0:wn], xn[:, k, tb * GT:(tb + 1) * GT], wv[:, k, wc0:wc0 + wn], k == 0, k == 7,
                                   [Tw, Txn[k]], [Tpb[b]])
                            cp(Vt[0:GT, tb, vc0:vc0 + wn], pb[b][0:GT, 0:wn], [Tpb[b]], [TVt[tb]], eng="scalar")
                    for jj in range(ncols // 128):
                        j = c0 // 128 + jj
                        if 12 <= j < 18:
                            continue
                        b = ps()
                        for k in range(8):
                            mm(pb[b][:, 0:T_], wv[:, k, jj * 128:(jj + 1) * 128], xn[:, k, 0:T_], k == 0, k == 7,
                               [Tw, Txn[k]], [Tpb[b]])
                        if j < 6:
                            act(sqb[:, j, 0:T_], pb[b][:, 0:T_], AF.Silu, [Tpb[b]], [Tsq_[j]])
                        elif j < 12:
                            hh = j - 6
                            i0 = nxt()
                            act(tmp[i0][:, 0:T_], pb[b][:, 0:T_], AF.Exp, [Tpb[b]], [Ttmp[i0]], scale=-1.0)
                            act(tmp[i0][:, 0:T_], tmp[i0][:, 0:T_], AF.Ln, [Ttmp[i0]], [Ttmp[i0]], bias=1.0)
                            act(tmp[i0][:, 0:T_], tmp[i0][:, 0:T_], AF.Exp, [Ttmp[i0]], [Ttmp[i0]], scale=-1.0)
                            tsc(tmp[i0][:, 0:T_], tmp[i0][:, 0:T_], drv[:, 20 + l * 6 + hh:21 + l * 6 + hh], ALU.mult,
                                [Ttmp[i0], Tdrv], [Ttmp[i0]], s2=drv[:, l * 6 + hh:l * 6 + hh + 1], op1=ALU.add)
                            tsc(tmp[i0][:, 0:T_], tmp[i0][:, 0:T_], K_MAX, ALU.min, [Ttmp[i0]], [Ttmp[i0]])
                            cp(kb[:, hh, 0:T_], tmp[i0][:, 0:T_], [Ttmp[i0]], [Tkb[hh]], eng="gpsimd")
                            act(bfa[:, hh, 0:T_], tmp[i0][:, 0:T_], AF.Ln, [Ttmp[i0]], [Tbf[hh]], scale=-1.0, bias=1.0)
                            hgrn_elem(hh)
                        elif j < 24:
                            act(gate[:, j - 18, 0:T_], pb[b][:, 0:T_], AF.Silu, [Tpb[b]], [Tgate[j - 18]])
                        else:
                            cp(qm[:, j - 24, 0:T_], pb[b][:, 0:T_], [Tpb[b]], [Tqm[j - 24]], eng="scalar")
                chk(30)
                with ExitStack() as a2:
                    mem_attend(l, qm, Tqm, T_, a2)
                chk(31)
                chk(32)
                for tb in range(NG):
                    b = ps()
                    pbv = pb[b][:].bitcast(BF16)
                    for hh in range(6):
                        tr(pbv[0:GT, hh * 128:(hh + 1) * 128], KhT[:, hh, tb * GT:(tb + 1) * GT], id_b, [TKhT[hh], Tcb], [Tpb[b]])
                    cp(Kh[0:GT, tb, :], pbv[0:GT, 0:MAIN], [Tpb[b]], [TKh[tb]])
                for hh in range(6):
                    for gi in range(NG):
                        b = ps()
                        r = 0
                        sl = slice(gi * GT, (gi + 1) * GT)
                        mm(pb[b][0:GT, r * 128:r * 128 + GT], Kt[:, hh, sl], Qt[:, hh, sl], True, True,
                           [TKt[hh], TQt[hh]], [Tpb[b]])
                        for ci in range(G):
                            ps_ = slice(ci * CL, (ci + 1) * CL)
                            tt(ATt[ps_, hh, gi, ci * CL:(ci + 1) * CL], pb[b][ps_, r * 128 + ci * CL:r * 128 + (ci + 1) * CL],
                               tri_f[ps_, ci * CL:(ci + 1) * CL], ALU.mult, [Tpb[b], Tcst], [TAT[hh][gi]])
                P.barrier()
                chk(33)
                srn = [0]
                for gi in range(NG):
                    sl = slice(gi * GT, (gi + 1) * GT)
                    for hh in range(6):
                        mm(pb[hh][:, sl], Vt[0:GT, gi, hh * 128:(hh + 1) * 128], ATt[0:GT, hh, gi, 0:GT], True, False,
                           [TVt[gi], TAT[hh][gi]], [Tpb[hh]])
                    for ci in range(G):
                        c = gi * G + ci
                        cs = slice(c * CL, (c + 1) * CL)
                        for hh in range(6):
                            mm(pb[hh][:, cs], Sbf[l][:, hh, :], Qp[:, hh, cs], False, ci == G - 1,
                               [TSbf[l][hh], TQp[hh]], [Tpb[hh]])
                        prow = slice(ci * CL, (ci + 1) * CL)
                        for hh in range(6):
                            srn[0] = (srn[0] + 1) % 2
                            r = srn[0]
                            reg = pb[6 + r][:, 0:128]
                            mm(reg, Kh[prow, gi, hh * 128:(hh + 1) * 128], Vt[prow, gi, hh * 128:(hh + 1) * 128], True, True,
                               [TKh[gi], TVt[gi]], [Tpb[6 + r]])
                            stt(S32[l][:, hh, :], S32[l][:, hh, :], dec[:, hh, c:c + 1], reg, ALU.mult, ALU.add,
                                [TS32[l][hh], Tdec[hh], Tpb[6 + r]], [TS32[l][hh]])
                            cp(Sbf[l][:, hh, :], S32[l][:, hh, :], [TS32[l][hh]], [TSbf[l][hh]], eng="gpsimd")
                P.barrier()
                chk(34)
                To32 = [T() for _ in range(6)]
                Tsq6 = [T() for _ in range(6)]
                Trs6 = [T() for _ in range(6)]
                for hh in range(6):
                    cp(bfa[:, hh, 0:T_], pb[hh][:, 0:T_], [Tpb[hh]], [To32[hh]], eng="scalar")
                    act(Qt[:, hh, 0:T_], pb[hh][:, 0:T_], AF.Square, [Tpb[hh]], [Tsq6[hh]])
                for hh in range(6):
                    mm(pb[hh][:, 0:T_], ones_b, Qt[:, hh, 0:T_], True, True, [Tsq6[hh], Tcb], [Tpb[hh]])
                for hh in range(6):
                    act(rs6[:, hh, 0:T_], pb[hh][:, 0:T_], AF.Ln, [Tpb[hh]], [Trs6[hh]], scale=1.0 / 128, bias=EPS)
                    act(rs6[:, hh, 0:T_], rs6[:, hh, 0:T_], AF.Exp, [Trs6[hh]], [Trs6[hh]], scale=-0.5)
                    stt(bfa[:, hh, 0:T_], bfa[:, hh, 0:T_], pcol(P_GN + l), rs6[:, hh, 0:T_], ALU.mult, ALU.mult,
                        [To32[hh], Trs6[hh], Tprm], [To32[hh]])
                    tt(mix[:, hh, 0:T_], bfa[:, hh, 0:T_], gate[:, hh, 0:T_], ALU.mult, [To32[hh], Tgate[hh]], [Tmix[hh]])
                P.barrier()
            chk(35)
            out_proj_ffn(l, T_, (P_NMIX + 8) if l == 0 else P_NKV)

        def c_block(lf_ap, Tlf, n, cT_ap, TcT, first_col_prev):
            b = ps()
            mm(pb[b][0:12, 0:n], lf_ap, tri_f[0:n, 0:n], True, True, [Tlf, Tcst], [Tpb[b]])
            tsc(cT_ap, pb[b][0:12, 0:n], first_col_prev, ALU.add, [Tpb[b], TcT, TcTl], [TcT])

        def split_and_store(si, cT, TcT, n, t0, arena, qa=None):
            n0 = sb(arena, "n0", [12, 512], F32)
            r1 = sb(arena, "r1", [12, 512], F32)
            parts = sb(arena, "parts", [12, 3, 512], BF16)
            qparts = sb(arena, "qparts", [12, 3, 512], BF16)
            onesr = sb(arena, "onesr", [12, 3, 512], BF16)
            Tn0, Tr1, Tparts, Tqp, Tones = T(), T(), T(), T(), T()
            mset(onesr[:], 1.0, [Tones], eng="gpsimd")
            tsc(n0[:, 0:n], cT, -1.0, ALU.mult, [TcT], [Tn0])
            cp(parts[:, 0, 0:n], n0[:, 0:n], [Tn0], [Tparts])
            tt(r1[:, 0:n], n0[:, 0:n], parts[:, 0, 0:n], ALU.subtract, [Tn0, Tparts], [Tr1])
            cp(parts[:, 1, 0:n], r1[:, 0:n], [Tr1], [Tparts])
            tt(n0[:, 0:n], r1[:, 0:n], parts[:, 1, 0:n], ALU.subtract, [Tr1, Tparts], [Tn0])
            cp(parts[:, 2, 0:n], n0[:, 0:n], [Tn0], [Tparts])
            dma(KA[si][:, 64:67, t0:t0 + n], parts[:, :, 0:n], [Tparts], [], q="sync", acc=[TKA[si]])
            dma(KA[si][:, 67:70, t0:t0 + n], onesr[:, :, 0:n], [Tones], [], q="sync", acc=[TKA[si]])
            if qa is not None:
                tsc(qparts[:, :, 0:n], parts[:, :, 0:n], -1.0, ALU.mult, [Tparts], [Tqp])
                dma(QA[qa][:, 64:67, 0:n], onesr[:, :, 0:n], [Tones], [], q="sync", acc=[TQA[qa]])
                dma(QA[qa][:, 67:70, 0:n], qparts[:, :, 0:n], [Tqp], [], q="sync", acc=[TQA[qa]])

        TKA = [T(f"KA{i}") for i in range(NS + NSS)]
        TVS = [T(f"VS{i}") for i in range(NS + NSS)]
        TQA = [T("QA0"), T("QA1")]

        def kv_stage(si, sl_, t0, T_, qa, kout, vout, lfout, o0):
            GT = min(128, T_)
            NTB = T_ // GT
            with ExitStack() as arena:
                rms_xn(P_NKV, T_, arena)
                kvt = [sb(arena, "kvt", [128, KV_OUT], F32) for _ in range(NTB)]
                Tkvt = [T() for _ in range(NTB)]
                kT = sb(arena, "kT", [128, 6, 512], BF16)
                TkT = [T() for _ in range(6)]
                vb = sb(arena, "vb", [128, 4, MAIN], BF16)
                Tvb = [T() for _ in range(4)]
                sqk = sb(arena, "sqk", [128, MAIN], F32)
                Tsqk = T()
                cT = sb(arena, "cT", [12, 512], F32)
                TcT = T()
                for spec in plan_cols("w_kv", 0, KV_OUT):
                    wv, Tw = w_next(spec)
                    c0, ncols = spec[4], spec[5]
                    for tb in range(NTB):
                        b = ps()
                        for k in range(8):
                            mm(pb[b][0:GT, 0:ncols], xn[:, k, tb * GT:(tb + 1) * GT], wv[:, k, 0:ncols], k == 0, k == 7,
                               [Tw, Txn[k]], [Tpb[b]])
                        cp(kvt[tb][0:GT, c0:c0 + ncols], pb[b][0:GT, 0:ncols], [Tpb[b]], [Tkvt[tb]], eng="scalar")
                lft4 = sb(arena, "lft4", [128, 4, 12], F32)
                Tlft4 = [T() for _ in range(4)]
                ss4 = sb(arena, "ss4", [128, 4, 12], F32)
                Tss4 = [T() for _ in range(4)]
                for tb in range(NTB):
                    ki = tb
                    rows = slice(t0 + tb * GT, t0 + (tb + 1) * GT)
                    orows = slice(o0 + tb * GT, o0 + (tb + 1) * GT)
                    dma(vout[sl_, orows, :], kvt[ki][0:GT, MAIN:2 * MAIN], [Tkvt[ki]], [], q="gpsimd")
                    cp(vb[0:GT, tb, :], kvt[ki][0:GT, MAIN:2 * MAIN], [Tkvt[ki]], [Tvb[tb]], eng="gpsimd")
                    dma(VS[si][rows, :], vb[0:GT, tb, :], [Tvb[tb]], [], q="sync", acc=[TVS[si]])
                    k3 = kvt[ki][0:GT, 0:MAIN].rearrange("p (h d) -> p h d", d=64)
                    tt(sqk[0:GT, :], kvt[ki][0:GT, 0:MAIN], kvt[ki][0:GT, 0:MAIN], ALU.mult, [Tkvt[ki]], [Tsqk])
                    P.op("vector", lambda e, GT=GT, tb=tb: e.tensor_reduce(out=ss4[0:GT, tb, :], in_=sqk[0:GT, :].rearrange("p (h d) -> p h d", d=64),
                                                                            axis=AX.X, op=ALU.add), reads=[Tsqk], writes=[Tss4[tb]])
                    act(ss4[0:GT, tb, :], ss4[0:GT, tb, :], AF.Ln, [Tss4[tb]], [Tss4[tb]], scale=1.0 / 64, bias=EPS)
                    act(ss4[0:GT, tb, :], ss4[0:GT, tb, :], AF.Exp, [Tss4[tb]], [Tss4[tb]], scale=-0.5)
                    tt(k3, k3, ss4[0:GT, tb, :].rearrange("p (h o) -> p h o", o=1).broadcast_to([GT, 12, 64]), ALU.mult,
                       [Tkvt[ki], Tss4[tb]], [Tkvt[ki]])
                    tt(k3, k3, prm[0:GT, P_FGK:P_FGK + 64].rearrange("p (o d) -> p o d", o=1).broadcast_to([GT, 12, 64]), ALU.mult,
                       [Tkvt[ki], Tprm], [Tkvt[ki]])
                    dma(kout[sl_, orows, :], kvt[ki][0:GT, 0:MAIN], [Tkvt[ki]], [], q="gpsimd")
                    tt(lft4[0:GT, tb, :], kvt[ki][0:GT, 2 * MAIN:2 * MAIN + 12], prm[0:GT, P_BF:P_BF + 12], ALU.add,
                       [Tkvt[ki], Tprm], [Tlft4[tb]])
                    act(lft4[0:GT, tb, :], lft4[0:GT, tb, :], AF.Exp, [Tlft4[tb]], [Tlft4[tb]], scale=-1.0)
                    act(lft4[0:GT, tb, :], lft4[0:GT, tb, :], AF.Ln, [Tlft4[tb]], [Tlft4[tb]], bias=1.0)
                    tsc(lft4[0:GT, tb, :], lft4[0:GT, tb, :], -1.0, ALU.mult, [Tlft4[tb]], [Tlft4[tb]])
                    dma(lfout[sl_, orows, :], lft4[0:GT, tb, :], [Tlft4[tb]], [], q="gpsimd")
                for tb in range(NTB):
                    ki = tb
                    for cc in range(6):
                        b = ps()
                        tr(pb[b][:, 0:GT], kvt[ki][0:GT, cc * 128:(cc + 1) * 128], id_f[0:GT, 0:GT], [Tkvt[ki], Tcst], [Tpb[b]])
                        cp(kT[:, cc, tb * GT:(tb + 1) * GT], pb[b][:, 0:GT], [Tpb[b]], [TkT[cc]], eng=("scalar" if cc % 2 else "vector"))
                    prev = cTlast[:, 0:1] if tb == 0 else cT[:, tb * GT - 1:tb * GT]
                    c_block(lft4[0:GT, tb, :], Tlft4[tb], GT, cT[:, tb * GT:(tb + 1) * GT], TcT, prev)
                cp(cTlast[:, 0:1], cT[:, T_ - 1:T_], [TcT], [TcTl])
                for cc in range(6):
                    for hh in range(2):
                        dma(KA[si][2 * cc + hh, 0:64, t0:t0 + T_], kT[hh * 64:(hh + 1) * 64, cc, 0:T_], [TkT[cc]], [], q="sync", acc=[TKA[si]])
                split_and_store(si, cT[:, 0:T_], TcT, T_, t0, arena, qa=qa)
                rms_xn(P_NMIX + 16, T_)
                prenorm["done"] = True
                P.barrier()

        def layer_b(l, si, t0, T_, qa):
            j2 = l - 2
            L = t0 + T_
            NBF = L // 128
            rem = L - NBF * 128
            NB = NBF + (1 if rem else 0)
            with ExitStack() as arena:
                rms_xn(P_NMIX + 8 * l, T_, arena)
                bufs = attn_bufs(arena)
                KAb = [[sb(arena, "KAb", [70, L], BF16) for _ in range(2)], None]
                QAb = [[sb(arena, "QAb", [70, 512], BF16) for _ in range(2)] for _ in range(2)]
                Vb = [sb(arena, "Vb", [128, NB, 256], BF16), None]
                TKAb = [[T(), T()], [T(), T()]]
                TQAb = [[T(), T()], [T(), T()]]
                TVb = [T(), T()]
                Qn = sb(arena, "Qn", [128, 6, 512], BF16)
                TQn = [T() for _ in range(6)]
                qm = sb(arena, "qm", [128, 2, 512], F32)
                Tqm = [T(), T()]
                mset(Vb[0][:, :, 64:192], 1.0, [TVb[0]], eng="gpsimd")

                def issue_kv(cc):
                    st_ = cc % 2
                    for hh in range(2):
                        dma(KAb[st_][hh][:, :], KA[si][2 * cc + hh, :, 0:L], [TKA[si]], [TKAb[st_][hh]], q="sync", key=f"kab{st_}{hh}")
                    for u in range(2):
                        vc = cc * 128 + u * 64
                        dc = u * 192
                        if NBF:
                            dma(Vb[st_][:, 0:NBF, dc:dc + 64], VS[si][0:NBF * 128, vc:vc + 64].rearrange("(b p) d -> p b d", p=128),
                                [TVS[si]], [], q="sync", key=f"vb{st_}{u}", acc=[TVb[st_]])
                        if rem:
                            dma(Vb[st_][0:rem, NBF, dc:dc + 64], VS[si][NBF * 128:L, vc:vc + 64],
                                [TVS[si]], [], q="sync", key=f"vb{st_}{u}", acc=[TVb[st_]])

                def issue_q(cc):
                    st_ = cc % 2
                    for hh in range(2):
                        dma(QAb[st_][hh][:, 0:T_], QA[qa][2 * cc + hh, :, 0:T_], [TQA[qa]], [TQAb[st_][hh]], q="sync", key=f"qab{st_}{hh}")

                issue_kv(0)
                with ExitStack() as a1:
                    qf6 = sb(a1, "qf6", [128, 6, 512], F32)
                    Tqf = [T() for _ in range(6)]
                    sq6 = sb(a1, "sq6", [128, 6, 512], BF16)
                    Tsq6 = [T() for _ in range(6)]
                    rs6 = sb(a1, "rs6b", [128, 6, 512], F32)
                    Trs6 = [T() for _ in range(6)]
                    for (name, ll, r0, nk, c0, ncols) in plan_cols("w_in_b", j2, D):
                        wv, Tw = w_next((name, ll, r0, nk, c0, ncols))
                        for jj in range(ncols // 128):
                            j = c0 // 128 + jj
                            b = ps()
                            for k in range(8):
                                mm(pb[b][:, 0:T_], wv[:, k, jj * 128:(jj + 1) * 128], xn[:, k, 0:T_], k == 0, k == 7,
                                   [Tw, Txn[k]], [Tpb[b]])
                            if j < 6:
                                cp(qf6[:, j, 0:T_], pb[b][:, 0:T_], [Tpb[b]], [Tqf[j]], eng="scalar")
                                act(sq6[:, j, 0:T_], pb[b][:, 0:T_], AF.Square, [Tpb[b]], [Tsq6[j]])
                            else:
                                cp(qm[:, j - 6, 0:T_], pb[b][:, 0:T_], [Tpb[b]], [Tqm[j - 6]], eng="scalar")
                    nb_ = []
                    for j in range(6):
                        b = ps()
                        nb_.append(b)
                        mm(pb[b][:, 0:T_], bd_b, sq6[:, j, 0:T_], True, True, [Tsq6[j], Tcb], [Tpb[b]])
                    for j in range(6):
                        b = nb_[j]
                        act(rs6[:, j, 0:T_], pb[b][:, 0:T_], AF.Ln, [Tpb[b]], [Trs6[j]], scale=1.0 / 64, bias=EPS)
                        act(rs6[:, j, 0:T_], rs6[:, j, 0:T_], AF.Exp, [Trs6[j]], [Trs6[j]], scale=-0.5)
                        stt(Qn[:, j, 0:T_], qf6[:, j, 0:T_], drv[:, 12 + j2:13 + j2], rs6[:, j, 0:T_], ALU.mult, ALU.mult,
                            [Tqf[j], Trs6[j], Tdrv], [TQn[j]])
                        for hh in range(2):
                            dma(QA[qa][2 * j + hh, 0:64, 0:T_], Qn[hh * 64:(hh + 1) * 64, j, 0:T_], [TQn[j]], [], q="sync", acc=[TQA[qa]])
                    P.barrier()
                issue_q(0)
                issue_q(1)
                with ExitStack() as a2:
                    mem_attend(l, qm, Tqm, T_, a2)
                    P.barrier()
                KAb[1] = [sb(arena, "KAb", [70, L], BF16) for _ in range(2)]
                Vb[1] = sb(arena, "Vb", [128, NB, 256], BF16)
                mset(Vb[1][:, :, 64:192], 1.0, [TVb[1]], eng="gpsimd")
                issue_kv(1)
                blocks = []
                for j in range(NB):
                    nk = 128 if j < NBF else rem
                    if j * 128 < t0:
                        blocks.append((j, nk, 0, False))
                    else:
                        blocks.append((j, nk, j * 128 - t0, True))
                pend = None
                for cc in range(6):
                    st_ = cc % 2

                    def lhsK(hh, j, nk, st_=st_):
                        return KAb[st_][hh][:, j * 128:j * 128 + nk]

                    def rhsQ(hh, q0, st_=st_):
                        return QAb[st_][hh][:, q0:T_]

                    def lhsV(hh, j, nk, st_=st_):
                        return Vb[st_][0:nk, j, 0:128] if hh == 0 else Vb[st_][0:nk, j, 128:256]
                    pend = attn_pair(lhsK, rhsQ, lhsV, blocks, T_, mix[:, cc, 0:T_], Tmix[cc], TKAb[st_], TQAb[st_], [TVb[st_]], bufs,
                                     pending=pend)
                    if cc + 2 < 6:
                        issue_kv(cc + 2)
                        issue_q(cc + 2)
                pend[1]()
                P.barrier()
            out_proj_ffn(l, T_, (P_NMIX + 24) if l == 2 else None)

        def run_tile(si, sl_, t0, T_, xin_d, yout_d, kout, vout, lfout, qa, sample):
            GT = min(128, T_)
            NTB = T_ // GT
            with ExitStack() as arena:
                xin = sb(arena, "xin", [128, 4, D], F32)
                Txin = T()
                dma(xin[0:GT, 0:NTB, :], xin_d[sl_, t0 - (PAST if sample else 0):t0 - (PAST if sample else 0) + T_, :]
                    .rearrange("(b p) d -> p b d", p=GT), [], [Txin], q="sync", key="xin")
                for c in range(8):
                    b = ps()
                    for tb in range(NTB):
                        tr(pb[b][:, tb * GT:(tb + 1) * GT], xin[0:GT, tb, c * 128:(c + 1) * 128], id_f[0:GT, 0:GT], [Txin, Tcst], [Tpb[b]])
                    cp(h[:, c, 0:T_], pb[b][:, 0:T_], [Tpb[b]], [Th[c]], eng=("scalar" if c % 2 else "vector"))
                P.barrier()
            chk(2)
            layer_a(0, T_, sample)
            chk(3)
            layer_a(1, T_, sample)
            chk(4)
            kv_stage(si, sl_, t0, T_, qa, kout, vout, lfout, t0 - (PAST if sample else 0))
            chk(5)
            layer_b(2, si, t0, T_, qa)
            chk(6)
            layer_b(3, si, t0, T_, qa)
            chk(7)
            with ExitStack() as arena:
                yo = sb(arena, "yo", [128, 4, D], F32)
                Tyo = T()
                for tb in range(NTB):
                    for c4 in range(2):
                        b = ps()
                        for c in range(4):
                            cc = c4 * 4 + c
                            tr(pb[b][0:GT, c * 128:(c + 1) * 128], h[:, cc, tb * GT:(tb + 1) * GT], id_f, [Th[cc], Tcst], [Tpb[b]])
                        cp(yo[0:GT, tb, c4 * 512:(c4 + 1) * 512], pb[b][0:GT, :], [Tpb[b]], [Tyo], eng=("scalar" if c4 else "vector"))
                tq = t0 - (PAST if sample else 0)
                dma(yout_d[sl_, tq:tq + T_, :].rearrange("(b p) d -> p b d", p=GT), yo[0:GT, 0:NTB, :], [Tyo], [], q="gpsimd")
                P.barrier()

        def mem_prologue(sl_, sample):
            with ExitStack() as arena:
                mset(MVb[:, :, :, :, 64:192], 1.0, [TMV], eng="gpsimd")
                kvm = [sb(arena, "kvm", [128, 512], F32) for _ in range(2)]
                Tkvm = [T(), T()]
                sqk = sb(arena, "msq", [128, 256], F32)
                ss = sb(arena, "mss", [128, 4], F32)
                Tsqk, Tss = T(), T()
                if not sample:
                    mt = sb(arena, "mt", [128, 2, D], F32)
                    Tmt = T()
                    dma(mt[:, :, :], memp[sl_].rearrange("(b p) d -> p b d", p=128), [], [Tmt], q="sync", key="xin")
                    for c in range(8):
                        b = ps()
                        for tb in range(2):
                            tr(pb[b][:, tb * 128:(tb + 1) * 128], mt[:, tb, c * 128:(c + 1) * 128], id_f, [Tmt, Tcst], [Tpb[b]])
                        cp(h[:, c, 0:256], pb[b][:, 0:256], [Tpb[b]], [Th[c]], eng=("scalar" if c % 2 else "vector"))
                n = 0
                for l in range(4):
                    if not sample:
                        with ExitStack() as a2:
                            rms_xn(P_NMEM + 8 * l, 256, a2)
                            P.barrier()
                        spec = ("w_mem_kv", l, 0, 8, 0, 512)
                        wv, Tw = w_next(spec)
                    for mb in range(2):
                        ki = n % 2
                        n += 1
                        if not sample:
                            b = ps()
                            for k in range(8):
                                mm(pb[b][:, 0:512], xn[:, k, mb * 128:(mb + 1) * 128], wv[:, k, :], k == 0, k == 7, [Tw, Txn[k]], [Tpb[b]])
                            cp(kvm[ki][:, :], pb[b][:, :], [Tpb[b]], [Tkvm[ki]], eng="scalar")
                            k3 = kvm[ki][:, 0:256].rearrange("p (h d) -> p h d", d=64)
                            tt(sqk[:, :], kvm[ki][:, 0:256], kvm[ki][:, 0:256], ALU.mult, [Tkvm[ki]], [Tsqk])
                            P.op("vector", lambda e: e.tensor_reduce(out=ss[:, :], in_=sqk[:, :].rearrange("p (h d) -> p h d", d=64),
                                                                     axis=AX.X, op=ALU.add), reads=[Tsqk], writes=[Tss])
                            act(ss[:, :], ss[:, :], AF.Sqrt, [Tss], [Tss], scale=1.0 / 64, bias=EPS)
                            recip(ss[:, :], ss[:, :], [Tss], [Tss])
                            tt(k3, k3, ss[:, :].rearrange("p (h o) -> p h o", o=1).broadcast_to([128, 4, 64]), ALU.mult, [Tkvm[ki], Tss], [Tkvm[ki]])
                            tt(k3, k3, prm[:, P_MGK + 64 * l:P_MGK + 64 * (l + 1)].rearrange("p (o d) -> p o d", o=1).broadcast_to([128, 4, 64]),
                               ALU.mult, [Tkvm[ki], Tprm], [Tkvm[ki]])
                            dma(pmk[l, sl_, mb * 128:(mb + 1) * 128, :], kvm[ki][:, 0:256], [Tkvm[ki]], [], q="gpsimd")
                            dma(pmv[l, sl_, mb * 128:(mb + 1) * 128, :], kvm[ki][:, 256:512], [Tkvm[ki]], [], q="gpsimd")
                        else:
                            dma(kvm[ki][:, 0:256], cmk[l, sl_, mb * 128:(mb + 1) * 128, :], [], [Tkvm[ki]], q="sync", key=f"kvm{ki}")
                            dma(kvm[ki][:, 256:512], cmv[l, sl_, mb * 128:(mb + 1) * 128, :], [], [Tkvm[ki]], q="sync", key=f"kvm{ki}")
                        for cc in range(2):
                            b = ps()
                            tr(pb[b][:, 0:128], kvm[ki][:, cc * 128:(cc + 1) * 128], id_f, [Tkvm[ki], Tcst], [Tpb[b]])
                            cp(MKT[:, l, cc, mb * 128:(mb + 1) * 128], pb[b][:, 0:128], [Tpb[b]], [TMK], eng=("scalar" if cc else "vector"))
                        cp(MVb[:, l, mb, :, :].rearrange("p c (s d) -> p c s d", d=64)[:, :, 0::3, :], kvm[ki][:, 256:512].rearrange("p (c u d) -> p c u d", c=2, u=2), [Tkvm[ki]], [TMV])
                P.barrier()

        def cache_import(si, sl_):
            with ExitStack() as arena:
                ckt = [sb(arena, "ckt", [128, MAIN], F32) for _ in range(2)]
                Tckt = [T(), T()]
                kT = sb(arena, "kTc", [128, 6, 512], BF16)
                TkT = [T() for _ in range(6)]
                lft = sb(arena, "lftc", [128, 4, 12], F32)
                Tlft = T()
                cT = sb(arena, "cTc", [12, 512], F32)
                TcT = T()
                for r in range(0, PAST, 1024):
                    r1 = min(PAST, r + 1024)
                    dma(VS[si][r:r1, :], cv[sl_, r:r1, :], [], [], q="gpsimd", acc=[TVS[si]])
                mset(cTlast[:, 0:1], 0.0, [TcTl], eng="vector")
                for g0 in range(0, PAST, 512):
                    n = min(512, PAST - g0)
                    nb = n // 128
                    dma(lft[:, 0:nb, :], clf[sl_, g0:g0 + n, :].rearrange("(b p) h -> p b h", p=128), [], [Tlft], q="sync", key="lftc")
                    for tb in range(nb):
                        ki = tb % 2
                        dma(ckt[ki][:, :], ck[sl_, g0 + tb * 128:g0 + (tb + 1) * 128, :], [], [Tckt[ki]], q="sync", key=f"ckt{ki}")
                        for cc in range(6):
                            b = ps()
                            tr(pb[b][:, 0:128], ckt[ki][:, cc * 128:(cc + 1) * 128], id_f, [Tckt[ki], Tcst], [Tpb[b]])
                            cp(kT[:, cc, tb * 128:(tb + 1) * 128], pb[b][:, 0:128], [Tpb[b]], [TkT[cc]], eng=("scalar" if cc % 2 else "vector"))
                        prev = cTlast[:, 0:1] if tb == 0 else cT[:, tb * 128 - 1:tb * 128]
                        c_block(lft[:, tb, :], Tlft, 128, cT[:, tb * 128:(tb + 1) * 128], TcT, prev)
                    cp(cTlast[:, 0:1], cT[:, n - 1:n], [TcT], [TcTl])
                    for cc in range(6):
                        for hh in range(2):
                            dma(KA[si][2 * cc + hh, 0:64, g0:g0 + n], kT[hh * 64:(hh + 1) * 64, cc, 0:n], [TkT[cc]], [], q="sync", acc=[TKA[si]])
                    with ExitStack() as a2:
                        split_and_store(si, cT[:, 0:n], TcT, n, g0, a2, qa=None)
                        P.barrier()
                P.barrier()

        try:
            dma(cst[:, :], consts_d, [], [Tcst], q="sync", key="cst")
            dma(prm[:, :], prm_d, [], [Tprm], q="sync", key="prm")
            for l in range(4):
                cast_weight("w_mem_kv", l)
            for l in range(4):
                if l < 2:
                    cast_weight("w_in_a", l)
                else:
                    cast_weight("w_in_b", l - 2)
                cast_weight("w_out", l)
                cast_weight("w_ffn_up", l)
                cast_weight("w_ffn_down", l)
                if l == 1:
                    cast_weight("w_kv", 0)
            cp(cb[:, 0, :], cst[:, C_ONES:C_ONES + 128], [Tcst], [Tcb])
            cp(cb[:, 1, :], cst[:, C_BD:C_BD + 128], [Tcst], [Tcb])
            cp(cb[:, 2, :], cst[:, C_ID:C_ID + 128], [Tcst], [Tcb])
            cp(cb[:, 3, :], cst[:, C_TRI:C_TRI + 128], [Tcst], [Tcb])
            mset(drv[:, :], 1.0, [Tdrv], eng="vector")
            tt(drv[:, 6:12], prm[:, P_LB:P_LB + 6], prm[:, P_LB + 6:P_LB + 12], ALU.subtract, [Tprm], [Tdrv])
            act(drv[:, 6:12], drv[:, 6:12], AF.Sigmoid, [Tdrv], [Tdrv])
            tsc(drv[:, 20:32], drv[:, 0:12], -1.0, ALU.mult, [Tdrv], [Tdrv])
            tsc(drv[:, 12:14], prm[:, P_FGQ:P_FGQ + 2], 0.125, ALU.mult, [Tprm], [Tdrv])
            tsc(drv[:, 16:20], prm[:, P_MGQ:P_MGQ + 4], 0.125, ALU.mult, [Tprm], [Tdrv])

            NT = SEQ // 512
            for s in range(NS):
                wplan.extend([("w_mem_kv", l, 0, 8, 0, 512) for l in range(4)])
                for _ in range(NT):
                    wplan.extend(plan_tile())
            for s in range(NSS):
                wplan.extend(plan_tile())

            chk(0)
            for s in range(NS):
                mem_prologue(s, False)
                chk(1)
                for l in range(2):
                    mset(S32[l][:], 0.0, TS32[l], eng="vector")
                    mset(Sbf[l][:], 0.0, TSbf[l], eng="gpsimd")
                mset(cTlast[:, 0:1], 0.0, [TcTl], eng="vector")
                for ti in range(NT):
                    run_tile(s, s, ti * 512, 512, xp, yp, pk, pv, plf, ti % 2, False)
                for l in range(2):
                    dma(pst[l][s].rearrange("h k v -> k h v"), S32[l][:, :, :], TS32[l], [], q="gpsimd")
                P.barrier()
            for s in range(NSS):
                si = NS + s
                mem_prologue(s, True)
                for l in range(2):
                    dma(S32[l][:, :, :], st_in[l][s].rearrange("h k v -> k h v"), [], TS32[l], q="sync", key="stin")
                    for hh in range(6):
                        cp(Sbf[l][:, hh, :], S32[l][:, hh, :], [TS32[l][hh]], [TSbf[l][hh]], eng="gpsimd")
                cache_import(si, s)
                run_tile(si, s, PAST, TS, xs, ys, sk, sv, slf, 0, True)
                for l in range(2):
                    dma(sst[l][s].rearrange("h k v -> k h v"), S32[l][:, :, :], TS32[l], [], q="gpsimd")
                P.barrier()
        except _Stop:
            wstate["used"] = len(wplan)
        assert P.dead or wstate["used"] == len(wplan), (wstate, len(wplan))
        P.emit()
    return nc, P


def make_consts():
    c = np.zeros((128, CW), np.float32)
    c[:, C_ONES:C_ONES + 128] = 1.0
    c[:64, C_BD:C_BD + 64] = 1.0
    c[64:, C_BD + 64:C_BD + 128] = 1.0
    c[:, C_ID:C_ID + 128] = np.eye(128, dtype=np.float32)
    c[:, C_TRI:C_TRI + 128] = np.triu(np.ones((128, 128), np.float32))
    sw = np.zeros((128, 128), np.float32)
    for k in range(128):
        sw[k, (k + 64) % 128] = 1.0
    c[:, C_SWAP:C_SWAP + 128] = sw
    r = np.ones(512, np.float32)
    r[0::64] = 0.0
    c[:, C_RESET:C_RESET + 512] = r[None, :]
    return c


def make_params(norm_mix, norm_ffn, norm_mem, norm_kv, lb_logits, hg_gnorm, fox_gq, mem_gq, fox_gk, mem_gk, b_f):
    p = np.zeros((128, PW), np.float32)

    def fm(v):
        v = np.asarray(v, np.float32).reshape(-1, 8, 128)
        return v.transpose(2, 0, 1).reshape(128, -1)
    p[:, P_NMIX:P_NMIX + 32] = fm(norm_mix)
    p[:, P_NFFN:P_NFFN + 32] = fm(norm_ffn)
    p[:, P_NMEM:P_NMEM + 32] = fm(norm_mem)
    p[:, P_NKV:P_NKV + 8] = fm(np.asarray(norm_kv)[None])
    p[:, P_LB:P_LB + 12] = np.asarray(lb_logits, np.float32).reshape(2, 6, 128).transpose(2, 0, 1).reshape(128, 12)
    p[:, P_GN:P_GN + 2] = np.asarray(hg_gnorm, np.float32).T
    p[:, P_FGQ:P_FGQ + 2] = np.tile(np.asarray(fox_gq, np.float32).T, (2, 1))
    p[:, P_MGQ:P_MGQ + 4] = np.tile(np.asarray(mem_gq, np.float32).T, (2, 1))
    p[:, P_FGK:P_FGK + 64] = np.asarray(fox_gk, np.float32)[None, :]
    p[:, P_MGK:P_MGK + 256] = np.asarray(mem_gk, np.float32).reshape(1, 256)
    p[:, P_BF:P_BF + 12] = np.asarray(b_f, np.float32)[None, :]
    return p


_CACHE = {}


def run(inputs, n_cores, NS, SEQ, NSS, PAST):
    key = (NS, SEQ, NSS, PAST)
    if key not in _CACHE:
        _CACHE[key] = build(NS, SEQ, NSS, PAST)[0]
    nc = _CACHE[key]
    f = lambda a: np.ascontiguousarray(np.asarray(a, np.float32))
    consts = make_consts()
    prm = make_params(inputs["norm_mix"], inputs["norm_ffn"], inputs["norm_mem"], inputs["norm_kv"], inputs["lb_logits"],
                      inputs["hg_gnorm"], inputs["fox_gq"], inputs["mem_gq"], inputs["fox_gk"], inputs["mem_gk"], inputs["b_f"])
    shared = {"consts": consts, "prm": prm}
    for k in ("w_in_a", "w_in_b", "w_mem_kv", "w_out", "w_ffn_up", "w_ffn_down"):
        shared[k] = f(inputs[k])
    shared["w_kv"] = f(inputs["w_kv"])[None]
    in_maps = []
    for c in range(n_cores):
        ps_ = slice(c * NS, (c + 1) * NS)
        ss_ = slice(c * NSS, (c + 1) * NSS)
        m = dict(shared)
        m["xp"] = f(inputs["x_prompt"][ps_])
        m["xs"] = f(inputs["x_sample"][ss_])
        m["memp"] = f(inputs["mem_prompt"][ps_])
        m["st0"] = f(inputs["state_hgrn_0"][ss_])
        m["st1"] = f(inputs["state_hgrn_1"][ss_])
        m["ck"] = f(inputs["cache_fox_k"][ss_]).reshape(NSS, PAST, MAIN)
        m["cv"] = f(inputs["cache_fox_v"][ss_]).reshape(NSS, PAST, MAIN)
        m["clf"] = f(inputs["cache_fox_logf"][ss_])
        m["cmk"] = f(inputs["cache_mem_k"][:, ss_]).reshape(4, NSS, N_MEM, 256)
        m["cmv"] = f(inputs["cache_mem_v"][:, ss_]).reshape(4, NSS, N_MEM, 256)
        in_maps.append(m)
    res = run_bass_kernel_spmd(nc, in_maps, core_ids=list(range(n_cores)))
    R = res.results
    cat = lambda k, ax=0: np.concatenate([np.asarray(r[k]) for r in R], axis=ax)
    B = n_cores * NS
    BS = n_cores * NSS
    outs = (
        cat("yp"), cat("ys"), cat("pst0"), cat("pst1"),
        cat("pk").reshape(B, SEQ, 12, 64), cat("pv").reshape(B, SEQ, 12, 64), cat("plf"),
        cat("pmk", 1).reshape(4, B, N_MEM, 4, 64), cat("pmv", 1).reshape(4, B, N_MEM, 4, 64),
        cat("sst0"), cat("sst1"),
        cat("sk").reshape(BS, TS, 12, 64), cat("sv").reshape(BS, TS, 12, 64), cat("slf"),
    )
    return tuple(np.ascontiguousarray(o, dtype=np.float32) for o in outs)


def kernel(**inputs):
    return run(inputs, 8, 2, 4096, 2, 4096)
```
